# Optimizing a Trainium2 kernel written in Bass

```python
import jax, jax.numpy as jnp
from jax import lax
import numpy as np

D_MODEL = 2048
BATCH = 4
SEQ = 4096
DEPTH = 1

GRID_W = 64
CTX_LEN = 256
MLA_HEADS = 8
Q_LORA = 512
KV_LORA = 256
NOPE_DIM = 128
ROPE_DIM = 64
V_DIM = 128
QK_DIM = NOPE_DIM + ROPE_DIM
ROPE_THETA = 10000.0
Q_BLOCK = 128
CHUNK = 128
SGU_GROUPS = 8
SGU_WIDTH = 1024
SGU_GROUP_DIM = SGU_WIDTH // SGU_GROUPS
D_FF = ((8 * D_MODEL // 3 + 255) // 256) * 256
N_BRANCH = 2
N_MOD = 6
EPS = 1e-6
OFF_KVC = Q_LORA
OFF_U = OFF_KVC + KV_LORA + ROPE_DIM
OFF_V = OFF_U + SGU_WIDTH
OFF_GATE = OFF_V + SGU_WIDTH
IN_COLS = OFF_GATE + N_BRANCH * D_MODEL

kernel_name = "hybrid_mla_sgu_prefix_dit_block"


def _rms(x, g):
    xf = x.astype(jnp.float32)
    y = xf * lax.rsqrt(jnp.mean(xf * xf, axis=-1, keepdims=True) + EPS)
    return (y * g.astype(jnp.float32)).astype(x.dtype)


def _modulate(h, shift, scale):
    return h * (1 + scale) + shift


def _axial_angles(n):
    rows = n // GRID_W
    row = jnp.repeat(jnp.arange(rows, dtype=jnp.float32), GRID_W)
    col = jnp.tile(jnp.arange(GRID_W, dtype=jnp.float32), rows)
    nf = ROPE_DIM // 4
    freqs = ROPE_THETA ** (-jnp.arange(nf, dtype=jnp.float32) / nf)
    return row[:, None] * freqs[None, :], col[:, None] * freqs[None, :]


def _rotate(t, ang):
    t1, t2 = jnp.split(t, 2, axis=-1)
    cos = jnp.cos(ang)[:, None, :]
    sin = jnp.sin(ang)[:, None, :]
    return jnp.concatenate([t1 * cos - t2 * sin, t2 * cos + t1 * sin], axis=-1)


def _rope_tail(t, ang_r, ang_c):
    nope, rope = t[..., :NOPE_DIM], t[..., NOPE_DIM:].astype(jnp.float32)
    rot = jnp.concatenate([_rotate(rope[..., :ROPE_DIM // 2], ang_r),
                           _rotate(rope[..., ROPE_DIM // 2:], ang_c)], axis=-1)
    return jnp.concatenate([nope, rot.astype(t.dtype)], axis=-1)


def _mla_queries(qc, q_norm_g, w_uq, qk_norm_q):
    q = _rms(qc, q_norm_g) @ w_uq
    q = q.reshape(q.shape[:-1] + (MLA_HEADS, QK_DIM))
    return _rms(q, qk_norm_q)


def _mla_keys_values(kv_in, kv_norm_g, w_ukv, qk_norm_k):
    kvc, k_rope = kv_in[..., :KV_LORA], kv_in[..., KV_LORA:]
    kv = _rms(kvc, kv_norm_g) @ w_ukv
    kv = kv.reshape(kv.shape[:-1] + (MLA_HEADS, NOPE_DIM + V_DIM))
    k_nope, v = kv[..., :NOPE_DIM], kv[..., NOPE_DIM:]
    k_rope = jnp.broadcast_to(k_rope[..., None, :], k_nope.shape[:-1] + (ROPE_DIM,))
    k = _rms(jnp.concatenate([k_nope, k_rope], axis=-1), qk_norm_k)
    return k, v


def _attention(q, k, v):
    b, n = q.shape[0], q.shape[1]
    nblk = n // Q_BLOCK
    qb = q.reshape(b, nblk, Q_BLOCK, MLA_HEADS, QK_DIM).transpose(1, 0, 2, 3, 4)
    scale = QK_DIM ** -0.5

    def one_block(qi):
        s = jnp.einsum('bqhd,bkhd->bhqk', qi, k, preferred_element_type=jnp.float32) * scale
        p = jax.nn.softmax(s, axis=-1)
        o = jnp.einsum('bhqk,bkhd->bqhd', p.astype(v.dtype), v, preferred_element_type=jnp.float32)
        return o.astype(v.dtype)

    o = lax.map(one_block, qb)
    return o.transpose(1, 0, 2, 3, 4).reshape(b, n, MLA_HEADS * V_DIM)


def _sgu(u_in, v_in, norm_g, norm_b, w_s, b_s):
    u = jax.nn.gelu(u_in, approximate=False)
    v = jax.nn.gelu(v_in, approximate=False)
    vf = v.astype(jnp.float32)
    mu = jnp.mean(vf, axis=-1, keepdims=True)
    var = jnp.mean(jnp.square(vf - mu), axis=-1, keepdims=True)
    vn = ((vf - mu) * lax.rsqrt(var + EPS) * norm_g.astype(jnp.float32)
          + norm_b.astype(jnp.float32)).astype(v.dtype)
    b, n = v.shape[0], v.shape[1]
    nc = n // CHUNK
    vs = vn.reshape(b, nc, CHUNK, SGU_GROUPS, SGU_GROUP_DIM)
    mixed = jnp.einsum('gij,bnjgc->bnigc', w_s, vs) + b_s.T[:, :, None]
    out = u.reshape(b, nc, CHUNK, SGU_GROUPS, SGU_GROUP_DIM) * mixed
    return out.reshape(b, n, SGU_WIDTH)


def _merge(attn_o, sgu_o, gate_in, w_br_attn, w_br_sgu, w_out):
    g = jax.nn.sigmoid(gate_in.astype(jnp.float32)).astype(attn_o.dtype)
    merged = g[..., :D_MODEL] * (attn_o @ w_br_attn) + g[..., D_MODEL:] * (sgu_o @ w_br_sgu)
    return merged @ w_out


def _swiglu(h, w_ffn_in, w_ffn_out):
    a, b = jnp.split(h @ w_ffn_in, 2, axis=-1)
    return (jax.nn.silu(a) * b) @ w_ffn_out


def setup_inputs(seed: int = 0) -> dict:
    key = jax.random.key(seed)
    ks = jax.random.split(key, 26)
    f32 = jnp.float32

    def dense(k, shape, fan_in, gain=1.0):
        return jax.random.normal(k, shape, f32) * (gain * fan_in ** -0.5)

    def gain_vec(k, shape):
        return 1.0 + 0.02 * jax.random.normal(k, shape, f32)

    def bias_vec(k, shape):
        return 0.02 * jax.random.normal(k, shape, f32)

    L = DEPTH
    return {
        "x": jax.random.normal(ks[0], (BATCH, SEQ, D_MODEL), f32),
        "c": jax.random.normal(ks[1], (BATCH, D_MODEL), f32),
        "ctx": jax.random.normal(ks[2], (BATCH, CTX_LEN, D_MODEL), f32),
        "c_ctx": jax.random.normal(ks[3], (D_MODEL,), f32),
        "w_mod": dense(ks[4], (L, D_MODEL, N_MOD * D_MODEL), D_MODEL, 0.5),
        "b_mod": bias_vec(ks[5], (L, N_MOD * D_MODEL)),
        "norm1_g": gain_vec(ks[6], (L, D_MODEL)),
        "w_in": dense(ks[7], (L, D_MODEL, IN_COLS), D_MODEL),
        "q_norm_g": gain_vec(ks[8], (L, Q_LORA)),
        "kv_norm_g": gain_vec(ks[9], (L, KV_LORA)),
        "w_uq": dense(ks[10], (L, Q_LORA, MLA_HEADS * QK_DIM), Q_LORA),
        "w_ukv": dense(ks[11], (L, KV_LORA, MLA_HEADS * (NOPE_DIM + V_DIM)), KV_LORA),
        "qk_norm_q": gain_vec(ks[12], (L, QK_DIM)),
        "qk_norm_k": gain_vec(ks[13], (L, QK_DIM)),
        "sgu_norm_g": gain_vec(ks[14], (L, SGU_WIDTH)),
        "sgu_norm_b": bias_vec(ks[15], (L, SGU_WIDTH)),
        "w_spatial": dense(ks[16], (L, SGU_GROUPS, CHUNK, CHUNK), CHUNK),
        "b_spatial": gain_vec(ks[17], (L, SGU_GROUPS, CHUNK)),
        "w_br_attn": dense(ks[18], (L, MLA_HEADS * V_DIM, D_MODEL), MLA_HEADS * V_DIM),
        "w_br_sgu": dense(ks[19], (L, SGU_WIDTH, D_MODEL), SGU_WIDTH),
        "w_out": dense(ks[20], (L, D_MODEL, D_MODEL), D_MODEL),
        "norm2_g": gain_vec(ks[21], (L, D_MODEL)),
        "w_ffn_in": dense(ks[22], (L, D_MODEL, 2 * D_FF), D_MODEL),
        "w_ffn_out": dense(ks[23], (L, D_FF, D_MODEL), D_FF),
    }


def reference(x, c, ctx, c_ctx, w_mod, b_mod, norm1_g, w_in, q_norm_g, kv_norm_g, w_uq,
              w_ukv, qk_norm_q, qk_norm_k, sgu_norm_g, sgu_norm_b, w_spatial, b_spatial,
              w_br_attn, w_br_sgu, w_out, norm2_g, w_ffn_in, w_ffn_out):
    n = x.shape[1]
    ang_r, ang_c = _axial_angles(n)
    silu_c = jax.nn.silu(c)
    silu_cc = jax.nn.silu(c_ctx)
    for l in range(DEPTH):
        mod = silu_c @ w_mod[l] + b_mod[l]
        sh1, sc1, g1, sh2, sc2, g2 = [m[:, None, :] for m in jnp.split(mod, N_MOD, axis=-1)]
        mod_c = silu_cc @ w_mod[l][:, :2 * D_MODEL] + b_mod[l][:2 * D_MODEL]
        sh1c, sc1c = jnp.split(mod_c, 2)
        ctx_h = _modulate(_rms(ctx, norm1_g[l]), sh1c, sc1c)
        k_ctx, v_ctx = _mla_keys_values(ctx_h @ w_in[l][:, OFF_KVC:OFF_U],
                                        kv_norm_g[l], w_ukv[l], qk_norm_k[l])

        h = _modulate(_rms(x, norm1_g[l]), sh1, sc1)
        proj = h @ w_in[l]
        q = _rope_tail(_mla_queries(proj[..., :OFF_KVC], q_norm_g[l], w_uq[l], qk_norm_q[l]),
                       ang_r, ang_c)
        k_lat, v_lat = _mla_keys_values(proj[..., OFF_KVC:OFF_U], kv_norm_g[l], w_ukv[l],
                                        qk_norm_k[l])
        k_lat = _rope_tail(k_lat, ang_r, ang_c)
        attn_o = _attention(q, jnp.concatenate([k_lat, k_ctx], axis=1),
                            jnp.concatenate([v_lat, v_ctx], axis=1))
        sgu_o = _sgu(proj[..., OFF_U:OFF_V], proj[..., OFF_V:OFF_GATE], sgu_norm_g[l],
                     sgu_norm_b[l], w_spatial[l], b_spatial[l])
        x_new = x + g1 * _merge(attn_o, sgu_o, proj[..., OFF_GATE:], w_br_attn[l],
                                w_br_sgu[l], w_out[l])
        h2 = _modulate(_rms(x_new, norm2_g[l]), sh2, sc2)
        x_new = x_new + g2 * _swiglu(h2, w_ffn_in[l], w_ffn_out[l])

        if l + 1 < DEPTH:
            mod_r = silu_cc @ w_mod[l][:, 2 * D_MODEL:] + b_mod[l][2 * D_MODEL:]
            g1c, sh2c, sc2c, g2c = jnp.split(mod_r, 4)
            proj_c = ctx_h @ w_in[l]
            q_c = _mla_queries(proj_c[..., :OFF_KVC], q_norm_g[l], w_uq[l], qk_norm_q[l])
            attn_c = _attention(q_c, k_ctx, v_ctx)
            sgu_c = _sgu(proj_c[..., OFF_U:OFF_V], proj_c[..., OFF_V:OFF_GATE], sgu_norm_g[l],
                         sgu_norm_b[l], w_spatial[l], b_spatial[l])
            ctx = ctx + g1c * _merge(attn_c, sgu_c, proj_c[..., OFF_GATE:], w_br_attn[l],
                                     w_br_sgu[l], w_out[l])
            ctx = ctx + g2c * _swiglu(_modulate(_rms(ctx, norm2_g[l]), sh2c, sc2c),
                                      w_ffn_in[l], w_ffn_out[l])
        x = x_new
    return x
```

```python
import os
import numpy as np
from contextlib import ExitStack
import concourse.bass as bass
import concourse.mybir as mybir
from concourse.bass_utils import run_bass_kernel_spmd

F32 = mybir.dt.float32
BF16 = mybir.dt.bfloat16
AF = mybir.ActivationFunctionType
ALU = mybir.AluOpType
AX = mybir.AxisListType

D = 2048
SEQ = 4096
NOWN = 2048
CTX = 256
NKEY = SEQ + CTX
NKT = NKEY // 128
H = 8
OFF_U = 832
OFF_V = 1856
OFF_GATE = 2880
IN_COLS = 6976
DFF = 5632
EPS = 1e-6
GRAN = 256
ARENA_BYTES = 206 * 1024


class Op:
    __slots__ = ("eng", "fn", "deps", "dma", "waits", "sig", "semval", "sem", "name", "banks", "cost", "odeps")

    def __init__(self, eng, fn, deps, dma, name):
        self.eng, self.fn, self.deps, self.dma, self.name = eng, fn, deps, dma, name
        self.waits = []
        self.sig = False
        self.semval = None
        self.sem = None
        self.banks = ()
        self.cost = 0.5
        self.odeps = set()


class _FakeInst:
    def then_inc(self, *a, **k):
        return self


class _FakeEng:
    def __init__(self):
        self.cost = 0.0
        self.bytes = 0

    @staticmethod
    def _free(ap):
        n = 1
        for d in ap.shape[1:]:
            n *= d
        return n

    def __getattr__(self, name):
        def f(*a, **k):
            if name == "matmul":
                n = self._free(k["rhs"])
                self.cost += max(n, 64) / 2000.0 + 0.01
            elif name == "transpose":
                self.cost += 0.12
            elif name == "dma_start":
                o = k["out"]
                self.bytes += self._free(o) * o.shape[0] * (4 if o.dtype == F32 else 2)
            elif name == "memset":
                self.cost += 0.1
            else:
                o = k.get("out", a[0] if a else None)
                n = self._free(o) if o is not None else 512
                self.cost += 0.2 + n * (0.0065 if name == "reciprocal" else 0.00105)
            return _FakeInst()
        return f


class Sched:
    def __init__(self, nc, n_dma_sems=12):
        self.nc = nc
        self.ops = []
        self.last_w = {}
        self.readers = {}
        self.n_dma_sems = n_dma_sems
        self.dram_keys = []
        self.bank_acc = {}

    @staticmethod
    def _expand(keys):
        out = []
        for k in keys:
            if isinstance(k, tuple) and len(k) == 4 and k[0] == "R":
                for g in range(k[2] // GRAN, (k[3] + GRAN - 1) // GRAN):
                    out.append((k[1], g))
            else:
                out.append(k)
        return out

    def add(self, eng, fn, reads=(), writes=(), dma=False, name=""):
        i = len(self.ops)
        for k in writes:
            if not (isinstance(k, tuple) and len(k) == 4 and k[0] == "R"):
                self.dram_keys.append(k)
        reads = self._expand(reads)
        writes = self._expand(writes)
        deps = set()
        lw, rd = self.last_w, self.readers
        for k in reads:
            j = lw.get(k)
            if j is not None:
                deps.add(j)
        for k in writes:
            j = lw.get(k)
            if j is not None:
                deps.add(j)
            r = rd.get(k)
            if r:
                deps.update(r)
        banks = set()
        for k in reads + writes:
            if isinstance(k, tuple) and len(k) == 2 and k[0] == "P":
                banks.add(k[1] // 8)
        for b in banks:
            d = self.bank_acc.setdefault(b, {})
            for e2, idx in d.items():
                if e2 != eng:
                    deps.add(idx)
            d[eng] = i
        deps.discard(i)
        for k in reads:
            rd.setdefault(k, []).append(i)
        for k in writes:
            lw[k] = i
            rd[k] = []
        op = Op(eng, fn, deps, dma, name)
        op.banks = tuple(banks)
        self.ops.append(op)
        return i

    def pe(self, fn, reads=(), writes=(), name=""):
        return self.add("pe", fn, reads, writes, name=name)

    def act(self, fn, reads=(), writes=(), name=""):
        return self.add("act", fn, reads, writes, name=name)

    def dve(self, fn, reads=(), writes=(), name=""):
        return self.add("dve", fn, reads, writes, name=name)

    def dma(self, out, in_, reads=(), writes=(), q="sp", name="", **kw):
        return self.add(q, lambda e: e.dma_start(out=out, in_=in_, **kw), reads, writes,
                        dma=True, name=name)

    def dma_own(self, out, in_, reads=(), writes=(), q="pool"):
        i = self.dma(out, in_, reads=reads, writes=writes, q=q, name="own")
        return i

    def reorder(self, window=48):
        ops = self.ops
        n = len(ops)
        for op in ops:
            if op.fn is None:
                op.cost = 0.0
                continue
            fe = _FakeEng()
            op.fn(fe)
            if op.dma:
                op.cost = (1.0 if op.eng == "pool" else 0.06, 2.0 + fe.bytes / 250e3)
            else:
                op.cost = fe.cost * (2.5 if op.eng == "pool" else 1.0)
        lastb = {}
        for i, op in enumerate(ops):
            op.odeps = set()
            for b in op.banks:
                j = lastb.get((op.eng, b))
                if j is not None:
                    op.odeps.add(j)
                lastb[(op.eng, b)] = i
        engs = ("pe", "act", "dve", "pool", "sp")
        queues = {e: [i for i, op in enumerate(ops) if op.eng == e] for e in engs}
        qpos = {e: 0 for e in engs}
        succ = [[] for _ in range(n)]
        npred = [0] * n
        for i, op in enumerate(ops):
            ps = op.deps | op.odeps
            npred[i] = len(ps)
            for j in ps:
                succ[j].append(i)
        ready_t = [0.0] * n
        done = [False] * n
        free_t = {e: 0.0 for e in engs}
        last_i = n - 1
        order = []
        remaining = n
        while remaining:
            best = None
            for e in engs:
                q = queues[e]
                p = qpos[e]
                while p < len(q) and done[q[p]]:
                    p += 1
                qpos[e] = p
                cnt = 0
                k = p
                while k < len(q) and cnt < window:
                    i = q[k]
                    k += 1
                    if done[i]:
                        continue
                    cnt += 1
                    if npred[i] or (i == last_i and remaining > 1):
                        continue
                    st = max(free_t[e], ready_t[i])
                    if best is None or st < best[0] - 1e-9 or (abs(st - best[0]) <= 1e-9 and i < best[1]):
                        best = (st, i, e)
                    if ready_t[i] <= free_t[e]:
                        break
            assert best is not None, "scheduler deadlock"
            st, i, e = best
            op = ops[i]
            if op.dma:
                free_t[e] = st + op.cost[0]
                fin = st + op.cost[0] + op.cost[1]
            else:
                free_t[e] = st + op.cost
                fin = free_t[e] + 0.15
            done[i] = True
            remaining -= 1
            order.append(i)
            for j in succ[i]:
                npred[j] -= 1
                if ready_t[j] < fin:
                    ready_t[j] = fin
        newidx = {old: new for new, old in enumerate(order)}
        new_ops = []
        for old in order:
            op = ops[old]
            op.deps = set(newidx[j] for j in op.deps)
            new_ops.append(op)
        self.ops = new_ops
        self.est_time = max(free_t.values())

    def finalize(self, final_reads=(), reorder=True):
        self.add("sp", None, reads=final_reads, name="final")
        if reorder:
            self.reorder()
        ops = self.ops
        engs = ("pe", "act", "dve", "pool", "sp")
        cur = {e: {} for e in engs}
        dma_known = {e: set() for e in engs}
        clock = [None] * len(ops)
        dma_rr = {e: 0 for e in engs}
        dma_last = {}
        dma_uses = {}
        for i, op in enumerate(ops):
            E = op.eng
            c = cur[E]
            deps = set(op.deps)
            if op.dma and op.name == "own":
                self._own = getattr(self, "_own", 0) + 1
                op.sem = ("own", self._own)
                op.semval = 16
            elif op.dma:
                slot = dma_rr[E] % self.n_dma_sems
                dma_rr[E] += 1
                prev = dma_last.get((E, slot))
                if prev is not None:
                    deps.add(prev)
                dma_last[(E, slot)] = i
                dma_uses[(E, slot)] = dma_uses.get((E, slot), 0) + 1
                op.sem = (E, slot)
                op.semval = 16 * dma_uses[(E, slot)]
            waits = []
            for j in sorted(deps):
                oj = ops[j]
                if oj.dma:
                    if j in dma_known[E]:
                        continue
                    waits.append(j)
                    dma_known[E].add(j)
                else:
                    if c.get(oj.eng, -1) >= j:
                        continue
                    if oj.eng == E and E == "pe":
                        continue
                    waits.append(j)
                    oj.sig = True
            for j in waits:
                for k, v in clock[j].items():
                    if c.get(k, -1) < v:
                        c[k] = v
            op.waits = waits
            ck = dict(c)
            if not op.dma:
                ck[E] = i
                c[E] = max(c.get(E, -1), -1)
            clock[i] = ck
        cnt = {e: 0 for e in engs}
        for op in ops:
            if op.sig and not op.dma:
                cnt[op.eng] += 1
                op.semval = cnt[op.eng]
        self.sig_counts = cnt
        return self

    def emit(self):
        nc = self.nc
        ops = self.ops
        engs = ("pe", "act", "dve", "pool", "sp")
        sems = {}
        for e in ("pe", "act", "dve", "pool"):
            sems[e] = nc.alloc_semaphore(name=f"sig_{e}")
        used = set(op.sem for op in ops if op.dma)
        for key in sorted(used, key=str):
            sems[key] = nc.alloc_semaphore(name=f"dma_{key[0]}_{key[1]}")
        per_eng = {e: [op for op in ops if op.eng == e] for e in engs}

        def run(e_name):
            def body(eng):
                for op in per_eng[e_name]:
                    for j in op.waits:
                        oj = ops[j]
                        eng.wait_ge(sems[oj.sem] if oj.dma else sems[oj.eng], oj.semval)
                    if op.fn is None:
                        continue
                    inst = op.fn(eng)
                    if op.dma:
                        inst.then_inc(sems[op.sem], 16)
                    elif op.sig:
                        inst.then_inc(sems[op.eng], 1)
            return body

        with nc.Block() as block:
            block.sync(run("sp"))
            block.tensor(run("pe"))
            block.scalar(run("act"))
            block.vector(run("dve"))
            block.gpsimd(run("pool"))


class Buf:
    def __init__(self, space, base, off, shape, dt):
        self.space, self.off, self.shape, self.dt = space, off, list(shape), dt
        esz = 4 if dt == F32 else 2
        n = 1
        for s in shape[1:]:
            n *= s
        self.nbytes = n * esz
        pad = (self.nbytes + 3) // 4
        v = base[:, off // 4: off // 4 + pad]
        if dt != F32:
            v = v.bitcast(dt)
        v = v[:, 0:n]
        if len(shape) == 3:
            v = v.rearrange("p (a b) -> p a b", a=shape[1])
        elif len(shape) == 4:
            v = v.rearrange("p (a b c) -> p a b c", a=shape[1], b=shape[2])
        if shape[0] < 128:
            v = v[0:shape[0]]
        self.ap = v
        self.part = self.nbytes // shape[1]

    def k(self, i=None, j=None):
        if i is None:
            return ("R", self.space, self.off, self.off + self.nbytes)
        if j is None:
            j = i + 1
        return ("R", self.space, self.off + i * self.part, self.off + j * self.part)


class Arena:
    def __init__(self, space, base, limit):
        self.space, self.base, self.limit, self.ptr = space, base, limit, 0

    def alloc(self, shape, dt):
        b = Buf(self.space, self.base, self.ptr, shape, dt)
        self.ptr += (b.nbytes + GRAN - 1) // GRAN * GRAN
        assert self.ptr <= self.limit, ("arena overflow", self.space, self.ptr)
        return b

    def view(self, off, shape, dt):
        return Buf(self.space, self.base, off, shape, dt)


def build(debug=False, limit=9, kq_nb=9, kq_own=True, kq_sec=9):
    nc = bass.Bass("TRN2", target_bir_lowering=False)

    def din(name, shape):
        return nc.dram_tensor(name, list(shape), F32, kind="ExternalInput").ap()

    def dscr(name, shape, dt):
        return nc.dram_tensor(name, list(shape), dt, kind="ExternalOutput" if debug else "Internal").ap()

    xk = din("xk", [SEQ, D])
    ctxb = din("ctxb", [CTX, D])
    cT_d = din("cT", [128, 16, 2])
    bmT_d = din("bmT", [128, 96])
    n1g_d = din("n1g", [128, 16])
    n2g_d = din("n2g", [128, 16])
    gqn_d = din("gqn", [128, 4])
    gkv_d = din("gkv", [128, 2])
    gvec_d = din("gvec", [128, 8])
    gqk_d = din("gqk", [128, 384])
    sgun_d = din("sgun", [128, 16])
    w_mod = din("w_mod", [D, 6 * D])
    w_in = din("w_in", [D, IN_COLS])
    w_kv = din("w_kv", [D, 384])
    w_uq = din("w_uq", [512, 2048])
    w_ukv = din("w_ukv", [256, 2048])
    wsT_d = din("wsT", [128, 8, 128])
    bs_d = din("bsbc", [128, 1024])
    w_bra = din("w_bra", [1024, D])
    w_brs = din("w_brs", [1024, D])
    w_out = din("w_out", [D, D])
    w_f1 = din("w_f1", [D, 2 * DFF])
    w_f2 = din("w_f2", [DFF, D])
    ident_d = din("ident", [128, 128])
    cos_d = din("cosT", [64, SEQ])
    sin_d = din("sinS", [64, SEQ])
    y = nc.dram_tensor("y", [NOWN, D], F32, kind="ExternalOutput").ap()

    hT_d = dscr("hT_s", [16, 128, NOWN], BF16)
    KT_d = dscr("KT_s", [H, 128, NKEY], BF16)
    V_d = dscr("V_s", [NKT, 128, 1024], BF16)
    QTn_d = dscr("QTn_s", [H, 128, NOWN], BF16)
    QTr_d = dscr("QTr_s", [H, 64, NOWN], BF16)
    sgT_d = dscr("sgT_s", [8, 128, NOWN], BF16)
    atT_d = dscr("atT_s", [H, 128, NOWN], BF16)
    mrow_d = dscr("mrow_s", [96, 128], F32)

    def dint(name, shape):
        return nc.dram_tensor(name, list(shape), BF16, kind="Internal").ap()
    wg_b = dint("wg_b", [D, 2 * D])
    wba_b = dint("wba_b", [1024, D])
    wbs_b = dint("wbs_b", [1024, D])
    wo_b = dint("wo_b", [D, D])
    wf1_b = dint("wf1_b", [D, 2 * DFF])
    wf2_b = dint("wf2_b", [DFF, D])
    rk_dbg = dscr("rk_s", [128, NKT * H], F32) if debug else None

    es = ExitStack()
    with es:
        arena_t = es.enter_context(nc.sbuf_tensor("arena", [128, ARENA_BYTES // 4], F32))
        psum_t = es.enter_context(nc.psum_tensor("psum", [128, 4096], F32))
        S = Sched(nc)
        A = Arena("S", arena_t, ARENA_BYTES)
        PA = Arena("P", psum_t, 16384)
        PB = [PA.view(b * 2048, [128, 512], F32) for b in range(8)]

        identb = A.alloc([128, 128], BF16)
        identf = A.alloc([128, 128], F32)
        onesb = A.alloc([128, 128], BF16)
        onesf = A.alloc([128, 128], F32)
        scT = A.alloc([128, 16, 2], BF16)
        mrow_sb = A.alloc([128, 128], F32)
        modFM = A.alloc([128, 96, 2], F32)
        A1 = A.alloc([128, 16, 2], F32)
        A2 = A.alloc([128, 16], F32)
        bmT = A.alloc([128, 96], F32)
        n1g = A.alloc([128, 16], F32)
        n2g = A.alloc([128, 16], F32)
        gqn = A.alloc([128, 4], F32)
        gkv = A.alloc([128, 2], F32)
        gvec = A.alloc([128, 8], F32)
        sgun = A.alloc([128, 16], F32)
        small = A.alloc([128, 16], F32)
        rk = A.alloc([128, NKT * H], F32)
        krot = A.alloc([64, NKEY], BF16)
        wslot = []
        wctr = [0]

        def alloc_slots(n):
            wslot.clear()
            wslot.extend(A.alloc([128, 4096], BF16) for _ in range(n))

        def next_slot():
            s = wslot[wctr[0] % len(wslot)]
            wctr[0] += 1
            return s

        def wload(src_ap, kchunks, ncols, reads=()):
            s = next_slot()
            assert kchunks * ncols <= 4096
            v = s.ap[:, 0:kchunks * ncols].rearrange("p (k n) -> p k n", k=kchunks)
            S.dma(v, src_ap.rearrange("(k p) n -> p k n", p=128), reads=list(reads), writes=[s.k()], q="pool")
            return s, v

        def wload_b(src_ap, kchunks, ncols, keys):
            s_ = next_slot()
            assert kchunks * ncols <= 4096
            v = s_.ap[:, 0:kchunks * ncols].rearrange("p (k n) -> p k n", k=kchunks)
            S.dma(v, src_ap.rearrange("(k p) n -> p k n", p=128), reads=list(keys), writes=[s_.k()], q="sp")
            return s_, v

        phase_base = A.ptr

        alloc_slots(6)
        cT = A.alloc([128, 16, 2], F32)
        gqk = A.alloc([128, 384], F32)
        gq2 = A.alloc([128, 384], F32)

        S.dma(identf.ap, ident_d, writes=[identf.k()])
        S.dma(cT.ap, cT_d, writes=[cT.k()])
        S.dma(bmT.ap, bmT_d, writes=[bmT.k()])
        S.dma(n1g.ap, n1g_d, writes=[n1g.k()])
        S.dma(n2g.ap, n2g_d, writes=[n2g.k()])
        S.dma(gqn.ap, gqn_d, writes=[gqn.k()])
        S.dma(gkv.ap, gkv_d, writes=[gkv.k()])
        S.dma(gvec.ap, gvec_d, writes=[gvec.k()])
        S.dma(sgun.ap, sgun_d, writes=[sgun.k()])
        S.dma(gqk.ap, gqk_d, writes=[gqk.k()])
        S.dve(lambda e: e.tensor_copy(out=identb.ap, in_=identf.ap), reads=[identf.k()], writes=[identb.k()])
        S.dve(lambda e: e.memset(onesb.ap, 1.0), writes=[onesb.k()])
        S.dve(lambda e: e.memset(onesf.ap, 1.0), writes=[onesf.k()])
        S.dve(lambda e: e.memset(small.ap[:, 2:3], EPS), writes=[small.k()])
        S.act(lambda e: e.activation(out=scT.ap, in_=cT.ap, func=AF.Silu), reads=[cT.k()], writes=[scT.k()])

        pmod = PA.view(0, [128, 96, 2], F32)
        for pn in range(16):
            s, v = wload(w_mod[:, pn * 256:(pn + 1) * 256], 16, 256)

            def f(e, v=v, pn=pn):
                i = None
                for nt in range(2):
                    col = pn * 2 + nt
                    for k in range(16):
                        i = e.matmul(pmod.ap[:, col, :], lhsT=v[:, k, nt * 128:(nt + 1) * 128], rhs=scT.ap[:, k, :],
                                     start=(k == 0), stop=(k == 15))
                return i
            S.pe(f, reads=[s.k(), scT.k()], writes=[pmod.k()])
        for r in range(2):
            S.dve(lambda e, r=r: e.tensor_tensor(out=modFM.ap[:, 0:32, r], in0=pmod.ap[:, 0:32, r], in1=bmT.ap[:, 0:32], op=ALU.add),
                  reads=[pmod.k(), bmT.k()], writes=[modFM.k(0, 32)])
        for r in range(2):
            S.dve(lambda e, r=r: e.scalar_tensor_tensor(out=A1.ap[:, :, r], in0=modFM.ap[:, 16:32, r], scalar=1.0,
                                                        in1=n1g.ap, op0=ALU.add, op1=ALU.mult),
                  reads=[modFM.k(0, 32), n1g.k()], writes=[A1.k()])
        S.dve(lambda e: e.tensor_tensor(out=small.ap[:, 0:1], in0=gvec.ap[:, 0:1], in1=gvec.ap[:, 1:2], op=ALU.mult),
              reads=[gvec.k()], writes=[small.k()])
        S.dve(lambda e: e.tensor_tensor(out=gq2.ap, in0=gqk.ap, in1=gqk.ap, op=ALU.mult), reads=[gqk.k()], writes=[gq2.k()])
        S.dve(lambda e: e.tensor_reduce(out=small.ap[:, 4:5], in_=gq2.ap[:, 0:192], axis=AX.X, op=ALU.max),
              reads=[gq2.k()], writes=[small.k()])
        S.dve(lambda e: e.tensor_reduce(out=small.ap[:, 5:6], in_=gq2.ap[:, 192:384], axis=AX.X, op=ALU.max),
              reads=[gq2.k()], writes=[small.k()])
        S.dve(lambda e: e.tensor_tensor(out=small.ap[:, 6:7], in0=small.ap[:, 4:5], in1=small.ap[:, 5:6], op=ALU.mult),
              reads=[small.k()], writes=[small.k()])
        S.act(lambda e: e.activation(out=small.ap[:, 7:8], in_=small.ap[:, 6:7], func=AF.Sqrt), reads=[small.k()], writes=[small.k()])
        S.dve(lambda e: e.tensor_scalar(out=small.ap[:, 1:2], in0=small.ap[:, 7:8], scalar1=-(192.0 ** 0.5), scalar2=None,
                                        op0=ALU.mult), reads=[small.k()], writes=[small.k()])
        if limit == 0:
            S.finalize(final_reads=[])
            S.emit()
            return nc
        A.ptr = phase_base
        bsl = [A.alloc([128, 16, 128], BF16) for _ in range(2)]
        alloc_slots(4)
        pmB = PA.view(6 * 2048 + 1792, [128, 2], F32)
        conv = []
        for r in range(4):
            conv.append((wg_b[r * 512:(r + 1) * 512, :], w_in[r * 512:(r + 1) * 512, OFF_GATE:OFF_GATE + 2 * D], ("cv", "g", r)))
        conv.append((wba_b, w_bra, ("cv", "ba", 0)))
        conv.append((wbs_b, w_brs, ("cv", "bs", 0)))
        for r in range(2):
            conv.append((wo_b[r * 1024:(r + 1) * 1024, :], w_out[r * 1024:(r + 1) * 1024, :], ("cv", "o", r)))
        for r in range(16):
            conv.append((wf1_b[r * 128:(r + 1) * 128, :], w_f1[r * 128:(r + 1) * 128, :], ("cv", "f1", r)))
        for r in range(11):
            conv.append((wf2_b[r * 512:(r + 1) * 512, :], w_f2[r * 512:(r + 1) * 512, :], ("cv", "f2", r)))
        cv_keys = {}
        for _, _, k_ in conv:
            cv_keys.setdefault(k_[1], []).append(k_)

        def emit_partB(idx):
            c = 32 + idx
            bs_ = bsl[idx % 2]
            S.dma(bs_.ap, w_mod[:, c * 128:(c + 1) * 128].rearrange("(k p) n -> p k n", p=128), writes=[bs_.k()], q="pool")

            def f(e, bs_=bs_):
                i = None
                for k in range(16):
                    i = e.matmul(pmB.ap, lhsT=bs_.ap[:, k, :], rhs=scT.ap[:, k, :], start=(k == 0), stop=(k == 15))
                return i
            S.pe(f, reads=[bs_.k(), scT.k()], writes=[pmB.k()])
            S.dve(lambda e, c=c: e.tensor_scalar(out=modFM.ap[:, c, :], in0=pmB.ap, scalar1=bmT.ap[:, c:c + 1], scalar2=None, op0=ALU.add),
                  reads=[pmB.k(), bmT.k()], writes=[("modB", c)])
        xt = [A.alloc([128, D], F32) for _ in range(2)]
        xs = A.alloc([128, 4, D], BF16)
        hTb = A.alloc([128, 16, 512], BF16)
        wkv = A.alloc([128, 16, 384], BF16)
        wukv = A.alloc([128, 2, 2048], BF16)
        kvnT = A.alloc([128, 2, 512], BF16)
        abc = A.alloc([128, 512], F32)
        sq = [A.alloc([128, 512], BF16) for _ in range(3)]
        sqq = [A.alloc([128, 512], BF16) for _ in range(4)]
        cs = A.alloc([64, 512], F32)
        sn = A.alloc([64, 512], F32)
        KTs = [A.alloc([128, 512], BF16) for _ in range(2)]
        Vs = [A.alloc([128, 1024], BF16) for _ in range(2)]
        rt = [A.alloc([64, 512], F32) for _ in range(4)]
        qcg = A.alloc([128, 4, 512], BF16)
        epsq = A.alloc([128, 512], F32)
        rqbs = [A.alloc([128, 512], F32) for _ in range(2)]
        uT = A.alloc([128, 8, 512], BF16)
        vg = A.alloc([128, 2, 1024], F32)
        vns = [A.alloc([128, 1024], BF16) for _ in range(2)]
        sgs = [A.alloc([128, 8, 128], BF16) for _ in range(2)]
        sgts = [A.alloc([128, 8, 128], F32) for _ in range(2)]
        QNs = [A.alloc([128, 512], BF16) for _ in range(2)]
        QRs = [A.alloc([64, 512], BF16) for _ in range(2)]
        T2 = A.alloc([128, 8, 128], F32)
        wsTb = A.alloc([128, 8, 128], BF16)
        stat = A.alloc([128, 64], F32)
        kq_end = A.ptr

        S.dma(wkv.ap, w_kv.rearrange("(k p) n -> p k n", p=128), writes=[wkv.k()], q="pool")
        S.dma(wukv.ap, w_ukv.rearrange("(k p) n -> p k n", p=128), writes=[wukv.k()], q="pool")
        S.dma(wsTb.ap, wsT_d, writes=[wsTb.k()], q="pool")
        S.dma(T2.ap.rearrange("p a b -> p (a b)"), bs_d, writes=[T2.k()])
        for hh in range(2):
            S.pe(lambda e, hh=hh: e.matmul(PB[hh].ap, lhsT=onesb.ap, rhs=wsTb.ap[:, 4 * hh:4 * hh + 4, :], start=True, stop=True),
                 reads=[onesb.k(), wsTb.k()], writes=[PB[hh].k()])
        for g in range(8):
            S.dve(lambda e, g=g: e.scalar_tensor_tensor(out=T2.ap[:, g, :], in0=PB[g // 4].ap[:, (g % 4) * 128:(g % 4 + 1) * 128],
                                                        scalar=sgun.ap[:, 8 + g:9 + g], in1=T2.ap[:, g, :],
                                                        op0=ALU.mult, op1=ALU.add),
                  reads=[PB[g // 4].k(), sgun.k(), T2.k(g)], writes=[T2.k(g)])

        rkP = PA.view(7 * 2048, [128, 32], F32)
        ptr_b = PA.view(0, [128, 4, 512], BF16)
        sctr = [0]

        def rstd_from_sum(dst, src, scale, n_read_keys, wkeys):
            S.dve(lambda e, dst=dst, src=src, scale=scale: e.tensor_scalar(out=dst, in0=src, scalar1=scale, scalar2=EPS, op0=ALU.mult, op1=ALU.add),
                  reads=n_read_keys, writes=wkeys)
            S.act(lambda e, dst=dst: e.activation(out=dst, in_=dst, func=AF.Sqrt), reads=wkeys, writes=wkeys)
            S.dve(lambda e, dst=dst: e.reciprocal(out=dst, in_=dst), reads=wkeys, writes=wkeys)

        def norm_block(xs, stat, src_rows, ntile, dst_hT, Acol, Bcol, mod_keys):
            for a in range(ntile):
                xb, xkey = src_rows(a)
                c0 = (sctr[0] % 16) * 2
                st = stat.ap[:, c0:c0 + 2]
                stk = stat.k(c0, c0 + 2)
                sctr[0] += 1
                xsa = xs.ap[:, a, :]
                S.dve(lambda e, st=st: e.memset(st, 0.0), writes=[stk])
                S.act(lambda e, xb=xb, st=st, xsa=xsa: e.activation(out=xsa, in_=xb, func=AF.Square, accum_out=st[:, 0:1]),
                      reads=[xkey, stk], writes=[xs.k(a), stk])
                rstd_from_sum(st[:, 1:2], st[:, 0:1], 1.0 / D, [stk], [stk])
                S.act(lambda e, xb=xb, st=st, xsa=xsa: e.activation(out=xsa, in_=xb, func=AF.Copy, scale=st[:, 1:2]),
                      reads=[xkey, stk], writes=[xs.k(a)])
            n = ntile * 128
            nbs = int(os.environ.get("NB_STAGE", 9))
            for jg in range(4 if nbs >= 1 else 0):
                def tr(e, jg=jg, xs=xs, ntile=ntile):
                    i = None
                    for jj in range(4):
                        for a in range(ntile):
                            i = e.transpose(out=ptr_b.ap[:, jj, a * 128:(a + 1) * 128],
                                            in_=xs.ap[:, a, (jg * 4 + jj) * 128:(jg * 4 + jj + 1) * 128], identity=identb.ap)
                    return i
                S.pe(tr, reads=[xs.k(), identb.k()], writes=[ptr_b.k()])
                for jj in range(4 if nbs >= 2 else 0):
                    j = jg * 4 + jj
                    o_ap = dst_hT.ap[:, j, 0:n]
                    i_ap = ptr_b.ap[:, jj, 0:n]
                    sc_ap, bi_ap = Acol(j), Bcol(j)
                    if True:
                        S.act(lambda e, o_ap=o_ap, i_ap=i_ap, sc_ap=sc_ap, bi_ap=bi_ap: e.activation(
                            out=o_ap, in_=i_ap, func=AF.Identity, scale=sc_ap, bias=bi_ap),
                            reads=[ptr_b.k(jj)] + mod_keys, writes=[dst_hT.k(j)])
                    else:
                        S.dve(lambda e, o_ap=o_ap, i_ap=i_ap, sc_ap=sc_ap, bi_ap=bi_ap: e.tensor_scalar(
                            out=o_ap, in0=i_ap, scalar1=sc_ap, scalar2=bi_ap, op0=ALU.mult, op1=ALU.add),
                            reads=[ptr_b.k(jj)] + mod_keys, writes=[dst_hT.k(j)])

        xctr = [0]
        for tb in range(9):
            if tb >= kq_nb and (tb != 8 or kq_sec < 0):
                continue
            is_ctx = tb == 8
            if tb < 8 and kq_nb == 9:
                for q_ in range(8):
                    emit_partB(tb * 8 + q_)
            if kq_nb == 9:
                for _ in range(4):
                    if conv:
                        d_, s_, k_ = conv.pop(0)
                        S.dma_own(d_, s_, writes=[k_])
            ntile = 2 if is_ctx else 4
            ntok = ntile * 128
            r = 1 if is_ctx else 0
            def src_rows(a, tb=tb, is_ctx=is_ctx):
                b = xt[(tb * 4 + a) % 2]
                src = ctxb[a * 128:(a + 1) * 128, :] if is_ctx else xk[tb * 512 + a * 128: tb * 512 + (a + 1) * 128, :]
                S.dma(b.ap, src, writes=[b.k()])
                return b.ap, b.k()

            norm_block(xs, stat, src_rows, ntile, hTb, lambda j, r=r: A1.ap[:, j, r:r + 1], lambda j, r=r: modFM.ap[:, j, r:r + 1], [A1.k(), modFM.k(0, 32)])
            if tb < 4:
                S.dma(hT_d[:, :, tb * 512:(tb + 1) * 512].rearrange("j p t -> p j t"), hTb.ap, reads=[hTb.k()], writes=[("hT", tb)])
            if kq_sec < 1:
                continue
            pk = [PB[2], PB[3], PB[4], PB[5]]
            for m in range(4):
                mc = (m * 128, 128) if m < 2 else (256 + (m - 2) * 64, 64)

                def f(e, m=m, mc=mc, ntok=ntok):
                    i = None
                    for k in range(16):
                        i = e.matmul(pk[m].ap[0:mc[1], 0:ntok], lhsT=wkv.ap[:, k, mc[0]:mc[0] + mc[1]], rhs=hTb.ap[:, k, 0:ntok],
                                     start=(k == 0), stop=(k == 15))
                    return i
                S.pe(f, reads=[wkv.k(), hTb.k()], writes=[pk[m].k()])
            for c in range(2):
                S.act(lambda e, c=c, ntok=ntok: e.activation(out=sq[c].ap[:, 0:ntok], in_=pk[c].ap[:, 0:ntok], func=AF.Square),
                      reads=[pk[c].k()], writes=[sq[c].k()])

            def f(e, ntok=ntok):
                e.matmul(PB[6].ap[:, 0:ntok], lhsT=onesb.ap, rhs=sq[0].ap[:, 0:ntok], start=True, stop=False)
                return e.matmul(PB[6].ap[:, 0:ntok], lhsT=onesb.ap, rhs=sq[1].ap[:, 0:ntok], start=False, stop=True)
            S.pe(f, reads=[onesb.k(), sq[0].k(), sq[1].k()], writes=[PB[6].k()])
            rstd_from_sum(abc.ap[:, 0:ntok], PB[6].ap[:, 0:ntok], 1.0 / 256, [PB[6].k()], [abc.k()])
            for c in range(2):
                S.dve(lambda e, c=c, ntok=ntok: e.scalar_tensor_tensor(out=kvnT.ap[:, c, 0:ntok], in0=pk[c].ap[:, 0:ntok],
                                                                       scalar=gkv.ap[:, c:c + 1], in1=abc.ap[:, 0:ntok],
                                                                       op0=ALU.mult, op1=ALU.mult),
                      reads=[pk[c].k(), gkv.k(), abc.k()], writes=[kvnT.k(c)])
            S.act(lambda e, ntok=ntok: e.activation(out=sq[2].ap[0:64, 0:ntok], in_=pk[2].ap[0:64, 0:ntok], func=AF.Square),
                  reads=[pk[2].k()], writes=[sq[2].k()])
            kcols = slice(tb * 512, tb * 512 + ntok)
            kr_key = krot.k(tb * 512, tb * 512 + ntok)
            if is_ctx:
                S.dve(lambda e, ntok=ntok, kcols=kcols: e.tensor_scalar(out=krot.ap[:, kcols], in0=pk[2].ap[0:64, 0:ntok],
                                                                        scalar1=gvec.ap[0:64, 4:5], scalar2=None, op0=ALU.mult),
                      reads=[pk[2].k(), gvec.k()], writes=[kr_key])
            else:
                S.dma(cs.ap, cos_d[:, tb * 512:(tb + 1) * 512], writes=[cs.k()])
                S.dma(sn.ap, sin_d[:, tb * 512:(tb + 1) * 512], writes=[sn.k()])
                S.dve(lambda e: e.scalar_tensor_tensor(out=rt[0].ap, in0=pk[2].ap[0:64, :], scalar=gvec.ap[0:64, 4:5], in1=cs.ap,
                                                       op0=ALU.mult, op1=ALU.mult),
                      reads=[pk[2].k(), gvec.k(), cs.k()], writes=[rt[0].k()])
                S.dve(lambda e: e.scalar_tensor_tensor(out=rt[1].ap, in0=pk[3].ap[0:64, :], scalar=gvec.ap[0:64, 5:6], in1=sn.ap,
                                                       op0=ALU.mult, op1=ALU.mult),
                      reads=[pk[3].k(), gvec.k(), sn.k()], writes=[rt[1].k()])
                S.dve(lambda e, kcols=kcols: e.tensor_tensor(out=krot.ap[:, kcols], in0=rt[0].ap, in1=rt[1].ap, op=ALU.add),
                      reads=[rt[0].k(), rt[1].k()], writes=[kr_key])
            for h in range(H if kq_sec >= 2 else 0):
                pb = PB[2 + (h % 2)] if False else PB[2 + (h % 4)]

                def f(e, h=h, pb=pb, ntok=ntok):
                    e.matmul(pb.ap[:, 0:ntok], lhsT=wukv.ap[:, 0, h * 128:(h + 1) * 128], rhs=kvnT.ap[:, 0, 0:ntok], start=True, stop=False)
                    return e.matmul(pb.ap[:, 0:ntok], lhsT=wukv.ap[:, 1, h * 128:(h + 1) * 128], rhs=kvnT.ap[:, 1, 0:ntok], start=False, stop=True)
                S.pe(f, reads=[wukv.k(), kvnT.k()], writes=[pb.k()])
                ks = KTs[h % 2]
                S.act(lambda e, pb=pb, ks=ks, ntok=ntok: e.activation(out=ks.ap[:, 0:ntok], in_=pb.ap[:, 0:ntok], func=AF.Copy),
                      reads=[pb.k()], writes=[ks.k()])
                S.dma(KT_d[h, :, tb * 512: tb * 512 + ntok], ks.ap[:, 0:ntok], reads=[ks.k()], writes=[("KT", h, tb)])
                sqb = sq[h % 2]
                S.act(lambda e, pb=pb, sqb=sqb, ntok=ntok: e.activation(out=sqb.ap[:, 0:ntok], in_=pb.ap[:, 0:ntok], func=AF.Square),
                      reads=[pb.k()], writes=[sqb.k()])

                def f(e, h=h, sqb=sqb, tb=tb, ntile=ntile):
                    i = None
                    for a in range(ntile):
                        col = a * H + h
                        e.matmul(rkP.ap[:, col:col + 1], lhsT=sqb.ap[:, a * 128:(a + 1) * 128], rhs=onesb.ap[:, 0:1], start=True, stop=False)
                        i = e.matmul(rkP.ap[:, col:col + 1], lhsT=sq[2].ap[0:64, a * 128:(a + 1) * 128], rhs=onesb.ap[0:64, 0:1],
                                     start=False, stop=True)
                    return i
                S.pe(f, reads=[sqb.k(), sq[2].k(), onesb.k()], writes=[rkP.k()])
            if kq_sec >= 2:
                S.dve(lambda e, tb=tb, ntile=ntile: e.tensor_copy(out=rk.ap[:, tb * 32: tb * 32 + ntile * H], in_=rkP.ap[:, 0:ntile * H]),
                      reads=[rkP.k()], writes=[rk.k(tb * 32, tb * 32 + ntile * H)])
            for a in range(ntile if kq_sec >= 3 else 0):
                kt = tb * 4 + a
                vb = Vs[a % 2]
                for hh in range(2):
                    pb = PB[2 + ((a * 2 + hh) % 4)]

                    def f(e, a=a, hh=hh, pb=pb):
                        e.matmul(pb.ap, lhsT=kvnT.ap[:, 0, a * 128:(a + 1) * 128], rhs=wukv.ap[:, 0, 1024 + hh * 512:1024 + (hh + 1) * 512],
                                 start=True, stop=False)
                        return e.matmul(pb.ap, lhsT=kvnT.ap[:, 1, a * 128:(a + 1) * 128], rhs=wukv.ap[:, 1, 1024 + hh * 512:1024 + (hh + 1) * 512],
                                        start=False, stop=True)
                    S.pe(f, reads=[kvnT.k(), wukv.k()], writes=[pb.k()])
                    if hh == 0:
                        S.dve(lambda e, pb=pb, vb=vb: e.tensor_copy(out=vb.ap[:, 0:512], in_=pb.ap), reads=[pb.k()], writes=[vb.k(0, 512)])
                    else:
                        S.act(lambda e, pb=pb, vb=vb: e.activation(out=vb.ap[:, 512:1024], in_=pb.ap, func=AF.Copy),
                              reads=[pb.k()], writes=[vb.k(512, 1024)])
                S.dma(V_d[kt], vb.ap, reads=[vb.k()], writes=[("V", kt)])

            if tb >= 4 or not kq_own:
                continue
            tsl = slice(tb * 512, (tb + 1) * 512)
            s0, wq0 = wload(w_in[:, 0:256], 16, 256)
            s1, wq1 = wload(w_in[:, 256:512], 16, 256)
            osub = int(os.environ.get("OWN_SUB", 9))
            for c in range(4):
                ws_, wv_ = (s0, wq0) if c < 2 else (s1, wq1)
                pb = PB[2 + (c % 2)]

                def f(e, c=c, wv_=wv_, pb=pb):
                    i = None
                    for k in range(16):
                        i = e.matmul(pb.ap, lhsT=wv_[:, k, (c % 2) * 128:(c % 2 + 1) * 128], rhs=hTb.ap[:, k, :], start=(k == 0), stop=(k == 15))
                    return i
                if osub >= 1:
                    S.pe(f, reads=[ws_.k(), hTb.k()], writes=[pb.k()])
                if osub >= 2:
                    S.dve(lambda e, c=c, pb=pb: e.tensor_scalar(out=qcg.ap[:, c, :], in0=pb.ap, scalar1=gqn.ap[:, c:c + 1], scalar2=None, op0=ALU.mult),
                          reads=[pb.k(), gqn.k()], writes=[qcg.k(c)])
                sqb = sq[c % 2]
                if osub >= 3:
                    S.act(lambda e, pb=pb, sqb=sqb: e.activation(out=sqb.ap, in_=pb.ap, func=AF.Square), reads=[pb.k()], writes=[sqb.k()])
                if osub >= 4:
                    S.pe(lambda e, c=c, sqb=sqb: e.matmul(PB[6].ap, lhsT=onesb.ap, rhs=sqb.ap, start=(c == 0), stop=(c == 3)),
                         reads=[onesb.k(), sqb.k()], writes=[PB[6].k()])
            if osub >= 5:
                S.dve(lambda e: e.tensor_scalar(out=epsq.ap, in0=PB[6].ap, scalar1=EPS / 512.0, scalar2=EPS * EPS, op0=ALU.mult, op1=ALU.add),
                      reads=[PB[6].k()], writes=[epsq.k()])
            own_sec = int(os.environ.get("OWN_SEC", 9))
            if own_sec < 2:
                continue
            sa, wuA = wload(w_uq[:, 0:1024], 4, 1024)
            sb_, wuB = wload(w_uq[:, 1024:2048], 4, 1024)
            S.dma(cs.ap, cos_d[:, tsl], writes=[cs.k()])
            S.dma(sn.ap, sin_d[:, tsl], writes=[sn.k()])
            for h in range(H):
                pn, pr, psw, pss = (PB[2], PB[3], PB[4], PB[5]) if h % 2 == 0 else (PB[0], PB[1], PB[6], PB[7])
                sq0, sq1 = sqq[(h % 2) * 2], sqq[(h % 2) * 2 + 1]
                rqb = rqbs[h % 2]
                rt0, rt1 = rt[(h % 2) * 2], rt[(h % 2) * 2 + 1]

                def f(e, h=h, wuA=wuA, wuB=wuB, pn=pn, pr=pr, psw=psw):
                    i = None
                    for c in range(4):
                        i = e.matmul(pn.ap, lhsT=wuA[:, c, h * 128:(h + 1) * 128], rhs=qcg.ap[:, c, :], start=(c == 0), stop=(c == 3))
                    for c in range(4):
                        i = e.matmul(pr.ap[0:64, :], lhsT=wuB[:, c, h * 64:(h + 1) * 64], rhs=qcg.ap[:, c, :], start=(c == 0), stop=(c == 3))
                    for c in range(4):
                        i = e.matmul(psw.ap[0:64, :], lhsT=wuB[:, c, 512 + h * 64:512 + (h + 1) * 64], rhs=qcg.ap[:, c, :], start=(c == 0), stop=(c == 3))
                    return i
                S.pe(f, reads=[sa.k(), sb_.k(), qcg.k()], writes=[pn.k(), pr.k(), psw.k()])
                S.act(lambda e, pn=pn, sq0=sq0: e.activation(out=sq0.ap, in_=pn.ap, func=AF.Square), reads=[pn.k()], writes=[sq0.k()])
                S.act(lambda e, pr=pr, sq1=sq1: e.activation(out=sq1.ap[0:64, :], in_=pr.ap[0:64, :], func=AF.Square), reads=[pr.k()], writes=[sq1.k()])

                def f(e, pss=pss, sq0=sq0, sq1=sq1):
                    e.matmul(pss.ap, lhsT=onesb.ap, rhs=sq0.ap, start=True, stop=False)
                    return e.matmul(pss.ap, lhsT=onesb.ap[0:64, :], rhs=sq1.ap[0:64, :], start=False, stop=True)
                S.pe(f, reads=[onesb.k(), sq0.k(), sq1.k()], writes=[pss.k()])
                S.dve(lambda e, pss=pss, rqb=rqb: e.scalar_tensor_tensor(out=rqb.ap, in0=pss.ap, scalar=1.0 / 192, in1=epsq.ap, op0=ALU.mult, op1=ALU.add),
                      reads=[pss.k(), epsq.k()], writes=[rqb.k()])
                S.act(lambda e, rqb=rqb: e.activation(out=rqb.ap, in_=rqb.ap, func=AF.Sqrt), reads=[rqb.k()], writes=[rqb.k()])
                S.dve(lambda e, rqb=rqb: e.reciprocal(out=rqb.ap, in_=rqb.ap), reads=[rqb.k()], writes=[rqb.k()])
                qn_, qr_ = QNs[h % 2], QRs[h % 2]
                S.dve(lambda e, qn_=qn_, pn=pn, rqb=rqb: e.scalar_tensor_tensor(out=qn_.ap, in0=pn.ap, scalar=small.ap[:, 0:1], in1=rqb.ap,
                                                                                op0=ALU.mult, op1=ALU.mult),
                      reads=[pn.k(), small.k(), rqb.k()], writes=[qn_.k()])
                S.dve(lambda e, pr=pr, rt0=rt0: e.scalar_tensor_tensor(out=rt0.ap, in0=pr.ap[0:64, :], scalar=gvec.ap[0:64, 2:3], in1=cs.ap,
                                                                       op0=ALU.mult, op1=ALU.mult),
                      reads=[pr.k(), gvec.k(), cs.k()], writes=[rt0.k()])
                S.dve(lambda e, psw=psw, rt1=rt1: e.scalar_tensor_tensor(out=rt1.ap, in0=psw.ap[0:64, :], scalar=gvec.ap[0:64, 3:4], in1=sn.ap,
                                                                         op0=ALU.mult, op1=ALU.mult),
                      reads=[psw.k(), gvec.k(), sn.k()], writes=[rt1.k()])
                S.dve(lambda e, rt0=rt0, rt1=rt1: e.tensor_tensor(out=rt0.ap, in0=rt0.ap, in1=rt1.ap, op=ALU.add),
                      reads=[rt0.k(), rt1.k()], writes=[rt0.k()])
                S.dve(lambda e, qr_=qr_, rt0=rt0, rqb=rqb: e.tensor_tensor(out=qr_.ap, in0=rt0.ap, in1=rqb.ap[0:64, :], op=ALU.mult),
                      reads=[rt0.k(), rqb.k()], writes=[qr_.k()])
                S.dma(QTn_d[h, :, tsl], qn_.ap, reads=[qn_.k()], writes=[("QTn", h, tb)])
                S.dma(QTr_d[h, :, tsl], qr_.ap, reads=[qr_.k()], writes=[("QTr", h, tb)])
            if own_sec < 3:
                continue
            for pnl in range(4):
                s, wu_ = wload(w_in[:, OFF_U + pnl * 256: OFF_U + (pnl + 1) * 256], 16, 256)
                for nt in range(2):
                    g = pnl * 2 + nt
                    pb = PB[2 + (g % 4)]

                    def f(e, wu_=wu_, nt=nt, pb=pb):
                        i = None
                        for k in range(16):
                            i = e.matmul(pb.ap, lhsT=wu_[:, k, nt * 128:(nt + 1) * 128], rhs=hTb.ap[:, k, :], start=(k == 0), stop=(k == 15))
                        return i
                    S.pe(f, reads=[s.k(), hTb.k()], writes=[pb.k()])
                    S.act(lambda e, g=g, pb=pb: e.activation(out=uT.ap[:, g, :], in_=pb.ap, func=AF.Gelu), reads=[pb.k()], writes=[uT.k(g)])
            for half in range(2 if own_sec >= 4 else 0):
                S.dve(lambda e: e.memset(stat.ap[:, 32:48], 0.0), writes=[stat.k(32, 48)])
                for pnl in range(4):
                    s, wv_ = wload(w_in[:, OFF_V + pnl * 256: OFF_V + (pnl + 1) * 256], 16, 256)
                    for t2 in range(2):
                        a = half * 2 + t2
                        pb = PB[2 + ((pnl * 2 + t2) % 4)]

                        def f(e, wv_=wv_, a=a, pb=pb):
                            i = None
                            for k in range(16):
                                i = e.matmul(pb.ap[:, 0:256], lhsT=hTb.ap[:, k, a * 128:(a + 1) * 128], rhs=wv_[:, k, :], start=(k == 0), stop=(k == 15))
                            return i
                        S.pe(f, reads=[s.k(), hTb.k()], writes=[pb.k()])
                        c0 = 32 + t2 * 8 + pnl
                        S.act(lambda e, pb=pb, t2=t2, pnl=pnl, c0=c0: e.activation(out=vg.ap[:, t2, pnl * 256:(pnl + 1) * 256], in_=pb.ap[:, 0:256],
                                                                                   func=AF.Gelu, accum_out=stat.ap[:, c0:c0 + 1]),
                              reads=[pb.k()], writes=[vg.k(t2), stat.k(c0, c0 + 1)])
                        S.act(lambda e, t2=t2, pnl=pnl, c0=c0: e.activation(out=vns[t2].ap[:, pnl * 256:(pnl + 1) * 256], in_=vg.ap[:, t2, pnl * 256:(pnl + 1) * 256],
                                                                            func=AF.Square, accum_out=stat.ap[:, c0 + 4:c0 + 5]),
                              reads=[vg.k(t2)], writes=[vns[t2].k(), stat.k(c0 + 4, c0 + 5)])
                for t2 in range(2):
                    a = half * 2 + t2
                    c0 = 32 + t2 * 8
                    vn, sgt = vns[t2], sgts[t2]
                    spm0, spm1 = (PB[0], PB[1]) if t2 == 0 else (PB[6], PB[7])
                    mu, e2, var = stat.ap[:, 48 + t2 * 4:49 + t2 * 4], stat.ap[:, 49 + t2 * 4:50 + t2 * 4], stat.ap[:, 50 + t2 * 4:51 + t2 * 4]
                    sk = stat.k(48 + t2 * 4, 52 + t2 * 4)
                    sk_in = stat.k(c0, c0 + 8)
                    S.dve(lambda e, c0=c0, mu=mu: e.tensor_reduce(out=mu, in_=stat.ap[:, c0:c0 + 4], axis=AX.X, op=ALU.add), reads=[sk_in], writes=[sk])
                    S.dve(lambda e, c0=c0, e2=e2: e.tensor_reduce(out=e2, in_=stat.ap[:, c0 + 4:c0 + 8], axis=AX.X, op=ALU.add), reads=[sk_in], writes=[sk])
                    S.dve(lambda e, mu=mu: e.tensor_scalar(out=mu, in0=mu, scalar1=1.0 / 1024, scalar2=None, op0=ALU.mult), reads=[sk], writes=[sk])
                    S.dve(lambda e, mu=mu, var=var: e.tensor_tensor(out=var, in0=mu, in1=mu, op=ALU.mult), reads=[sk], writes=[sk])
                    S.dve(lambda e, e2=e2, var=var: e.scalar_tensor_tensor(out=var, in0=e2, scalar=1.0 / 1024, in1=var, op0=ALU.mult, op1=ALU.subtract),
                          reads=[sk], writes=[sk])
                    S.dve(lambda e, var=var: e.tensor_scalar(out=var, in0=var, scalar1=EPS, scalar2=None, op0=ALU.add), reads=[sk], writes=[sk])
                    S.act(lambda e, var=var: e.activation(out=var, in_=var, func=AF.Sqrt), reads=[sk], writes=[sk])
                    S.dve(lambda e, var=var: e.reciprocal(out=var, in_=var), reads=[sk], writes=[sk])
                    S.dve(lambda e, t2=t2, mu=mu, var=var, vn=vn: e.tensor_scalar(out=vn.ap, in0=vg.ap[:, t2, :], scalar1=mu, scalar2=var,
                                                                                  op0=ALU.subtract, op1=ALU.mult),
                          reads=[vg.k(t2), sk], writes=[vn.k()])

                    def f(e, vn=vn, spm0=spm0, spm1=spm1):
                        i = None
                        for g in range(8):
                            pm = spm0 if g < 4 else spm1
                            i = e.matmul(pm.ap[:, (g % 4) * 128:(g % 4 + 1) * 128], lhsT=vn.ap[:, g * 128:(g + 1) * 128], rhs=wsTb.ap[:, g, :],
                                         start=True, stop=True)
                        return i
                    S.pe(f, reads=[vn.k(), wsTb.k()], writes=[spm0.k(), spm1.k()])
                    for g in range(8):
                        pm = spm0 if g < 4 else spm1
                        S.dve(lambda e, g=g, pm=pm, sgt=sgt: e.scalar_tensor_tensor(out=sgt.ap[:, g, :], in0=pm.ap[:, (g % 4) * 128:(g % 4 + 1) * 128],
                                                                                    scalar=sgun.ap[:, g:g + 1], in1=T2.ap[:, g, :], op0=ALU.mult, op1=ALU.add),
                              reads=[pm.k(), sgun.k(), T2.k(g)], writes=[sgt.k(g)])
                    so = sgs[a % 2]
                    S.dve(lambda e, so=so, a=a, sgt=sgt: e.tensor_tensor(out=so.ap, in0=sgt.ap, in1=uT.ap[:, :, a * 128:(a + 1) * 128], op=ALU.mult),
                          reads=[sgt.k(), uT.k()], writes=[so.k()])
                    S.dma(sgT_d[:, :, tb * 512 + a * 128: tb * 512 + (a + 1) * 128].rearrange("g p t -> p g t"), so.ap,
                          reads=[so.k()], writes=[("sgT", tb, a)])

        S.dve(lambda e: e.scalar_tensor_tensor(out=A2.ap, in0=modFM.ap[:, 64:80, 0], scalar=1.0, in1=n2g.ap,
                                               op0=ALU.add, op1=ALU.mult),
              reads=[("modB", c) for c in range(32, 96)] + [n2g.k()], writes=[A2.k()])
        pm1 = PA.view(2048, [128, 128], F32)
        S.pe(lambda e: e.transpose(out=pm1.ap[0:96, :], in_=modFM.ap[:, :, 0], identity=identf.ap),
             reads=[("modB", c) for c in range(32, 96)] + [modFM.k(0, 32), identf.k()], writes=[pm1.k()])
        S.dve(lambda e: e.tensor_copy(out=mrow_sb.ap[0:96, :], in_=pm1.ap[0:96, :]), reads=[pm1.k()], writes=[mrow_sb.k()])
        S.dma(mrow_d, mrow_sb.ap[0:96, :], reads=[mrow_sb.k()], writes=["mrow"])

        S.dve(lambda e: e.tensor_scalar(out=rk.ap, in0=rk.ap, scalar1=1.0, scalar2=192.0 * EPS, op0=ALU.mult, op1=ALU.add),
              reads=[rk.k()], writes=[rk.k()])
        S.act(lambda e: e.activation(out=rk.ap, in_=rk.ap, func=AF.Sqrt), reads=[rk.k()], writes=[rk.k()])
        S.dve(lambda e: e.reciprocal(out=rk.ap, in_=rk.ap), reads=[rk.k()], writes=[rk.k()])
        if debug:
            S.dma(rk_dbg, rk.ap, reads=[rk.k()], writes=["rkdbg"])

        if limit == 1:
            S.finalize(final_reads=list(S.dram_keys))
            S.emit()
            return nc
        A.ptr = phase_base
        KTh = [A.alloc([128, NKEY], BF16) for _ in range(2)]
        Vh = [A.alloc([128, NKT, 128], BF16) for _ in range(2)]
        Qn = [A.alloc([128, NOWN], BF16) for _ in range(2)]
        Qr = [A.alloc([64, NOWN], BF16) for _ in range(2)]
        PT = [A.alloc([128, 1024], BF16) for _ in range(4)]
        rden = A.alloc([128, 1024], F32)
        lnd = A.alloc([128, 1024], F32)
        daccs = [[A.alloc([128, 1024], F32) for _ in range(4)] for _ in range(2)]
        zer = A.alloc([128, 1024], F32)
        posb = [A.alloc([128, 1024], F32) for _ in range(2)]
        ats = [A.alloc([128, NOWN], BF16) for _ in range(2)]
        ST = [PA.view(i * 4096, [128, 1024], F32) for i in range(3)]
        po = PA.view(3 * 4096, [128, 1024], F32)
        S.dve(lambda e: e.memset(zer.ap, 0.0), writes=[zer.k()])
        qctr = 0
        pending = []

        def epilogue(dacc, psb, ab, q0, hh, last_qb, pd):
            d0, d1, d2, d3 = dacc
            S.add("pool", lambda e: e.tensor_tensor(out=d0.ap, in0=d0.ap, in1=d1.ap, op=ALU.add), reads=[d0.k(), d1.k()], writes=[d0.k()])
            S.add("pool", lambda e: e.tensor_tensor(out=d2.ap, in0=d2.ap, in1=d3.ap, op=ALU.add), reads=[d2.k(), d3.k()], writes=[d2.k()])
            S.add("pool", lambda e: e.tensor_tensor(out=d0.ap, in0=d0.ap, in1=d2.ap, op=ALU.add), reads=[d0.k(), d2.k()], writes=[d0.k()])

            def f(e):
                e.matmul(pd.ap[:, 0:512], lhsT=onesf.ap, rhs=d0.ap[:, 0:512], start=True, stop=True)
                return e.matmul(pd.ap[:, 512:1024], lhsT=onesf.ap, rhs=d0.ap[:, 512:1024], start=True, stop=True)
            S.pe(f, reads=[onesf.k(), d0.k()], writes=[pd.k()])
            S.act(lambda e: e.activation(out=lnd.ap, in_=pd.ap, func=AF.Ln), reads=[pd.k()], writes=[lnd.k()])
            S.act(lambda e: e.activation(out=rden.ap, in_=lnd.ap, func=AF.Exp, scale=-1.0), reads=[lnd.k()], writes=[rden.k()])
            S.add("pool", lambda e: e.tensor_tensor(out=ab.ap[:, q0:q0 + 1024], in0=psb.ap, in1=rden.ap, op=ALU.mult),
                  reads=[psb.k(), rden.k()], writes=[ab.k(q0, q0 + 1024)])
            if last_qb:
                S.dma(atT_d[hh], ab.ap, reads=[ab.k()], writes=[("atT", hh)])

        for h in range(H):
            kb, vb, qn_, qr_ = KTh[h % 2], Vh[h % 2], Qn[h % 2], Qr[h % 2]
            S.dma(kb.ap, KT_d[h], reads=[("KT", h, tb) for tb in range(9)], writes=[kb.k()])
            S.dma(vb.ap, V_d[:, :, h * 128:(h + 1) * 128].rearrange("k p d -> p k d"), reads=[("V", kt) for kt in range(NKT)], writes=[vb.k()])
            S.dma(qn_.ap, QTn_d[h], reads=[("QTn", h, tb) for tb in range(4)], writes=[qn_.k()])
            S.dma(qr_.ap, QTr_d[h], reads=[("QTr", h, tb) for tb in range(4)], writes=[qr_.k()])
            ab = ats[h % 2]
            for qb in range(2):
                q0 = qb * 1024
                dacc = daccs[qctr % 2]
                psb = posb[qctr % 2]
                qctr += 1

                def s_op(kt, kb=kb, qn_=qn_, qr_=qr_, q0=q0):
                    pb = ST[kt % 3]

                    def f(e):
                        i = None
                        for hf in range(2):
                            e.matmul(pb.ap[:, hf * 512:(hf + 1) * 512], lhsT=kb.ap[:, kt * 128:(kt + 1) * 128],
                                     rhs=qn_.ap[:, q0 + hf * 512: q0 + (hf + 1) * 512], start=True, stop=False)
                        for hf in range(2):
                            i = e.matmul(pb.ap[:, hf * 512:(hf + 1) * 512], lhsT=krot.ap[:, kt * 128:(kt + 1) * 128],
                                         rhs=qr_.ap[:, q0 + hf * 512: q0 + (hf + 1) * 512], start=False, stop=True)
                        return i
                    S.pe(f, reads=[kb.k(), qn_.k(), qr_.k(), krot.k()], writes=[pb.k()])
                s_op(0)
                s_op(1)
                for kt in range(NKT):
                    if kt == 14 and pending:
                        epilogue(*pending.pop(0), pd=ST[(kt + 2) % 3])
                    if kt + 2 < NKT and not (kt == 14 and False):
                        s_op(kt + 2)
                    pb = ST[kt % 3]
                    pt = PT[kt % 4]
                    S.act(lambda e, pb=pb, pt=pt, kt=kt, h=h: e.activation(out=pt.ap, in_=pb.ap, func=AF.Exp, scale=rk.ap[:, kt * H + h:kt * H + h + 1],
                                                                           bias=small.ap[:, 1:2]),
                          reads=[pb.k(), rk.k(), small.k()], writes=[pt.k()])

                    def f(e, pt=pt, kt=kt, vb=vb):
                        e.matmul(po.ap[:, 0:512], lhsT=vb.ap[:, kt, :], rhs=pt.ap[:, 0:512], start=(kt == 0), stop=(kt == NKT - 1))
                        return e.matmul(po.ap[:, 512:1024], lhsT=vb.ap[:, kt, :], rhs=pt.ap[:, 512:1024], start=(kt == 0), stop=(kt == NKT - 1))
                    S.pe(f, reads=[pt.k(), vb.k()], writes=[po.k()])
                    da = dacc[kt % 4]
                    src0 = zer if kt < 4 else da
                    S.dve(lambda e, pt=pt, da=da, src0=src0: e.tensor_tensor(out=da.ap, in0=src0.ap, in1=pt.ap, op=ALU.add),
                          reads=[pt.k(), src0.k()], writes=[da.k()])
                S.act(lambda e, psb=psb: e.activation(out=psb.ap, in_=po.ap, func=AF.Copy), reads=[po.k()], writes=[psb.k()])
                pending.append((dacc, psb, ab, q0, h, qb == 1))
                if h == H - 1 and qb == 1:
                    while pending:
                        epilogue(*pending.pop(0), pd=ST[0])

        if limit == 2:
            S.finalize(final_reads=["rkdbg"] + [("atT", h) for h in range(H)])
            S.emit()
            return nc
        A.ptr = phase_base
        alloc_slots(7)
        g1bc = A.alloc([128, D], F32)
        g2bc = A.alloc([128, D], F32)
        hTb2s = [A.alloc([128, 16, 512], BF16) for _ in range(1)]
        actT = A.alloc([128, 44, 512], BF16)
        atbs = [A.alloc([128, 8, 512], BF16)]
        sgbs = [A.alloc([128, 8, 512], BF16)]
        mgT = A.view(actT.off + 16384, [128, 16, 512], BF16)
        xnew = A.alloc([128, 4, D], F32)
        xs2 = A.view(actT.off, [128, 4, D], BF16)
        sg = [A.alloc([128, 512], BF16) for _ in range(2)]
        tt = [A.alloc([128, 512], F32) for _ in range(2)]
        stat2 = A.alloc([128, 64], F32)

        S.dma(g1bc.ap, mrow_d[32:48, :].rearrange("a b -> (a b)").partition_broadcast(128), reads=["mrow"], writes=[g1bc.k()])
        S.dma(g2bc.ap, mrow_d[80:96, :].rearrange("a b -> (a b)").partition_broadcast(128), reads=["mrow"], writes=[g2bc.k()])
        out_keys = []
        for tb in range(4):
            tsl = slice(tb * 512, (tb + 1) * 512)
            hTb2, atb, sgb = hTb2s[0], atbs[0], sgbs[0]
            S.dma(hTb2.ap, hT_d[:, :, tsl].rearrange("j p t -> p j t"), reads=[("hT", tb)], writes=[hTb2.k()])
            S.dma(atb.ap, atT_d[:, :, tsl].rearrange("h p t -> p h t"), reads=[("atT", h) for h in range(H)], writes=[atb.k()])
            S.dma(sgb.ap, sgT_d[:, :, tsl].rearrange("g p t -> p g t"), reads=[("sgT", tb, a) for a in range(4)], writes=[sgb.k()])
            for a in range(4):
                S.dma(xnew.ap[:, a, :], xk[tb * 512 + a * 128: tb * 512 + (a + 1) * 128, :], writes=[xnew.k(a)])
            for n2 in range(8):
                c0 = n2 * 256
                s_ga, w_ga = wload_b(wg_b[:, c0:c0 + 256], 16, 256, cv_keys['g'])
                s_gs, w_gs = wload_b(wg_b[:, D + c0: D + c0 + 256], 16, 256, cv_keys['g'])
                s_ba, w_ba = wload_b(wba_b[:, c0:c0 + 256], 8, 256, cv_keys['ba'])
                s_bs, w_bs = wload_b(wbs_b[:, c0:c0 + 256], 8, 256, cv_keys['bs'])
                for nt in range(2):
                    n = n2 * 2 + nt
                    base = 4 * (n % 2)
                    p0, p1, p2, p3 = PB[base], PB[base + 1], PB[base + 2], PB[base + 3]
                    cs_ = slice(nt * 128, (nt + 1) * 128)

                    def f(e, w_ga=w_ga, w_gs=w_gs, w_ba=w_ba, w_bs=w_bs, cs_=cs_, p0=p0, p1=p1, p2=p2, p3=p3, hTb2=hTb2, atb=atb, sgb=sgb):
                        i = None
                        for k in range(16):
                            i = e.matmul(p0.ap, lhsT=w_ga[:, k, cs_], rhs=hTb2.ap[:, k, :], start=(k == 0), stop=(k == 15))
                        for k in range(16):
                            i = e.matmul(p1.ap, lhsT=w_gs[:, k, cs_], rhs=hTb2.ap[:, k, :], start=(k == 0), stop=(k == 15))
                        for k in range(8):
                            i = e.matmul(p2.ap, lhsT=w_ba[:, k, cs_], rhs=atb.ap[:, k, :], start=(k == 0), stop=(k == 7))
                        for k in range(8):
                            i = e.matmul(p3.ap, lhsT=w_bs[:, k, cs_], rhs=sgb.ap[:, k, :], start=(k == 0), stop=(k == 7))
                        return i
                    S.pe(f, reads=[s_ga.k(), s_gs.k(), s_ba.k(), s_bs.k(), hTb2.k(), atb.k(), sgb.k()],
                         writes=[p0.k(), p1.k(), p2.k(), p3.k()])
                    S.act(lambda e, p0=p0: e.activation(out=sg[0].ap, in_=p0.ap, func=AF.Sigmoid), reads=[p0.k()], writes=[sg[0].k()])
                    S.act(lambda e, p1=p1: e.activation(out=sg[1].ap, in_=p1.ap, func=AF.Sigmoid), reads=[p1.k()], writes=[sg[1].k()])
                    S.dve(lambda e, p2=p2: e.tensor_tensor(out=tt[0].ap, in0=p2.ap, in1=sg[0].ap, op=ALU.mult),
                          reads=[p2.k(), sg[0].k()], writes=[tt[0].k()])
                    S.dve(lambda e, p3=p3: e.tensor_tensor(out=tt[1].ap, in0=p3.ap, in1=sg[1].ap, op=ALU.mult),
                          reads=[p3.k(), sg[1].k()], writes=[tt[1].k()])
                    S.dve(lambda e, n=n: e.tensor_tensor(out=mgT.ap[:, n, :], in0=tt[0].ap, in1=tt[1].ap, op=ALU.add),
                          reads=[tt[0].k(), tt[1].k()], writes=[mgT.k(n)])
            for cb in range(4):
                s_a, w_a = wload_b(wo_b[0:1024, cb * 512:(cb + 1) * 512], 8, 512, cv_keys['o'])
                s_b, w_b = wload_b(wo_b[1024:2048, cb * 512:(cb + 1) * 512], 8, 512, cv_keys['o'])
                for a in range(4):
                    pb = PB[(cb * 4 + a) % 8]

                    def f(e, a=a, w_a=w_a, w_b=w_b, pb=pb):
                        i = None
                        for n in range(16):
                            wv_ = w_a if n < 8 else w_b
                            i = e.matmul(pb.ap, lhsT=mgT.ap[:, n, a * 128:(a + 1) * 128], rhs=wv_[:, n % 8, :], start=(n == 0), stop=(n == 15))
                        return i
                    S.pe(f, reads=[s_a.k(), s_b.k(), mgT.k()], writes=[pb.k()])
                    t_ = tt[a % 2]
                    S.dve(lambda e, pb=pb, t_=t_, cb=cb: e.tensor_tensor(out=t_.ap, in0=pb.ap, in1=g1bc.ap[:, cb * 512:(cb + 1) * 512], op=ALU.mult),
                          reads=[pb.k(), g1bc.k()], writes=[t_.k()])
                    xv = xnew.ap[:, a, cb * 512:(cb + 1) * 512]
                    xkk = ("R", "S", xnew.off + a * xnew.part + cb * 2048, xnew.off + a * xnew.part + (cb + 1) * 2048)
                    S.dve(lambda e, xv=xv, t_=t_: e.tensor_tensor(out=xv, in0=xv, in1=t_.ap, op=ALU.add),
                          reads=[t_.k(), xkk], writes=[xkk])
            norm_block(xs2, stat2, lambda a: (xnew.ap[:, a, :], xnew.k(a)), 4, hTb2, lambda j: A2.ap[:, j:j + 1],
                       lambda j: modFM.ap[:, 48 + j, 0:1], [A2.k()] + [("modB", c) for c in range(48, 64)])
            for fp in range(22):
                s_a, w_a = wload_b(wf1_b[:, fp * 256:(fp + 1) * 256], 16, 256, cv_keys['f1'])
                s_b, w_b = wload_b(wf1_b[:, DFF + fp * 256: DFF + (fp + 1) * 256], 16, 256, cv_keys['f1'])
                for ft in range(2):
                    f_ = fp * 2 + ft
                    base = 2 * (f_ % 4)
                    pa, pb = PB[base], PB[base + 1]
                    cs_ = slice(ft * 128, (ft + 1) * 128)

                    def f(e, w_a=w_a, w_b=w_b, cs_=cs_, pa=pa, pb=pb, hTb2=hTb2):
                        i = None
                        for k in range(16):
                            i = e.matmul(pa.ap, lhsT=w_a[:, k, cs_], rhs=hTb2.ap[:, k, :], start=(k == 0), stop=(k == 15))
                        for k in range(16):
                            i = e.matmul(pb.ap, lhsT=w_b[:, k, cs_], rhs=hTb2.ap[:, k, :], start=(k == 0), stop=(k == 15))
                        return i
                    S.pe(f, reads=[s_a.k(), s_b.k(), hTb2.k()], writes=[pa.k(), pb.k()])
                    sl = sg[f_ % 2]
                    S.act(lambda e, pa=pa, sl=sl: e.activation(out=sl.ap, in_=pa.ap, func=AF.Silu), reads=[pa.k()], writes=[sl.k()])
                    S.dve(lambda e, pb=pb, sl=sl, f_=f_: e.tensor_tensor(out=actT.ap[:, f_, :], in0=pb.ap, in1=sl.ap, op=ALU.mult),
                          reads=[pb.k(), sl.k()], writes=[actT.k(f_)])
            for ch in range(2):
                for fg in range(11):
                    s_w, w_w = wload_b(wf2_b[fg * 512:(fg + 1) * 512, ch * 1024:(ch + 1) * 1024], 4, 1024, cv_keys['f2'])

                    def f(e, fg=fg, w_w=w_w):
                        i = None
                        for fi in range(4):
                            f_ = fg * 4 + fi
                            for a in range(4):
                                for c2 in range(2):
                                    i = e.matmul(PB[a * 2 + c2].ap, lhsT=actT.ap[:, f_, a * 128:(a + 1) * 128], rhs=w_w[:, fi, c2 * 512:(c2 + 1) * 512],
                                                 start=(f_ == 0), stop=(f_ == 43))
                        return i
                    S.pe(f, reads=[s_w.k(), actT.k(fg * 4, fg * 4 + 4)], writes=[PB[b_].k() for b_ in range(8)])
                for a in range(4):
                    for c2 in range(2):
                        cb = ch * 2 + c2
                        pb = PB[a * 2 + c2]
                        t_ = tt[(a * 2 + c2) % 2]
                        S.dve(lambda e, pb=pb, t_=t_, cb=cb: e.tensor_tensor(out=t_.ap, in0=pb.ap, in1=g2bc.ap[:, cb * 512:(cb + 1) * 512], op=ALU.mult),
                              reads=[pb.k(), g2bc.k()], writes=[t_.k()])
                        xv = xnew.ap[:, a, cb * 512:(cb + 1) * 512]
                        xkk = ("R", "S", xnew.off + a * xnew.part + cb * 2048, xnew.off + a * xnew.part + (cb + 1) * 2048)
                        S.dve(lambda e, xv=xv, t_=t_: e.tensor_tensor(out=xv, in0=xv, in1=t_.ap, op=ALU.add),
                              reads=[t_.k(), xkk], writes=[xkk])
            for a in range(4):
                S.dma(y[tb * 512 + a * 128: tb * 512 + (a + 1) * 128, :], xnew.ap[:, a, :], reads=[xnew.k(a)], writes=[("y", tb, a)])
                out_keys.append(("y", tb, a))

        fin = list(out_keys)
        if debug:
            fin += ["rkdbg"]
        S.finalize(final_reads=fin)
        S.emit()
    return nc


def _rope_tables():
    nf = 16
    freqs = (np.float32(10000.0) ** (-np.arange(nf, dtype=np.float32) / np.float32(nf))).astype(np.float32)
    pos = np.arange(SEQ)
    row = (pos // 64).astype(np.float32)
    col = (pos % 64).astype(np.float32)
    ang_r = (row[:, None] * freqs[None, :]).astype(np.float32)
    ang_c = (col[:, None] * freqs[None, :]).astype(np.float32)
    cosT = np.zeros((64, SEQ), np.float32)
    sinS = np.zeros((64, SEQ), np.float32)
    for r in range(64):
        ang = ang_r if r < 32 else ang_c
        f = r % 16
        first = (r % 32) < 16
        cosT[r] = np.cos(ang[:, f])
        sinS[r] = (-np.sin(ang[:, f])) if first else np.sin(ang[:, f])
    return cosT, sinS


_SIGMA = np.array([(r + 16) if (r % 32) < 16 else (r - 16) for r in range(64)])


def _fm(v, nchunk):
    return np.ascontiguousarray(np.asarray(v, np.float32).reshape(nchunk, 128).T)


def make_in_maps(inputs):
    f = lambda k: np.asarray(inputs[k], np.float32)
    x, c, ctx, c_ctx = f("x"), f("c"), f("ctx"), f("c_ctx")
    w_in = np.ascontiguousarray(f("w_in")[0])
    w_uq = f("w_uq")[0].reshape(512, H, 192)
    w_ukv = f("w_ukv")[0].reshape(256, H, 256)
    gq, gk = f("qk_norm_q")[0], f("qk_norm_k")[0]
    cosT, sinS = _rope_tables()
    kr = w_in[:, 768:832]
    w_kv = np.ascontiguousarray(np.concatenate([w_in[:, 512:768], kr, kr[:, _SIGMA]], axis=1))
    w_uq_l = np.ascontiguousarray(np.concatenate([
        w_uq[:, :, 0:128].reshape(512, 1024),
        w_uq[:, :, 128:192].reshape(512, 512),
        w_uq[:, :, 128:192][:, :, _SIGMA].reshape(512, 512)], axis=1))
    w_ukv_l = np.ascontiguousarray(np.concatenate([w_ukv[:, :, 0:128].reshape(256, 1024), w_ukv[:, :, 128:256].reshape(256, 1024)], axis=1))
    gvec = np.zeros((128, 8), np.float32)
    gvec[:, 0] = gq[0:128]
    gvec[:, 1] = gk[0:128]
    gvec[:64, 2] = gq[128:192]
    gvec[:64, 3] = gq[128:192][_SIGMA]
    gvec[:64, 4] = gk[128:192]
    gvec[:64, 5] = gk[128:192][_SIGMA]
    gqk = np.ascontiguousarray(np.broadcast_to(np.concatenate([gq, gk])[None, :], (128, 384)))
    sgun = np.ascontiguousarray(np.concatenate([_fm(f("sgu_norm_g")[0], 8), _fm(f("sgu_norm_b")[0], 8)], axis=1))
    wsT = np.ascontiguousarray(f("w_spatial")[0].transpose(2, 0, 1))
    bsbc = np.ascontiguousarray(np.broadcast_to(f("b_spatial")[0].reshape(1, 1024), (128, 1024)))
    shared = {
        "bmT": _fm(f("b_mod")[0], 96), "n1g": _fm(f("norm1_g")[0], 16), "n2g": _fm(f("norm2_g")[0], 16),
        "gqn": _fm(f("q_norm_g")[0], 4), "gkv": _fm(f("kv_norm_g")[0], 2), "gvec": gvec, "gqk": gqk, "sgun": sgun,
        "w_mod": np.ascontiguousarray(f("w_mod")[0]), "w_in": w_in, "w_kv": w_kv, "w_uq": w_uq_l, "w_ukv": w_ukv_l,
        "wsT": wsT, "bsbc": bsbc, "w_bra": np.ascontiguousarray(f("w_br_attn")[0]), "w_brs": np.ascontiguousarray(f("w_br_sgu")[0]),
        "w_out": np.ascontiguousarray(f("w_out")[0]), "w_f1": np.ascontiguousarray(f("w_ffn_in")[0]),
        "w_f2": np.ascontiguousarray(f("w_ffn_out")[0]), "ident": np.eye(128, dtype=np.float32),
    }
    maps = []
    for i in range(8):
        b, hf = i // 2, i % 2
        own = slice(hf * NOWN, (hf + 1) * NOWN)
        oth = slice((1 - hf) * NOWN, (2 - hf) * NOWN)
        m = dict(shared)
        m["xk"] = np.ascontiguousarray(np.concatenate([x[b, own], x[b, oth]], axis=0))
        m["ctxb"] = np.ascontiguousarray(ctx[b])
        cT = np.stack([_fm(c[b], 16), _fm(c_ctx, 16)], axis=-1)
        m["cT"] = np.ascontiguousarray(cT)
        m["cosT"] = np.ascontiguousarray(np.concatenate([cosT[:, own], cosT[:, oth]], axis=1))
        m["sinS"] = np.ascontiguousarray(np.concatenate([sinS[:, own], sinS[:, oth]], axis=1))
        maps.append(m)
    return maps


_NC_CACHE = {}


def kernel(**inputs):
    maps = make_in_maps(inputs)
    if "nc" not in _NC_CACHE:
        _NC_CACHE["nc"] = build(False)
    res = run_bass_kernel_spmd(_NC_CACHE["nc"], maps, core_ids=list(range(8)))
    out = np.empty((4, SEQ, D), np.float32)
    for i in range(8):
        b, hf = i // 2, i % 2
        out[b, hf * NOWN:(hf + 1) * NOWN] = res.results[i]["y"]
    return out
```

```python
import os
import numpy as np
from contextlib import ExitStack
import concourse.bass as bass
import concourse.mybir as mybir
from concourse.bass_utils import run_bass_kernel_spmd

F32 = mybir.dt.float32
BF16 = mybir.dt.bfloat16
AF = mybir.ActivationFunctionType
ALU = mybir.AluOpType
AX = mybir.AxisListType

D = 2048
SEQ = 4096
NOWN = 2048
CTX = 256
NKEY = SEQ + CTX
NKT = NKEY // 128
H = 8
OFF_U = 832
OFF_V = 1856
OFF_GATE = 2880
IN_COLS = 6976
DFF = 5632
EPS = 1e-6
GRAN = 256
ARENA_BYTES = 206 * 1024


class Op:
    __slots__ = ("eng", "fn", "deps", "dma", "waits", "sig", "semval", "sem", "name", "banks", "cost", "odeps")

    def __init__(self, eng, fn, deps, dma, name):
        self.eng, self.fn, self.deps, self.dma, self.name = eng, fn, deps, dma, name
        self.waits = []
        self.sig = False
        self.semval = None
        self.sem = None
        self.banks = ()
        self.cost = 0.5
        self.odeps = set()


class _FakeInst:
    def then_inc(self, *a, **k):
        return self


class _FakeEng:
    def __init__(self):
        self.cost = 0.0
        self.bytes = 0

    @staticmethod
    def _free(ap):
        n = 1
        for d in ap.shape[1:]:
            n *= d
        return n

    def __getattr__(self, name):
        def f(*a, **k):
            if name == "matmul":
                n = self._free(k["rhs"])
                self.cost += max(n, 64) / 2000.0 + 0.01
            elif name == "transpose":
                self.cost += 0.12
            elif name == "dma_start":
                o = k["out"]
                self.bytes += self._free(o) * o.shape[0] * (4 if o.dtype == F32 else 2)
            elif name == "memset":
                self.cost += 0.1
            else:
                o = k.get("out", a[0] if a else None)
                n = self._free(o) if o is not None else 512
                self.cost += 0.2 + n * (0.0065 if name == "reciprocal" else 0.00105)
            return _FakeInst()
        return f


class Sched:
    def __init__(self, nc, n_dma_sems=12):
        self.nc = nc
        self.ops = []
        self.last_w = {}
        self.readers = {}
        self.n_dma_sems = n_dma_sems
        self.dram_keys = []
        self.bank_acc = {}

    @staticmethod
    def _expand(keys):
        out = []
        for k in keys:
            if isinstance(k, tuple) and len(k) == 4 and k[0] == "R":
                for g in range(k[2] // GRAN, (k[3] + GRAN - 1) // GRAN):
                    out.append((k[1], g))
            else:
                out.append(k)
        return out

    def add(self, eng, fn, reads=(), writes=(), dma=False, name=""):
        i = len(self.ops)
        for k in writes:
            if not (isinstance(k, tuple) and len(k) == 4 and k[0] == "R"):
                self.dram_keys.append(k)
        reads = self._expand(reads)
        writes = self._expand(writes)
        deps = set()
        lw, rd = self.last_w, self.readers
        for k in reads:
            j = lw.get(k)
            if j is not None:
                deps.add(j)
        for k in writes:
            j = lw.get(k)
            if j is not None:
                deps.add(j)
            r = rd.get(k)
            if r:
                deps.update(r)
        banks = set()
        for k in reads + writes:
            if isinstance(k, tuple) and len(k) == 2 and k[0] == "P":
                banks.add(k[1] // 8)
        for b in banks:
            d = self.bank_acc.setdefault(b, {})
            for e2, idx in d.items():
                if e2 != eng:
                    deps.add(idx)
            d[eng] = i
        deps.discard(i)
        for k in reads:
            rd.setdefault(k, []).append(i)
        for k in writes:
            lw[k] = i
            rd[k] = []
        op = Op(eng, fn, deps, dma, name)
        op.banks = tuple(banks)
        self.ops.append(op)
        return i

    def pe(self, fn, reads=(), writes=(), name=""):
        return self.add("pe", fn, reads, writes, name=name)

    def act(self, fn, reads=(), writes=(), name=""):
        return self.add("act", fn, reads, writes, name=name)

    def dve(self, fn, reads=(), writes=(), name=""):
        return self.add("dve", fn, reads, writes, name=name)

    def dma(self, out, in_, reads=(), writes=(), q="sp", name="", **kw):
        return self.add(q, lambda e: e.dma_start(out=out, in_=in_, **kw), reads, writes,
                        dma=True, name=name)

    def reorder(self, window=48):
        ops = self.ops
        n = len(ops)
        for op in ops:
            if op.fn is None:
                op.cost = 0.0
                continue
            fe = _FakeEng()
            op.fn(fe)
            if op.dma:
                op.cost = (1.0 if op.eng == "pool" else 0.06, 2.0 + fe.bytes / 250e3)
            else:
                op.cost = fe.cost * (2.5 if op.eng == "pool" else 1.0)
        lastb = {}
        for i, op in enumerate(ops):
            op.odeps = set()
            for b in op.banks:
                j = lastb.get((op.eng, b))
                if j is not None:
                    op.odeps.add(j)
                lastb[(op.eng, b)] = i
        engs = ("pe", "act", "dve", "pool", "sp")
        queues = {e: [i for i, op in enumerate(ops) if op.eng == e] for e in engs}
        qpos = {e: 0 for e in engs}
        succ = [[] for _ in range(n)]
        npred = [0] * n
        for i, op in enumerate(ops):
            ps = op.deps | op.odeps
            npred[i] = len(ps)
            for j in ps:
                succ[j].append(i)
        ready_t = [0.0] * n
        done = [False] * n
        free_t = {e: 0.0 for e in engs}
        last_i = n - 1
        order = []
        remaining = n
        while remaining:
            best = None
            for e in engs:
                q = queues[e]
                p = qpos[e]
                while p < len(q) and done[q[p]]:
                    p += 1
                qpos[e] = p
                cnt = 0
                k = p
                while k < len(q) and cnt < window:
                    i = q[k]
                    k += 1
                    if done[i]:
                        continue
                    cnt += 1
                    if npred[i] or (i == last_i and remaining > 1):
                        continue
                    st = max(free_t[e], ready_t[i])
                    if best is None or st < best[0] - 1e-9 or (abs(st - best[0]) <= 1e-9 and i < best[1]):
                        best = (st, i, e)
                    if ready_t[i] <= free_t[e]:
                        break
            assert best is not None, "scheduler deadlock"
            st, i, e = best
            op = ops[i]
            if op.dma:
                free_t[e] = st + op.cost[0]
                fin = st + op.cost[0] + op.cost[1]
            else:
                free_t[e] = st + op.cost
                fin = free_t[e] + 0.15
            done[i] = True
            remaining -= 1
            order.append(i)
            for j in succ[i]:
                npred[j] -= 1
                if ready_t[j] < fin:
                    ready_t[j] = fin
        newidx = {old: new for new, old in enumerate(order)}
        new_ops = []
        for old in order:
            op = ops[old]
            op.deps = set(newidx[j] for j in op.deps)
            new_ops.append(op)
        self.ops = new_ops
        self.est_time = max(free_t.values())

    def finalize(self, final_reads=(), reorder=True):
        self.add("sp", None, reads=final_reads, name="final")
        if reorder:
            self.reorder()
        ops = self.ops
        engs = ("pe", "act", "dve", "pool", "sp")
        cur = {e: {} for e in engs}
        dma_known = {e: set() for e in engs}
        clock = [None] * len(ops)
        dma_rr = {e: 0 for e in engs}
        dma_last = {}
        dma_uses = {}
        for i, op in enumerate(ops):
            E = op.eng
            c = cur[E]
            deps = set(op.deps)
            if op.dma:
                slot = dma_rr[E] % self.n_dma_sems
                dma_rr[E] += 1
                prev = dma_last.get((E, slot))
                if prev is not None:
                    deps.add(prev)
                dma_last[(E, slot)] = i
                dma_uses[(E, slot)] = dma_uses.get((E, slot), 0) + 1
                op.sem = (E, slot)
                op.semval = 16 * dma_uses[(E, slot)]
            waits = []
            for j in sorted(deps):
                oj = ops[j]
                if oj.dma:
                    if j in dma_known[E]:
                        continue
                    waits.append(j)
                    dma_known[E].add(j)
                else:
                    if c.get(oj.eng, -1) >= j:
                        continue
                    if oj.eng == E and E == "pe":
                        continue
                    waits.append(j)
                    oj.sig = True
            for j in waits:
                for k, v in clock[j].items():
                    if c.get(k, -1) < v:
                        c[k] = v
            op.waits = waits
            ck = dict(c)
            if not op.dma:
                ck[E] = i
                c[E] = max(c.get(E, -1), -1)
            clock[i] = ck
        cnt = {e: 0 for e in engs}
        for op in ops:
            if op.sig and not op.dma:
                cnt[op.eng] += 1
                op.semval = cnt[op.eng]
        self.sig_counts = cnt
        return self

    def emit(self):
        nc = self.nc
        ops = self.ops
        engs = ("pe", "act", "dve", "pool", "sp")
        sems = {}
        for e in ("pe", "act", "dve", "pool"):
            sems[e] = nc.alloc_semaphore(name=f"sig_{e}")
        used = set(op.sem for op in ops if op.dma)
        for key in sorted(used):
            sems[key] = nc.alloc_semaphore(name=f"dma_{key[0]}_{key[1]}")
        per_eng = {e: [op for op in ops if op.eng == e] for e in engs}

        def run(e_name):
            def body(eng):
                for op in per_eng[e_name]:
                    for j in op.waits:
                        oj = ops[j]
                        eng.wait_ge(sems[oj.sem] if oj.dma else sems[oj.eng], oj.semval)
                    if op.fn is None:
                        continue
                    inst = op.fn(eng)
                    if op.dma:
                        inst.then_inc(sems[op.sem], 16)
                    elif op.sig:
                        inst.then_inc(sems[op.eng], 1)
            return body

        with nc.Block() as block:
            block.sync(run("sp"))
            block.tensor(run("pe"))
            block.scalar(run("act"))
            block.vector(run("dve"))
            block.gpsimd(run("pool"))


class Buf:
    def __init__(self, space, base, off, shape, dt):
        self.space, self.off, self.shape, self.dt = space, off, list(shape), dt
        esz = 4 if dt == F32 else 2
        n = 1
        for s in shape[1:]:
            n *= s
        self.nbytes = n * esz
        pad = (self.nbytes + 3) // 4
        v = base[:, off // 4: off // 4 + pad]
        if dt != F32:
            v = v.bitcast(dt)
        v = v[:, 0:n]
        if len(shape) == 3:
            v = v.rearrange("p (a b) -> p a b", a=shape[1])
        elif len(shape) == 4:
            v = v.rearrange("p (a b c) -> p a b c", a=shape[1], b=shape[2])
        if shape[0] < 128:
            v = v[0:shape[0]]
        self.ap = v
        self.part = self.nbytes // shape[1]

    def k(self, i=None, j=None):
        if i is None:
            return ("R", self.space, self.off, self.off + self.nbytes)
        if j is None:
            j = i + 1
        return ("R", self.space, self.off + i * self.part, self.off + j * self.part)


class Arena:
    def __init__(self, space, base, limit):
        self.space, self.base, self.limit, self.ptr = space, base, limit, 0

    def alloc(self, shape, dt):
        b = Buf(self.space, self.base, self.ptr, shape, dt)
        self.ptr += (b.nbytes + GRAN - 1) // GRAN * GRAN
        assert self.ptr <= self.limit, ("arena overflow", self.space, self.ptr)
        return b

    def view(self, off, shape, dt):
        return Buf(self.space, self.base, off, shape, dt)


def build(debug=False, limit=9, kq_nb=9, kq_own=True, kq_sec=9):
    nc = bass.Bass("TRN2", target_bir_lowering=False)

    def din(name, shape):
        return nc.dram_tensor(name, list(shape), F32, kind="ExternalInput").ap()

    def dscr(name, shape, dt):
        return nc.dram_tensor(name, list(shape), dt, kind="ExternalOutput" if debug else "Internal").ap()

    xk = din("xk", [SEQ, D])
    ctxb = din("ctxb", [CTX, D])
    cT_d = din("cT", [128, 16, 2])
    bmT_d = din("bmT", [128, 96])
    n1g_d = din("n1g", [128, 16])
    n2g_d = din("n2g", [128, 16])
    gqn_d = din("gqn", [128, 4])
    gkv_d = din("gkv", [128, 2])
    gvec_d = din("gvec", [128, 8])
    gqk_d = din("gqk", [128, 384])
    sgun_d = din("sgun", [128, 16])
    w_mod = din("w_mod", [D, 6 * D])
    w_in = din("w_in", [D, IN_COLS])
    w_kv = din("w_kv", [D, 384])
    w_uq = din("w_uq", [512, 2048])
    w_ukv = din("w_ukv", [256, 2048])
    wsT_d = din("wsT", [128, 8, 128])
    bs_d = din("bsbc", [128, 1024])
    w_bra = din("w_bra", [1024, D])
    w_brs = din("w_brs", [1024, D])
    w_out = din("w_out", [D, D])
    w_f1 = din("w_f1", [D, 2 * DFF])
    w_f2 = din("w_f2", [DFF, D])
    ident_d = din("ident", [128, 128])
    cos_d = din("cosT", [64, SEQ])
    sin_d = din("sinS", [64, SEQ])
    y = nc.dram_tensor("y", [NOWN, D], F32, kind="ExternalOutput").ap()

    hT_d = dscr("hT_s", [16, 128, NOWN], BF16)
    KT_d = dscr("KT_s", [H, 128, NKEY], BF16)
    V_d = dscr("V_s", [NKT, 128, 1024], BF16)
    QTn_d = dscr("QTn_s", [H, 128, NOWN], BF16)
    QTr_d = dscr("QTr_s", [H, 64, NOWN], BF16)
    sgT_d = dscr("sgT_s", [8, 128, NOWN], BF16)
    atT_d = dscr("atT_s", [H, 128, NOWN], BF16)
    mrow_d = dscr("mrow_s", [96, 128], F32)
    rk_dbg = dscr("rk_s", [128, NKT * H], F32) if debug else None

    es = ExitStack()
    with es:
        arena_t = es.enter_context(nc.sbuf_tensor("arena", [128, ARENA_BYTES // 4], F32))
        psum_t = es.enter_context(nc.psum_tensor("psum", [128, 4096], F32))
        S = Sched(nc)
        A = Arena("S", arena_t, ARENA_BYTES)
        PA = Arena("P", psum_t, 16384)
        PB = [PA.view(b * 2048, [128, 512], F32) for b in range(8)]

        identb = A.alloc([128, 128], BF16)
        identf = A.alloc([128, 128], F32)
        onesb = A.alloc([128, 128], BF16)
        onesf = A.alloc([128, 128], F32)
        scT = A.alloc([128, 16, 2], BF16)
        mrow_sb = A.alloc([128, 128], F32)
        modFM = A.alloc([128, 96, 2], F32)
        A1 = A.alloc([128, 16, 2], F32)
        A2 = A.alloc([128, 16], F32)
        bmT = A.alloc([128, 96], F32)
        n1g = A.alloc([128, 16], F32)
        n2g = A.alloc([128, 16], F32)
        gqn = A.alloc([128, 4], F32)
        gkv = A.alloc([128, 2], F32)
        gvec = A.alloc([128, 8], F32)
        sgun = A.alloc([128, 16], F32)
        small = A.alloc([128, 16], F32)
        rk = A.alloc([128, NKT * H], F32)
        krot = A.alloc([64, NKEY], BF16)
        wslot = []
        wctr = [0]

        def alloc_slots(n):
            wslot.clear()
            wslot.extend(A.alloc([128, 4096], BF16) for _ in range(n))

        def next_slot():
            s = wslot[wctr[0] % len(wslot)]
            wctr[0] += 1
            return s

        def wload(src_ap, kchunks, ncols, reads=()):
            s = next_slot()
            assert kchunks * ncols <= 4096
            v = s.ap[:, 0:kchunks * ncols].rearrange("p (k n) -> p k n", k=kchunks)
            S.dma(v, src_ap.rearrange("(k p) n -> p k n", p=128), reads=list(reads), writes=[s.k()], q="pool")
            return s, v

        phase_base = A.ptr

        alloc_slots(6)
        cT = A.alloc([128, 16, 2], F32)
        gqk = A.alloc([128, 384], F32)
        gq2 = A.alloc([128, 384], F32)

        S.dma(identf.ap, ident_d, writes=[identf.k()])
        S.dma(cT.ap, cT_d, writes=[cT.k()])
        S.dma(bmT.ap, bmT_d, writes=[bmT.k()])
        S.dma(n1g.ap, n1g_d, writes=[n1g.k()])
        S.dma(n2g.ap, n2g_d, writes=[n2g.k()])
        S.dma(gqn.ap, gqn_d, writes=[gqn.k()])
        S.dma(gkv.ap, gkv_d, writes=[gkv.k()])
        S.dma(gvec.ap, gvec_d, writes=[gvec.k()])
        S.dma(sgun.ap, sgun_d, writes=[sgun.k()])
        S.dma(gqk.ap, gqk_d, writes=[gqk.k()])
        S.dve(lambda e: e.tensor_copy(out=identb.ap, in_=identf.ap), reads=[identf.k()], writes=[identb.k()])
        S.dve(lambda e: e.memset(onesb.ap, 1.0), writes=[onesb.k()])
        S.dve(lambda e: e.memset(onesf.ap, 1.0), writes=[onesf.k()])
        S.dve(lambda e: e.memset(small.ap[:, 2:3], EPS), writes=[small.k()])
        S.act(lambda e: e.activation(out=scT.ap, in_=cT.ap, func=AF.Silu), reads=[cT.k()], writes=[scT.k()])

        pmod = PA.view(0, [128, 96, 2], F32)
        for pn in range(16):
            s, v = wload(w_mod[:, pn * 256:(pn + 1) * 256], 16, 256)

            def f(e, v=v, pn=pn):
                i = None
                for nt in range(2):
                    col = pn * 2 + nt
                    for k in range(16):
                        i = e.matmul(pmod.ap[:, col, :], lhsT=v[:, k, nt * 128:(nt + 1) * 128], rhs=scT.ap[:, k, :],
                                     start=(k == 0), stop=(k == 15))
                return i
            S.pe(f, reads=[s.k(), scT.k()], writes=[pmod.k()])
        for r in range(2):
            S.dve(lambda e, r=r: e.tensor_tensor(out=modFM.ap[:, 0:32, r], in0=pmod.ap[:, 0:32, r], in1=bmT.ap[:, 0:32], op=ALU.add),
                  reads=[pmod.k(), bmT.k()], writes=[modFM.k(0, 32)])
        for r in range(2):
            S.dve(lambda e, r=r: e.scalar_tensor_tensor(out=A1.ap[:, :, r], in0=modFM.ap[:, 16:32, r], scalar=1.0,
                                                        in1=n1g.ap, op0=ALU.add, op1=ALU.mult),
                  reads=[modFM.k(0, 32), n1g.k()], writes=[A1.k()])
        S.dve(lambda e: e.tensor_tensor(out=small.ap[:, 0:1], in0=gvec.ap[:, 0:1], in1=gvec.ap[:, 1:2], op=ALU.mult),
              reads=[gvec.k()], writes=[small.k()])
        S.dve(lambda e: e.tensor_tensor(out=gq2.ap, in0=gqk.ap, in1=gqk.ap, op=ALU.mult), reads=[gqk.k()], writes=[gq2.k()])
        S.dve(lambda e: e.tensor_reduce(out=small.ap[:, 4:5], in_=gq2.ap[:, 0:192], axis=AX.X, op=ALU.max),
              reads=[gq2.k()], writes=[small.k()])
        S.dve(lambda e: e.tensor_reduce(out=small.ap[:, 5:6], in_=gq2.ap[:, 192:384], axis=AX.X, op=ALU.max),
              reads=[gq2.k()], writes=[small.k()])
        S.dve(lambda e: e.tensor_tensor(out=small.ap[:, 6:7], in0=small.ap[:, 4:5], in1=small.ap[:, 5:6], op=ALU.mult),
              reads=[small.k()], writes=[small.k()])
        S.act(lambda e: e.activation(out=small.ap[:, 7:8], in_=small.ap[:, 6:7], func=AF.Sqrt), reads=[small.k()], writes=[small.k()])
        S.dve(lambda e: e.tensor_scalar(out=small.ap[:, 1:2], in0=small.ap[:, 7:8], scalar1=-(192.0 ** 0.5), scalar2=None,
                                        op0=ALU.mult), reads=[small.k()], writes=[small.k()])
        if limit == 0:
            S.finalize(final_reads=[])
            S.emit()
            return nc
        A.ptr = phase_base
        bsl = [A.alloc([128, 16, 128], BF16) for _ in range(2)]
        alloc_slots(4)
        pmB = PA.view(6 * 2048 + 1792, [128, 2], F32)

        def emit_partB(idx):
            c = 32 + idx
            bs_ = bsl[idx % 2]
            S.dma(bs_.ap, w_mod[:, c * 128:(c + 1) * 128].rearrange("(k p) n -> p k n", p=128), writes=[bs_.k()], q="pool")

            def f(e, bs_=bs_):
                i = None
                for k in range(16):
                    i = e.matmul(pmB.ap, lhsT=bs_.ap[:, k, :], rhs=scT.ap[:, k, :], start=(k == 0), stop=(k == 15))
                return i
            S.pe(f, reads=[bs_.k(), scT.k()], writes=[pmB.k()])
            S.dve(lambda e, c=c: e.tensor_scalar(out=modFM.ap[:, c, :], in0=pmB.ap, scalar1=bmT.ap[:, c:c + 1], scalar2=None, op0=ALU.add),
                  reads=[pmB.k(), bmT.k()], writes=[("modB", c)])
        xt = [A.alloc([128, D], F32) for _ in range(2)]
        xs = A.alloc([128, 4, D], BF16)
        hTb = A.alloc([128, 16, 512], BF16)
        wkv = A.alloc([128, 16, 384], BF16)
        wukv = A.alloc([128, 2, 2048], BF16)
        kvnT = A.alloc([128, 2, 512], BF16)
        abc = A.alloc([128, 512], F32)
        sq = [A.alloc([128, 512], BF16) for _ in range(3)]
        sqq = [A.alloc([128, 512], BF16) for _ in range(4)]
        cs = A.alloc([64, 512], F32)
        sn = A.alloc([64, 512], F32)
        KTs = [A.alloc([128, 512], BF16) for _ in range(2)]
        Vs = [A.alloc([128, 1024], BF16) for _ in range(2)]
        rt = [A.alloc([64, 512], F32) for _ in range(4)]
        qcg = A.alloc([128, 4, 512], BF16)
        epsq = A.alloc([128, 512], F32)
        rqbs = [A.alloc([128, 512], F32) for _ in range(2)]
        uT = A.alloc([128, 8, 512], BF16)
        vg = A.alloc([128, 2, 1024], F32)
        vns = [A.alloc([128, 1024], BF16) for _ in range(2)]
        sgs = [A.alloc([128, 8, 128], BF16) for _ in range(2)]
        sgts = [A.alloc([128, 8, 128], F32) for _ in range(2)]
        QNs = [A.alloc([128, 512], BF16) for _ in range(2)]
        QRs = [A.alloc([64, 512], BF16) for _ in range(2)]
        T2 = A.alloc([128, 8, 128], F32)
        wsTb = A.alloc([128, 8, 128], BF16)
        stat = A.alloc([128, 64], F32)
        kq_end = A.ptr

        S.dma(wkv.ap, w_kv.rearrange("(k p) n -> p k n", p=128), writes=[wkv.k()], q="pool")
        S.dma(wukv.ap, w_ukv.rearrange("(k p) n -> p k n", p=128), writes=[wukv.k()], q="pool")
        S.dma(wsTb.ap, wsT_d, writes=[wsTb.k()], q="pool")
        S.dma(T2.ap.rearrange("p a b -> p (a b)"), bs_d, writes=[T2.k()])
        for hh in range(2):
            S.pe(lambda e, hh=hh: e.matmul(PB[hh].ap, lhsT=onesb.ap, rhs=wsTb.ap[:, 4 * hh:4 * hh + 4, :], start=True, stop=True),
                 reads=[onesb.k(), wsTb.k()], writes=[PB[hh].k()])
        for g in range(8):
            S.dve(lambda e, g=g: e.scalar_tensor_tensor(out=T2.ap[:, g, :], in0=PB[g // 4].ap[:, (g % 4) * 128:(g % 4 + 1) * 128],
                                                        scalar=sgun.ap[:, 8 + g:9 + g], in1=T2.ap[:, g, :],
                                                        op0=ALU.mult, op1=ALU.add),
                  reads=[PB[g // 4].k(), sgun.k(), T2.k(g)], writes=[T2.k(g)])

        rkP = PA.view(7 * 2048, [128, 32], F32)
        ptr_b = PA.view(0, [128, 4, 512], BF16)
        sctr = [0]

        def rstd_from_sum(dst, src, scale, n_read_keys, wkeys):
            S.dve(lambda e, dst=dst, src=src, scale=scale: e.tensor_scalar(out=dst, in0=src, scalar1=scale, scalar2=EPS, op0=ALU.mult, op1=ALU.add),
                  reads=n_read_keys, writes=wkeys)
            S.act(lambda e, dst=dst: e.activation(out=dst, in_=dst, func=AF.Sqrt), reads=wkeys, writes=wkeys)
            S.dve(lambda e, dst=dst: e.reciprocal(out=dst, in_=dst), reads=wkeys, writes=wkeys)

        def norm_block(xs, stat, src_rows, ntile, dst_hT, Acol, Bcol, mod_keys):
            for a in range(ntile):
                xb, xkey = src_rows(a)
                c0 = (sctr[0] % 16) * 2
                st = stat.ap[:, c0:c0 + 2]
                stk = stat.k(c0, c0 + 2)
                sctr[0] += 1
                xsa = xs.ap[:, a, :]
                S.dve(lambda e, st=st: e.memset(st, 0.0), writes=[stk])
                S.act(lambda e, xb=xb, st=st, xsa=xsa: e.activation(out=xsa, in_=xb, func=AF.Square, accum_out=st[:, 0:1]),
                      reads=[xkey, stk], writes=[xs.k(a), stk])
                rstd_from_sum(st[:, 1:2], st[:, 0:1], 1.0 / D, [stk], [stk])
                S.act(lambda e, xb=xb, st=st, xsa=xsa: e.activation(out=xsa, in_=xb, func=AF.Copy, scale=st[:, 1:2]),
                      reads=[xkey, stk], writes=[xs.k(a)])
            n = ntile * 128
            nbs = int(os.environ.get("NB_STAGE", 9))
            for jg in range(4 if nbs >= 1 else 0):
                def tr(e, jg=jg, xs=xs, ntile=ntile):
                    i = None
                    for jj in range(4):
                        for a in range(ntile):
                            i = e.transpose(out=ptr_b.ap[:, jj, a * 128:(a + 1) * 128],
                                            in_=xs.ap[:, a, (jg * 4 + jj) * 128:(jg * 4 + jj + 1) * 128], identity=identb.ap)
                    return i
                S.pe(tr, reads=[xs.k(), identb.k()], writes=[ptr_b.k()])
                for jj in range(4 if nbs >= 2 else 0):
                    j = jg * 4 + jj
                    o_ap = dst_hT.ap[:, j, 0:n]
                    i_ap = ptr_b.ap[:, jj, 0:n]
                    sc_ap, bi_ap = Acol(j), Bcol(j)
                    if True:
                        S.act(lambda e, o_ap=o_ap, i_ap=i_ap, sc_ap=sc_ap, bi_ap=bi_ap: e.activation(
                            out=o_ap, in_=i_ap, func=AF.Identity, scale=sc_ap, bias=bi_ap),
                            reads=[ptr_b.k(jj)] + mod_keys, writes=[dst_hT.k(j)])
                    else:
                        S.dve(lambda e, o_ap=o_ap, i_ap=i_ap, sc_ap=sc_ap, bi_ap=bi_ap: e.tensor_scalar(
                            out=o_ap, in0=i_ap, scalar1=sc_ap, scalar2=bi_ap, op0=ALU.mult, op1=ALU.add),
                            reads=[ptr_b.k(jj)] + mod_keys, writes=[dst_hT.k(j)])

        xctr = [0]
        for tb in range(9):
            if tb >= kq_nb and (tb != 8 or kq_sec < 0):
                continue
            is_ctx = tb == 8
            if tb < 8 and kq_nb == 9:
                for q_ in range(8):
                    emit_partB(tb * 8 + q_)
            ntile = 2 if is_ctx else 4
            ntok = ntile * 128
            r = 1 if is_ctx else 0
            def src_rows(a, tb=tb, is_ctx=is_ctx):
                b = xt[(tb * 4 + a) % 2]
                src = ctxb[a * 128:(a + 1) * 128, :] if is_ctx else xk[tb * 512 + a * 128: tb * 512 + (a + 1) * 128, :]
                S.dma(b.ap, src, writes=[b.k()])
                return b.ap, b.k()

            norm_block(xs, stat, src_rows, ntile, hTb, lambda j, r=r: A1.ap[:, j, r:r + 1], lambda j, r=r: modFM.ap[:, j, r:r + 1], [A1.k(), modFM.k(0, 32)])
            if tb < 4:
                S.dma(hT_d[:, :, tb * 512:(tb + 1) * 512].rearrange("j p t -> p j t"), hTb.ap, reads=[hTb.k()], writes=[("hT", tb)])
            if kq_sec < 1:
                continue
            pk = [PB[2], PB[3], PB[4], PB[5]]
            for m in range(4):
                mc = (m * 128, 128) if m < 2 else (256 + (m - 2) * 64, 64)

                def f(e, m=m, mc=mc, ntok=ntok):
                    i = None
                    for k in range(16):
                        i = e.matmul(pk[m].ap[0:mc[1], 0:ntok], lhsT=wkv.ap[:, k, mc[0]:mc[0] + mc[1]], rhs=hTb.ap[:, k, 0:ntok],
                                     start=(k == 0), stop=(k == 15))
                    return i
                S.pe(f, reads=[wkv.k(), hTb.k()], writes=[pk[m].k()])
            for c in range(2):
                S.act(lambda e, c=c, ntok=ntok: e.activation(out=sq[c].ap[:, 0:ntok], in_=pk[c].ap[:, 0:ntok], func=AF.Square),
                      reads=[pk[c].k()], writes=[sq[c].k()])

            def f(e, ntok=ntok):
                e.matmul(PB[6].ap[:, 0:ntok], lhsT=onesb.ap, rhs=sq[0].ap[:, 0:ntok], start=True, stop=False)
                return e.matmul(PB[6].ap[:, 0:ntok], lhsT=onesb.ap, rhs=sq[1].ap[:, 0:ntok], start=False, stop=True)
            S.pe(f, reads=[onesb.k(), sq[0].k(), sq[1].k()], writes=[PB[6].k()])
            rstd_from_sum(abc.ap[:, 0:ntok], PB[6].ap[:, 0:ntok], 1.0 / 256, [PB[6].k()], [abc.k()])
            for c in range(2):
                S.dve(lambda e, c=c, ntok=ntok: e.scalar_tensor_tensor(out=kvnT.ap[:, c, 0:ntok], in0=pk[c].ap[:, 0:ntok],
                                                                       scalar=gkv.ap[:, c:c + 1], in1=abc.ap[:, 0:ntok],
                                                                       op0=ALU.mult, op1=ALU.mult),
                      reads=[pk[c].k(), gkv.k(), abc.k()], writes=[kvnT.k(c)])
            S.act(lambda e, ntok=ntok: e.activation(out=sq[2].ap[0:64, 0:ntok], in_=pk[2].ap[0:64, 0:ntok], func=AF.Square),
                  reads=[pk[2].k()], writes=[sq[2].k()])
            kcols = slice(tb * 512, tb * 512 + ntok)
            kr_key = krot.k(tb * 512, tb * 512 + ntok)
            if is_ctx:
                S.dve(lambda e, ntok=ntok, kcols=kcols: e.tensor_scalar(out=krot.ap[:, kcols], in0=pk[2].ap[0:64, 0:ntok],
                                                                        scalar1=gvec.ap[0:64, 4:5], scalar2=None, op0=ALU.mult),
                      reads=[pk[2].k(), gvec.k()], writes=[kr_key])
            else:
                S.dma(cs.ap, cos_d[:, tb * 512:(tb + 1) * 512], writes=[cs.k()])
                S.dma(sn.ap, sin_d[:, tb * 512:(tb + 1) * 512], writes=[sn.k()])
                S.dve(lambda e: e.scalar_tensor_tensor(out=rt[0].ap, in0=pk[2].ap[0:64, :], scalar=gvec.ap[0:64, 4:5], in1=cs.ap,
                                                       op0=ALU.mult, op1=ALU.mult),
                      reads=[pk[2].k(), gvec.k(), cs.k()], writes=[rt[0].k()])
                S.dve(lambda e: e.scalar_tensor_tensor(out=rt[1].ap, in0=pk[3].ap[0:64, :], scalar=gvec.ap[0:64, 5:6], in1=sn.ap,
                                                       op0=ALU.mult, op1=ALU.mult),
                      reads=[pk[3].k(), gvec.k(), sn.k()], writes=[rt[1].k()])
                S.dve(lambda e, kcols=kcols: e.tensor_tensor(out=krot.ap[:, kcols], in0=rt[0].ap, in1=rt[1].ap, op=ALU.add),
                      reads=[rt[0].k(), rt[1].k()], writes=[kr_key])
            for h in range(H if kq_sec >= 2 else 0):
                pb = PB[2 + (h % 2)] if False else PB[2 + (h % 4)]

                def f(e, h=h, pb=pb, ntok=ntok):
                    e.matmul(pb.ap[:, 0:ntok], lhsT=wukv.ap[:, 0, h * 128:(h + 1) * 128], rhs=kvnT.ap[:, 0, 0:ntok], start=True, stop=False)
                    return e.matmul(pb.ap[:, 0:ntok], lhsT=wukv.ap[:, 1, h * 128:(h + 1) * 128], rhs=kvnT.ap[:, 1, 0:ntok], start=False, stop=True)
                S.pe(f, reads=[wukv.k(), kvnT.k()], writes=[pb.k()])
                ks = KTs[h % 2]
                S.act(lambda e, pb=pb, ks=ks, ntok=ntok: e.activation(out=ks.ap[:, 0:ntok], in_=pb.ap[:, 0:ntok], func=AF.Copy),
                      reads=[pb.k()], writes=[ks.k()])
                S.dma(KT_d[h, :, tb * 512: tb * 512 + ntok], ks.ap[:, 0:ntok], reads=[ks.k()], writes=[("KT", h, tb)])
                sqb = sq[h % 2]
                S.act(lambda e, pb=pb, sqb=sqb, ntok=ntok: e.activation(out=sqb.ap[:, 0:ntok], in_=pb.ap[:, 0:ntok], func=AF.Square),
                      reads=[pb.k()], writes=[sqb.k()])

                def f(e, h=h, sqb=sqb, tb=tb, ntile=ntile):
                    i = None
                    for a in range(ntile):
                        col = a * H + h
                        e.matmul(rkP.ap[:, col:col + 1], lhsT=sqb.ap[:, a * 128:(a + 1) * 128], rhs=onesb.ap[:, 0:1], start=True, stop=False)
                        i = e.matmul(rkP.ap[:, col:col + 1], lhsT=sq[2].ap[0:64, a * 128:(a + 1) * 128], rhs=onesb.ap[0:64, 0:1],
                                     start=False, stop=True)
                    return i
                S.pe(f, reads=[sqb.k(), sq[2].k(), onesb.k()], writes=[rkP.k()])
            if kq_sec >= 2:
                S.dve(lambda e, tb=tb, ntile=ntile: e.tensor_copy(out=rk.ap[:, tb * 32: tb * 32 + ntile * H], in_=rkP.ap[:, 0:ntile * H]),
                      reads=[rkP.k()], writes=[rk.k(tb * 32, tb * 32 + ntile * H)])
            for a in range(ntile if kq_sec >= 3 else 0):
                kt = tb * 4 + a
                vb = Vs[a % 2]
                for hh in range(2):
                    pb = PB[2 + ((a * 2 + hh) % 4)]

                    def f(e, a=a, hh=hh, pb=pb):
                        e.matmul(pb.ap, lhsT=kvnT.ap[:, 0, a * 128:(a + 1) * 128], rhs=wukv.ap[:, 0, 1024 + hh * 512:1024 + (hh + 1) * 512],
                                 start=True, stop=False)
                        return e.matmul(pb.ap, lhsT=kvnT.ap[:, 1, a * 128:(a + 1) * 128], rhs=wukv.ap[:, 1, 1024 + hh * 512:1024 + (hh + 1) * 512],
                                        start=False, stop=True)
                    S.pe(f, reads=[kvnT.k(), wukv.k()], writes=[pb.k()])
                    if hh == 0:
                        S.dve(lambda e, pb=pb, vb=vb: e.tensor_copy(out=vb.ap[:, 0:512], in_=pb.ap), reads=[pb.k()], writes=[vb.k(0, 512)])
                    else:
                        S.act(lambda e, pb=pb, vb=vb: e.activation(out=vb.ap[:, 512:1024], in_=pb.ap, func=AF.Copy),
                              reads=[pb.k()], writes=[vb.k(512, 1024)])
                S.dma(V_d[kt], vb.ap, reads=[vb.k()], writes=[("V", kt)])

            if tb >= 4 or not kq_own:
                continue
            tsl = slice(tb * 512, (tb + 1) * 512)
            s0, wq0 = wload(w_in[:, 0:256], 16, 256)
            s1, wq1 = wload(w_in[:, 256:512], 16, 256)
            osub = int(os.environ.get("OWN_SUB", 9))
            for c in range(4):
                ws_, wv_ = (s0, wq0) if c < 2 else (s1, wq1)
                pb = PB[2 + (c % 2)]

                def f(e, c=c, wv_=wv_, pb=pb):
                    i = None
                    for k in range(16):
                        i = e.matmul(pb.ap, lhsT=wv_[:, k, (c % 2) * 128:(c % 2 + 1) * 128], rhs=hTb.ap[:, k, :], start=(k == 0), stop=(k == 15))
                    return i
                if osub >= 1:
                    S.pe(f, reads=[ws_.k(), hTb.k()], writes=[pb.k()])
                if osub >= 2:
                    S.dve(lambda e, c=c, pb=pb: e.tensor_scalar(out=qcg.ap[:, c, :], in0=pb.ap, scalar1=gqn.ap[:, c:c + 1], scalar2=None, op0=ALU.mult),
                          reads=[pb.k(), gqn.k()], writes=[qcg.k(c)])
                sqb = sq[c % 2]
                if osub >= 3:
                    S.act(lambda e, pb=pb, sqb=sqb: e.activation(out=sqb.ap, in_=pb.ap, func=AF.Square), reads=[pb.k()], writes=[sqb.k()])
                if osub >= 4:
                    S.pe(lambda e, c=c, sqb=sqb: e.matmul(PB[6].ap, lhsT=onesb.ap, rhs=sqb.ap, start=(c == 0), stop=(c == 3)),
                         reads=[onesb.k(), sqb.k()], writes=[PB[6].k()])
            if osub >= 5:
                S.dve(lambda e: e.tensor_scalar(out=epsq.ap, in0=PB[6].ap, scalar1=EPS / 512.0, scalar2=EPS * EPS, op0=ALU.mult, op1=ALU.add),
                      reads=[PB[6].k()], writes=[epsq.k()])
            own_sec = int(os.environ.get("OWN_SEC", 9))
            if own_sec < 2:
                continue
            sa, wuA = wload(w_uq[:, 0:1024], 4, 1024)
            sb_, wuB = wload(w_uq[:, 1024:2048], 4, 1024)
            S.dma(cs.ap, cos_d[:, tsl], writes=[cs.k()])
            S.dma(sn.ap, sin_d[:, tsl], writes=[sn.k()])
            for h in range(H):
                pn, pr, psw, pss = (PB[2], PB[3], PB[4], PB[5]) if h % 2 == 0 else (PB[0], PB[1], PB[6], PB[7])
                sq0, sq1 = sqq[(h % 2) * 2], sqq[(h % 2) * 2 + 1]
                rqb = rqbs[h % 2]
                rt0, rt1 = rt[(h % 2) * 2], rt[(h % 2) * 2 + 1]

                def f(e, h=h, wuA=wuA, wuB=wuB, pn=pn, pr=pr, psw=psw):
                    i = None
                    for c in range(4):
                        i = e.matmul(pn.ap, lhsT=wuA[:, c, h * 128:(h + 1) * 128], rhs=qcg.ap[:, c, :], start=(c == 0), stop=(c == 3))
                    for c in range(4):
                        i = e.matmul(pr.ap[0:64, :], lhsT=wuB[:, c, h * 64:(h + 1) * 64], rhs=qcg.ap[:, c, :], start=(c == 0), stop=(c == 3))
                    for c in range(4):
                        i = e.matmul(psw.ap[0:64, :], lhsT=wuB[:, c, 512 + h * 64:512 + (h + 1) * 64], rhs=qcg.ap[:, c, :], start=(c == 0), stop=(c == 3))
                    return i
                S.pe(f, reads=[sa.k(), sb_.k(), qcg.k()], writes=[pn.k(), pr.k(), psw.k()])
                S.act(lambda e, pn=pn, sq0=sq0: e.activation(out=sq0.ap, in_=pn.ap, func=AF.Square), reads=[pn.k()], writes=[sq0.k()])
                S.act(lambda e, pr=pr, sq1=sq1: e.activation(out=sq1.ap[0:64, :], in_=pr.ap[0:64, :], func=AF.Square), reads=[pr.k()], writes=[sq1.k()])

                def f(e, pss=pss, sq0=sq0, sq1=sq1):
                    e.matmul(pss.ap, lhsT=onesb.ap, rhs=sq0.ap, start=True, stop=False)
                    return e.matmul(pss.ap, lhsT=onesb.ap[0:64, :], rhs=sq1.ap[0:64, :], start=False, stop=True)
                S.pe(f, reads=[onesb.k(), sq0.k(), sq1.k()], writes=[pss.k()])
                S.dve(lambda e, pss=pss, rqb=rqb: e.scalar_tensor_tensor(out=rqb.ap, in0=pss.ap, scalar=1.0 / 192, in1=epsq.ap, op0=ALU.mult, op1=ALU.add),
                      reads=[pss.k(), epsq.k()], writes=[rqb.k()])
                S.act(lambda e, rqb=rqb: e.activation(out=rqb.ap, in_=rqb.ap, func=AF.Sqrt), reads=[rqb.k()], writes=[rqb.k()])
                S.dve(lambda e, rqb=rqb: e.reciprocal(out=rqb.ap, in_=rqb.ap), reads=[rqb.k()], writes=[rqb.k()])
                qn_, qr_ = QNs[h % 2], QRs[h % 2]
                S.dve(lambda e, qn_=qn_, pn=pn, rqb=rqb: e.scalar_tensor_tensor(out=qn_.ap, in0=pn.ap, scalar=small.ap[:, 0:1], in1=rqb.ap,
                                                                                op0=ALU.mult, op1=ALU.mult),
                      reads=[pn.k(), small.k(), rqb.k()], writes=[qn_.k()])
                S.dve(lambda e, pr=pr, rt0=rt0: e.scalar_tensor_tensor(out=rt0.ap, in0=pr.ap[0:64, :], scalar=gvec.ap[0:64, 2:3], in1=cs.ap,
                                                                       op0=ALU.mult, op1=ALU.mult),
                      reads=[pr.k(), gvec.k(), cs.k()], writes=[rt0.k()])
                S.dve(lambda e, psw=psw, rt1=rt1: e.scalar_tensor_tensor(out=rt1.ap, in0=psw.ap[0:64, :], scalar=gvec.ap[0:64, 3:4], in1=sn.ap,
                                                                         op0=ALU.mult, op1=ALU.mult),
                      reads=[psw.k(), gvec.k(), sn.k()], writes=[rt1.k()])
                S.dve(lambda e, rt0=rt0, rt1=rt1: e.tensor_tensor(out=rt0.ap, in0=rt0.ap, in1=rt1.ap, op=ALU.add),
                      reads=[rt0.k(), rt1.k()], writes=[rt0.k()])
                S.dve(lambda e, qr_=qr_, rt0=rt0, rqb=rqb: e.tensor_tensor(out=qr_.ap, in0=rt0.ap, in1=rqb.ap[0:64, :], op=ALU.mult),
                      reads=[rt0.k(), rqb.k()], writes=[qr_.k()])
                S.dma(QTn_d[h, :, tsl], qn_.ap, reads=[qn_.k()], writes=[("QTn", h, tb)])
                S.dma(QTr_d[h, :, tsl], qr_.ap, reads=[qr_.k()], writes=[("QTr", h, tb)])
            if own_sec < 3:
                continue
            for pnl in range(4):
                s, wu_ = wload(w_in[:, OFF_U + pnl * 256: OFF_U + (pnl + 1) * 256], 16, 256)
                for nt in range(2):
                    g = pnl * 2 + nt
                    pb = PB[2 + (g % 4)]

                    def f(e, wu_=wu_, nt=nt, pb=pb):
                        i = None
                        for k in range(16):
                            i = e.matmul(pb.ap, lhsT=wu_[:, k, nt * 128:(nt + 1) * 128], rhs=hTb.ap[:, k, :], start=(k == 0), stop=(k == 15))
                        return i
                    S.pe(f, reads=[s.k(), hTb.k()], writes=[pb.k()])
                    S.act(lambda e, g=g, pb=pb: e.activation(out=uT.ap[:, g, :], in_=pb.ap, func=AF.Gelu), reads=[pb.k()], writes=[uT.k(g)])
            for half in range(2 if own_sec >= 4 else 0):
                S.dve(lambda e: e.memset(stat.ap[:, 32:48], 0.0), writes=[stat.k(32, 48)])
                for pnl in range(4):
                    s, wv_ = wload(w_in[:, OFF_V + pnl * 256: OFF_V + (pnl + 1) * 256], 16, 256)
                    for t2 in range(2):
                        a = half * 2 + t2
                        pb = PB[2 + ((pnl * 2 + t2) % 4)]

                        def f(e, wv_=wv_, a=a, pb=pb):
                            i = None
                            for k in range(16):
                                i = e.matmul(pb.ap[:, 0:256], lhsT=hTb.ap[:, k, a * 128:(a + 1) * 128], rhs=wv_[:, k, :], start=(k == 0), stop=(k == 15))
                            return i
                        S.pe(f, reads=[s.k(), hTb.k()], writes=[pb.k()])
                        c0 = 32 + t2 * 8 + pnl
                        S.act(lambda e, pb=pb, t2=t2, pnl=pnl, c0=c0: e.activation(out=vg.ap[:, t2, pnl * 256:(pnl + 1) * 256], in_=pb.ap[:, 0:256],
                                                                                   func=AF.Gelu, accum_out=stat.ap[:, c0:c0 + 1]),
                              reads=[pb.k()], writes=[vg.k(t2), stat.k(c0, c0 + 1)])
                        S.act(lambda e, t2=t2, pnl=pnl, c0=c0: e.activation(out=vns[t2].ap[:, pnl * 256:(pnl + 1) * 256], in_=vg.ap[:, t2, pnl * 256:(pnl + 1) * 256],
                                                                            func=AF.Square, accum_out=stat.ap[:, c0 + 4:c0 + 5]),
                              reads=[vg.k(t2)], writes=[vns[t2].k(), stat.k(c0 + 4, c0 + 5)])
                for t2 in range(2):
                    a = half * 2 + t2
                    c0 = 32 + t2 * 8
                    vn, sgt = vns[t2], sgts[t2]
                    spm0, spm1 = (PB[0], PB[1]) if t2 == 0 else (PB[6], PB[7])
                    mu, e2, var = stat.ap[:, 48 + t2 * 4:49 + t2 * 4], stat.ap[:, 49 + t2 * 4:50 + t2 * 4], stat.ap[:, 50 + t2 * 4:51 + t2 * 4]
                    sk = stat.k(48 + t2 * 4, 52 + t2 * 4)
                    sk_in = stat.k(c0, c0 + 8)
                    S.dve(lambda e, c0=c0, mu=mu: e.tensor_reduce(out=mu, in_=stat.ap[:, c0:c0 + 4], axis=AX.X, op=ALU.add), reads=[sk_in], writes=[sk])
                    S.dve(lambda e, c0=c0, e2=e2: e.tensor_reduce(out=e2, in_=stat.ap[:, c0 + 4:c0 + 8], axis=AX.X, op=ALU.add), reads=[sk_in], writes=[sk])
                    S.dve(lambda e, mu=mu: e.tensor_scalar(out=mu, in0=mu, scalar1=1.0 / 1024, scalar2=None, op0=ALU.mult), reads=[sk], writes=[sk])
                    S.dve(lambda e, mu=mu, var=var: e.tensor_tensor(out=var, in0=mu, in1=mu, op=ALU.mult), reads=[sk], writes=[sk])
                    S.dve(lambda e, e2=e2, var=var: e.scalar_tensor_tensor(out=var, in0=e2, scalar=1.0 / 1024, in1=var, op0=ALU.mult, op1=ALU.subtract),
                          reads=[sk], writes=[sk])
                    S.dve(lambda e, var=var: e.tensor_scalar(out=var, in0=var, scalar1=EPS, scalar2=None, op0=ALU.add), reads=[sk], writes=[sk])
                    S.act(lambda e, var=var: e.activation(out=var, in_=var, func=AF.Sqrt), reads=[sk], writes=[sk])
                    S.dve(lambda e, var=var: e.reciprocal(out=var, in_=var), reads=[sk], writes=[sk])
                    S.dve(lambda e, t2=t2, mu=mu, var=var, vn=vn: e.tensor_scalar(out=vn.ap, in0=vg.ap[:, t2, :], scalar1=mu, scalar2=var,
                                                                                  op0=ALU.subtract, op1=ALU.mult),
                          reads=[vg.k(t2), sk], writes=[vn.k()])

                    def f(e, vn=vn, spm0=spm0, spm1=spm1):
                        i = None
                        for g in range(8):
                            pm = spm0 if g < 4 else spm1
                            i = e.matmul(pm.ap[:, (g % 4) * 128:(g % 4 + 1) * 128], lhsT=vn.ap[:, g * 128:(g + 1) * 128], rhs=wsTb.ap[:, g, :],
                                         start=True, stop=True)
                        return i
                    S.pe(f, reads=[vn.k(), wsTb.k()], writes=[spm0.k(), spm1.k()])
                    for g in range(8):
                        pm = spm0 if g < 4 else spm1
                        S.dve(lambda e, g=g, pm=pm, sgt=sgt: e.scalar_tensor_tensor(out=sgt.ap[:, g, :], in0=pm.ap[:, (g % 4) * 128:(g % 4 + 1) * 128],
                                                                                    scalar=sgun.ap[:, g:g + 1], in1=T2.ap[:, g, :], op0=ALU.mult, op1=ALU.add),
                              reads=[pm.k(), sgun.k(), T2.k(g)], writes=[sgt.k(g)])
                    so = sgs[a % 2]
                    S.dve(lambda e, so=so, a=a, sgt=sgt: e.tensor_tensor(out=so.ap, in0=sgt.ap, in1=uT.ap[:, :, a * 128:(a + 1) * 128], op=ALU.mult),
                          reads=[sgt.k(), uT.k()], writes=[so.k()])
                    S.dma(sgT_d[:, :, tb * 512 + a * 128: tb * 512 + (a + 1) * 128].rearrange("g p t -> p g t"), so.ap,
                          reads=[so.k()], writes=[("sgT", tb, a)])

        S.dve(lambda e: e.scalar_tensor_tensor(out=A2.ap, in0=modFM.ap[:, 64:80, 0], scalar=1.0, in1=n2g.ap,
                                               op0=ALU.add, op1=ALU.mult),
              reads=[("modB", c) for c in range(32, 96)] + [n2g.k()], writes=[A2.k()])
        pm1 = PA.view(2048, [128, 128], F32)
        S.pe(lambda e: e.transpose(out=pm1.ap[0:96, :], in_=modFM.ap[:, :, 0], identity=identf.ap),
             reads=[("modB", c) for c in range(32, 96)] + [modFM.k(0, 32), identf.k()], writes=[pm1.k()])
        S.dve(lambda e: e.tensor_copy(out=mrow_sb.ap[0:96, :], in_=pm1.ap[0:96, :]), reads=[pm1.k()], writes=[mrow_sb.k()])
        S.dma(mrow_d, mrow_sb.ap[0:96, :], reads=[mrow_sb.k()], writes=["mrow"])

        S.dve(lambda e: e.tensor_scalar(out=rk.ap, in0=rk.ap, scalar1=1.0, scalar2=192.0 * EPS, op0=ALU.mult, op1=ALU.add),
              reads=[rk.k()], writes=[rk.k()])
        S.act(lambda e: e.activation(out=rk.ap, in_=rk.ap, func=AF.Sqrt), reads=[rk.k()], writes=[rk.k()])
        S.dve(lambda e: e.reciprocal(out=rk.ap, in_=rk.ap), reads=[rk.k()], writes=[rk.k()])
        if debug:
            S.dma(rk_dbg, rk.ap, reads=[rk.k()], writes=["rkdbg"])

        if limit == 1:
            S.finalize(final_reads=list(S.dram_keys))
            S.emit()
            return nc
        A.ptr = phase_base
        KTh = [A.alloc([128, NKEY], BF16) for _ in range(2)]
        Vh = [A.alloc([128, NKT, 128], BF16) for _ in range(2)]
        Qn = [A.alloc([128, NOWN], BF16) for _ in range(2)]
        Qr = [A.alloc([64, NOWN], BF16) for _ in range(2)]
        PT = [A.alloc([128, 1024], BF16) for _ in range(4)]
        rden = A.alloc([128, 1024], F32)
        lnd = A.alloc([128, 1024], F32)
        daccs = [[A.alloc([128, 1024], F32) for _ in range(4)] for _ in range(2)]
        zer = A.alloc([128, 1024], F32)
        posb = [A.alloc([128, 1024], F32) for _ in range(2)]
        ats = [A.alloc([128, NOWN], BF16) for _ in range(2)]
        ST = [PA.view(i * 4096, [128, 1024], F32) for i in range(3)]
        po = PA.view(3 * 4096, [128, 1024], F32)
        S.dve(lambda e: e.memset(zer.ap, 0.0), writes=[zer.k()])
        qctr = 0
        pending = []

        def epilogue(dacc, psb, ab, q0, hh, last_qb, pd):
            d0, d1, d2, d3 = dacc
            S.add("pool", lambda e: e.tensor_tensor(out=d0.ap, in0=d0.ap, in1=d1.ap, op=ALU.add), reads=[d0.k(), d1.k()], writes=[d0.k()])
            S.add("pool", lambda e: e.tensor_tensor(out=d2.ap, in0=d2.ap, in1=d3.ap, op=ALU.add), reads=[d2.k(), d3.k()], writes=[d2.k()])
            S.add("pool", lambda e: e.tensor_tensor(out=d0.ap, in0=d0.ap, in1=d2.ap, op=ALU.add), reads=[d0.k(), d2.k()], writes=[d0.k()])

            def f(e):
                e.matmul(pd.ap[:, 0:512], lhsT=onesf.ap, rhs=d0.ap[:, 0:512], start=True, stop=True)
                return e.matmul(pd.ap[:, 512:1024], lhsT=onesf.ap, rhs=d0.ap[:, 512:1024], start=True, stop=True)
            S.pe(f, reads=[onesf.k(), d0.k()], writes=[pd.k()])
            S.act(lambda e: e.activation(out=lnd.ap, in_=pd.ap, func=AF.Ln), reads=[pd.k()], writes=[lnd.k()])
            S.act(lambda e: e.activation(out=rden.ap, in_=lnd.ap, func=AF.Exp, scale=-1.0), reads=[lnd.k()], writes=[rden.k()])
            S.add("pool", lambda e: e.tensor_tensor(out=ab.ap[:, q0:q0 + 1024], in0=psb.ap, in1=rden.ap, op=ALU.mult),
                  reads=[psb.k(), rden.k()], writes=[ab.k(q0, q0 + 1024)])
            if last_qb:
                S.dma(atT_d[hh], ab.ap, reads=[ab.k()], writes=[("atT", hh)])

        for h in range(H):
            kb, vb, qn_, qr_ = KTh[h % 2], Vh[h % 2], Qn[h % 2], Qr[h % 2]
            S.dma(kb.ap, KT_d[h], reads=[("KT", h, tb) for tb in range(9)], writes=[kb.k()])
            S.dma(vb.ap, V_d[:, :, h * 128:(h + 1) * 128].rearrange("k p d -> p k d"), reads=[("V", kt) for kt in range(NKT)], writes=[vb.k()])
            S.dma(qn_.ap, QTn_d[h], reads=[("QTn", h, tb) for tb in range(4)], writes=[qn_.k()])
            S.dma(qr_.ap, QTr_d[h], reads=[("QTr", h, tb) for tb in range(4)], writes=[qr_.k()])
            ab = ats[h % 2]
            for qb in range(2):
                q0 = qb * 1024
                dacc = daccs[qctr % 2]
                psb = posb[qctr % 2]
                qctr += 1

                def s_op(kt, kb=kb, qn_=qn_, qr_=qr_, q0=q0):
                    pb = ST[kt % 3]

                    def f(e):
                        i = None
                        for hf in range(2):
                            e.matmul(pb.ap[:, hf * 512:(hf + 1) * 512], lhsT=kb.ap[:, kt * 128:(kt + 1) * 128],
                                     rhs=qn_.ap[:, q0 + hf * 512: q0 + (hf + 1) * 512], start=True, stop=False)
                        for hf in range(2):
                            i = e.matmul(pb.ap[:, hf * 512:(hf + 1) * 512], lhsT=krot.ap[:, kt * 128:(kt + 1) * 128],
                                         rhs=qr_.ap[:, q0 + hf * 512: q0 + (hf + 1) * 512], start=False, stop=True)
                        return i
                    S.pe(f, reads=[kb.k(), qn_.k(), qr_.k(), krot.k()], writes=[pb.k()])
                s_op(0)
                s_op(1)
                for kt in range(NKT):
                    if kt == 14 and pending:
                        epilogue(*pending.pop(0), pd=ST[(kt + 2) % 3])
                    if kt + 2 < NKT and not (kt == 14 and False):
                        s_op(kt + 2)
                    pb = ST[kt % 3]
                    pt = PT[kt % 4]
                    S.act(lambda e, pb=pb, pt=pt, kt=kt, h=h: e.activation(out=pt.ap, in_=pb.ap, func=AF.Exp, scale=rk.ap[:, kt * H + h:kt * H + h + 1],
                                                                           bias=small.ap[:, 1:2]),
                          reads=[pb.k(), rk.k(), small.k()], writes=[pt.k()])

                    def f(e, pt=pt, kt=kt, vb=vb):
                        e.matmul(po.ap[:, 0:512], lhsT=vb.ap[:, kt, :], rhs=pt.ap[:, 0:512], start=(kt == 0), stop=(kt == NKT - 1))
                        return e.matmul(po.ap[:, 512:1024], lhsT=vb.ap[:, kt, :], rhs=pt.ap[:, 512:1024], start=(kt == 0), stop=(kt == NKT - 1))
                    S.pe(f, reads=[pt.k(), vb.k()], writes=[po.k()])
                    da = dacc[kt % 4]
                    src0 = zer if kt < 4 else da
                    S.dve(lambda e, pt=pt, da=da, src0=src0: e.tensor_tensor(out=da.ap, in0=src0.ap, in1=pt.ap, op=ALU.add),
                          reads=[pt.k(), src0.k()], writes=[da.k()])
                S.act(lambda e, psb=psb: e.activation(out=psb.ap, in_=po.ap, func=AF.Copy), reads=[po.k()], writes=[psb.k()])
                pending.append((dacc, psb, ab, q0, h, qb == 1))
                if h == H - 1 and qb == 1:
                    while pending:
                        epilogue(*pending.pop(0), pd=ST[0])

        if limit == 2:
            S.finalize(final_reads=["rkdbg"] + [("atT", h) for h in range(H)])
            S.emit()
            return nc
        A.ptr = phase_base
        alloc_slots(7)
        g1bc = A.alloc([128, D], F32)
        g2bc = A.alloc([128, D], F32)
        hTb2s = [A.alloc([128, 16, 512], BF16) for _ in range(1)]
        actT = A.alloc([128, 44, 512], BF16)
        atbs = [A.alloc([128, 8, 512], BF16)]
        sgbs = [A.alloc([128, 8, 512], BF16)]
        mgT = A.view(actT.off + 16384, [128, 16, 512], BF16)
        xnew = A.alloc([128, 4, D], F32)
        xs2 = A.view(actT.off, [128, 4, D], BF16)
        sg = [A.alloc([128, 512], BF16) for _ in range(2)]
        tt = [A.alloc([128, 512], F32) for _ in range(2)]
        stat2 = A.alloc([128, 64], F32)

        S.dma(g1bc.ap, mrow_d[32:48, :].rearrange("a b -> (a b)").partition_broadcast(128), reads=["mrow"], writes=[g1bc.k()])
        S.dma(g2bc.ap, mrow_d[80:96, :].rearrange("a b -> (a b)").partition_broadcast(128), reads=["mrow"], writes=[g2bc.k()])
        out_keys = []
        for tb in range(4):
            tsl = slice(tb * 512, (tb + 1) * 512)
            hTb2, atb, sgb = hTb2s[0], atbs[0], sgbs[0]
            S.dma(hTb2.ap, hT_d[:, :, tsl].rearrange("j p t -> p j t"), reads=[("hT", tb)], writes=[hTb2.k()])
            S.dma(atb.ap, atT_d[:, :, tsl].rearrange("h p t -> p h t"), reads=[("atT", h) for h in range(H)], writes=[atb.k()])
            S.dma(sgb.ap, sgT_d[:, :, tsl].rearrange("g p t -> p g t"), reads=[("sgT", tb, a) for a in range(4)], writes=[sgb.k()])
            for a in range(4):
                S.dma(xnew.ap[:, a, :], xk[tb * 512 + a * 128: tb * 512 + (a + 1) * 128, :], writes=[xnew.k(a)])
            for n2 in range(8):
                c0 = n2 * 256
                s_ga, w_ga = wload(w_in[:, OFF_GATE + c0: OFF_GATE + c0 + 256], 16, 256)
                s_gs, w_gs = wload(w_in[:, OFF_GATE + D + c0: OFF_GATE + D + c0 + 256], 16, 256)
                s_ba = next_slot()
                s_bs = s_ba
                w_ba = s_ba.ap[:, 0:2048].rearrange("p (k n) -> p k n", k=8)
                w_bs = s_ba.ap[:, 2048:4096].rearrange("p (k n) -> p k n", k=8)
                S.dma(w_ba, w_bra[:, c0:c0 + 256].rearrange("(k p) n -> p k n", p=128), writes=[s_ba.k(0, 2048)], q="pool")
                S.dma(w_bs, w_brs[:, c0:c0 + 256].rearrange("(k p) n -> p k n", p=128), writes=[s_ba.k(2048, 4096)], q="pool")
                for nt in range(2):
                    n = n2 * 2 + nt
                    base = 4 * (n % 2)
                    p0, p1, p2, p3 = PB[base], PB[base + 1], PB[base + 2], PB[base + 3]
                    cs_ = slice(nt * 128, (nt + 1) * 128)

                    def f(e, w_ga=w_ga, w_gs=w_gs, w_ba=w_ba, w_bs=w_bs, cs_=cs_, p0=p0, p1=p1, p2=p2, p3=p3, hTb2=hTb2, atb=atb, sgb=sgb):
                        i = None
                        for k in range(16):
                            i = e.matmul(p0.ap, lhsT=w_ga[:, k, cs_], rhs=hTb2.ap[:, k, :], start=(k == 0), stop=(k == 15))
                        for k in range(16):
                            i = e.matmul(p1.ap, lhsT=w_gs[:, k, cs_], rhs=hTb2.ap[:, k, :], start=(k == 0), stop=(k == 15))
                        for k in range(8):
                            i = e.matmul(p2.ap, lhsT=w_ba[:, k, cs_], rhs=atb.ap[:, k, :], start=(k == 0), stop=(k == 7))
                        for k in range(8):
                            i = e.matmul(p3.ap, lhsT=w_bs[:, k, cs_], rhs=sgb.ap[:, k, :], start=(k == 0), stop=(k == 7))
                        return i
                    S.pe(f, reads=[s_ga.k(), s_gs.k(), s_ba.k(), s_bs.k(), hTb2.k(), atb.k(), sgb.k()],
                         writes=[p0.k(), p1.k(), p2.k(), p3.k()])
                    S.act(lambda e, p0=p0: e.activation(out=sg[0].ap, in_=p0.ap, func=AF.Sigmoid), reads=[p0.k()], writes=[sg[0].k()])
                    S.act(lambda e, p1=p1: e.activation(out=sg[1].ap, in_=p1.ap, func=AF.Sigmoid), reads=[p1.k()], writes=[sg[1].k()])
                    S.dve(lambda e, p2=p2: e.tensor_tensor(out=tt[0].ap, in0=p2.ap, in1=sg[0].ap, op=ALU.mult),
                          reads=[p2.k(), sg[0].k()], writes=[tt[0].k()])
                    S.dve(lambda e, p3=p3: e.tensor_tensor(out=tt[1].ap, in0=p3.ap, in1=sg[1].ap, op=ALU.mult),
                          reads=[p3.k(), sg[1].k()], writes=[tt[1].k()])
                    S.dve(lambda e, n=n: e.tensor_tensor(out=mgT.ap[:, n, :], in0=tt[0].ap, in1=tt[1].ap, op=ALU.add),
                          reads=[tt[0].k(), tt[1].k()], writes=[mgT.k(n)])
            for cb in range(4):
                s_a, w_a = wload(w_out[0:1024, cb * 512:(cb + 1) * 512], 8, 512)
                s_b, w_b = wload(w_out[1024:2048, cb * 512:(cb + 1) * 512], 8, 512)
                for a in range(4):
                    pb = PB[(cb * 4 + a) % 8]

                    def f(e, a=a, w_a=w_a, w_b=w_b, pb=pb):
                        i = None
                        for n in range(16):
                            wv_ = w_a if n < 8 else w_b
                            i = e.matmul(pb.ap, lhsT=mgT.ap[:, n, a * 128:(a + 1) * 128], rhs=wv_[:, n % 8, :], start=(n == 0), stop=(n == 15))
                        return i
                    S.pe(f, reads=[s_a.k(), s_b.k(), mgT.k()], writes=[pb.k()])
                    t_ = tt[a % 2]
                    S.dve(lambda e, pb=pb, t_=t_, cb=cb: e.tensor_tensor(out=t_.ap, in0=pb.ap, in1=g1bc.ap[:, cb * 512:(cb + 1) * 512], op=ALU.mult),
                          reads=[pb.k(), g1bc.k()], writes=[t_.k()])
                    xv = xnew.ap[:, a, cb * 512:(cb + 1) * 512]
                    xkk = ("R", "S", xnew.off + a * xnew.part + cb * 2048, xnew.off + a * xnew.part + (cb + 1) * 2048)
                    S.dve(lambda e, xv=xv, t_=t_: e.tensor_tensor(out=xv, in0=xv, in1=t_.ap, op=ALU.add),
                          reads=[t_.k(), xkk], writes=[xkk])
            norm_block(xs2, stat2, lambda a: (xnew.ap[:, a, :], xnew.k(a)), 4, hTb2, lambda j: A2.ap[:, j:j + 1],
                       lambda j: modFM.ap[:, 48 + j, 0:1], [A2.k()] + [("modB", c) for c in range(48, 64)])
            for fp in range(22):
                s_a, w_a = wload(w_f1[:, fp * 256:(fp + 1) * 256], 16, 256)
                s_b, w_b = wload(w_f1[:, DFF + fp * 256: DFF + (fp + 1) * 256], 16, 256)
                for ft in range(2):
                    f_ = fp * 2 + ft
                    base = 2 * (f_ % 4)
                    pa, pb = PB[base], PB[base + 1]
                    cs_ = slice(ft * 128, (ft + 1) * 128)

                    def f(e, w_a=w_a, w_b=w_b, cs_=cs_, pa=pa, pb=pb, hTb2=hTb2):
                        i = None
                        for k in range(16):
                            i = e.matmul(pa.ap, lhsT=w_a[:, k, cs_], rhs=hTb2.ap[:, k, :], start=(k == 0), stop=(k == 15))
                        for k in range(16):
                            i = e.matmul(pb.ap, lhsT=w_b[:, k, cs_], rhs=hTb2.ap[:, k, :], start=(k == 0), stop=(k == 15))
                        return i
                    S.pe(f, reads=[s_a.k(), s_b.k(), hTb2.k()], writes=[pa.k(), pb.k()])
                    sl = sg[f_ % 2]
                    S.act(lambda e, pa=pa, sl=sl: e.activation(out=sl.ap, in_=pa.ap, func=AF.Silu), reads=[pa.k()], writes=[sl.k()])
                    S.dve(lambda e, pb=pb, sl=sl, f_=f_: e.tensor_tensor(out=actT.ap[:, f_, :], in0=pb.ap, in1=sl.ap, op=ALU.mult),
                          reads=[pb.k(), sl.k()], writes=[actT.k(f_)])
            for ch in range(2):
                for fg in range(11):
                    s_w, w_w = wload(w_f2[fg * 512:(fg + 1) * 512, ch * 1024:(ch + 1) * 1024], 4, 1024)

                    def f(e, fg=fg, w_w=w_w):
                        i = None
                        for fi in range(4):
                            f_ = fg * 4 + fi
                            for a in range(4):
                                for c2 in range(2):
                                    i = e.matmul(PB[a * 2 + c2].ap, lhsT=actT.ap[:, f_, a * 128:(a + 1) * 128], rhs=w_w[:, fi, c2 * 512:(c2 + 1) * 512],
                                                 start=(f_ == 0), stop=(f_ == 43))
                        return i
                    S.pe(f, reads=[s_w.k(), actT.k(fg * 4, fg * 4 + 4)], writes=[PB[b_].k() for b_ in range(8)])
                for a in range(4):
                    for c2 in range(2):
                        cb = ch * 2 + c2
                        pb = PB[a * 2 + c2]
                        t_ = tt[(a * 2 + c2) % 2]
                        S.dve(lambda e, pb=pb, t_=t_, cb=cb: e.tensor_tensor(out=t_.ap, in0=pb.ap, in1=g2bc.ap[:, cb * 512:(cb + 1) * 512], op=ALU.mult),
                              reads=[pb.k(), g2bc.k()], writes=[t_.k()])
                        xv = xnew.ap[:, a, cb * 512:(cb + 1) * 512]
                        xkk = ("R", "S", xnew.off + a * xnew.part + cb * 2048, xnew.off + a * xnew.part + (cb + 1) * 2048)
                        S.dve(lambda e, xv=xv, t_=t_: e.tensor_tensor(out=xv, in0=xv, in1=t_.ap, op=ALU.add),
                              reads=[t_.k(), xkk], writes=[xkk])
            for a in range(4):
                S.dma(y[tb * 512 + a * 128: tb * 512 + (a + 1) * 128, :], xnew.ap[:, a, :], reads=[xnew.k(a)], writes=[("y", tb, a)])
                out_keys.append(("y", tb, a))

        fin = list(out_keys)
        if debug:
            fin += ["rkdbg"]
        S.finalize(final_reads=fin)
        S.emit()
    return nc


def _rope_tables():
    nf = 16
    freqs = (np.float32(10000.0) ** (-np.arange(nf, dtype=np.float32) / np.float32(nf))).astype(np.float32)
    pos = np.arange(SEQ)
    row = (pos // 64).astype(np.float32)
    col = (pos % 64).astype(np.float32)
    ang_r = (row[:, None] * freqs[None, :]).astype(np.float32)
    ang_c = (col[:, None] * freqs[None, :]).astype(np.float32)
    cosT = np.zeros((64, SEQ), np.float32)
    sinS = np.zeros((64, SEQ), np.float32)
    for r in range(64):
        ang = ang_r if r < 32 else ang_c
        f = r % 16
        first = (r % 32) < 16
        cosT[r] = np.cos(ang[:, f])
        sinS[r] = (-np.sin(ang[:, f])) if first else np.sin(ang[:, f])
    return cosT, sinS


_SIGMA = np.array([(r + 16) if (r % 32) < 16 else (r - 16) for r in range(64)])


def _fm(v, nchunk):
    return np.ascontiguousarray(np.asarray(v, np.float32).reshape(nchunk, 128).T)


def make_in_maps(inputs):
    f = lambda k: np.asarray(inputs[k], np.float32)
    x, c, ctx, c_ctx = f("x"), f("c"), f("ctx"), f("c_ctx")
    w_in = np.ascontiguousarray(f("w_in")[0])
    w_uq = f("w_uq")[0].reshape(512, H, 192)
    w_ukv = f("w_ukv")[0].reshape(256, H, 256)
    gq, gk = f("qk_norm_q")[0], f("qk_norm_k")[0]
    cosT, sinS = _rope_tables()
    kr = w_in[:, 768:832]
    w_kv = np.ascontiguousarray(np.concatenate([w_in[:, 512:768], kr, kr[:, _SIGMA]], axis=1))
    w_uq_l = np.ascontiguousarray(np.concatenate([
        w_uq[:, :, 0:128].reshape(512, 1024),
        w_uq[:, :, 128:192].reshape(512, 512),
        w_uq[:, :, 128:192][:, :, _SIGMA].reshape(512, 512)], axis=1))
    w_ukv_l = np.ascontiguousarray(np.concatenate([w_ukv[:, :, 0:128].reshape(256, 1024), w_ukv[:, :, 128:256].reshape(256, 1024)], axis=1))
    gvec = np.zeros((128, 8), np.float32)
    gvec[:, 0] = gq[0:128]
    gvec[:, 1] = gk[0:128]
    gvec[:64, 2] = gq[128:192]
    gvec[:64, 3] = gq[128:192][_SIGMA]
    gvec[:64, 4] = gk[128:192]
    gvec[:64, 5] = gk[128:192][_SIGMA]
    gqk = np.ascontiguousarray(np.broadcast_to(np.concatenate([gq, gk])[None, :], (128, 384)))
    sgun = np.ascontiguousarray(np.concatenate([_fm(f("sgu_norm_g")[0], 8), _fm(f("sgu_norm_b")[0], 8)], axis=1))
    wsT = np.ascontiguousarray(f("w_spatial")[0].transpose(2, 0, 1))
    bsbc = np.ascontiguousarray(np.broadcast_to(f("b_spatial")[0].reshape(1, 1024), (128, 1024)))
    shared = {
        "bmT": _fm(f("b_mod")[0], 96), "n1g": _fm(f("norm1_g")[0], 16), "n2g": _fm(f("norm2_g")[0], 16),
        "gqn": _fm(f("q_norm_g")[0], 4), "gkv": _fm(f("kv_norm_g")[0], 2), "gvec": gvec, "gqk": gqk, "sgun": sgun,
        "w_mod": np.ascontiguousarray(f("w_mod")[0]), "w_in": w_in, "w_kv": w_kv, "w_uq": w_uq_l, "w_ukv": w_ukv_l,
        "wsT": wsT, "bsbc": bsbc, "w_bra": np.ascontiguousarray(f("w_br_attn")[0]), "w_brs": np.ascontiguousarray(f("w_br_sgu")[0]),
        "w_out": np.ascontiguousarray(f("w_out")[0]), "w_f1": np.ascontiguousarray(f("w_ffn_in")[0]),
        "w_f2": np.ascontiguousarray(f("w_ffn_out")[0]), "ident": np.eye(128, dtype=np.float32),
    }
    maps = []
    for i in range(8):
        b, hf = i // 2, i % 2
        own = slice(hf * NOWN, (hf + 1) * NOWN)
        oth = slice((1 - hf) * NOWN, (2 - hf) * NOWN)
        m = dict(shared)
        m["xk"] = np.ascontiguousarray(np.concatenate([x[b, own], x[b, oth]], axis=0))
        m["ctxb"] = np.ascontiguousarray(ctx[b])
        cT = np.stack([_fm(c[b], 16), _fm(c_ctx, 16)], axis=-1)
        m["cT"] = np.ascontiguousarray(cT)
        m["cosT"] = np.ascontiguousarray(np.concatenate([cosT[:, own], cosT[:, oth]], axis=1))
        m["sinS"] = np.ascontiguousarray(np.concatenate([sinS[:, own], sinS[:, oth]], axis=1))
        maps.append(m)
    return maps


_NC_CACHE = {}


def kernel(**inputs):
    maps = make_in_maps(inputs)
    if "nc" not in _NC_CACHE:
        _NC_CACHE["nc"] = build(False)
    res = run_bass_kernel_spmd(_NC_CACHE["nc"], maps, core_ids=list(range(8)))
    out = np.empty((4, SEQ, D), np.float32)
    for i in range(8):
        b, hf = i // 2, i % 2
        out[b, hf * NOWN:(hf + 1) * NOWN] = res.results[i]["y"]
    return out
```

```python
import os
import numpy as np
from contextlib import ExitStack
import concourse.bass as bass
import concourse.mybir as mybir
from concourse.bass_utils import run_bass_kernel_spmd

F32 = mybir.dt.float32
BF16 = mybir.dt.bfloat16
AF = mybir.ActivationFunctionType
ALU = mybir.AluOpType
AX = mybir.AxisListType

D = 2048
SEQ = 4096
NOWN = 2048
CTX = 256
NKEY = SEQ + CTX
NKT = NKEY // 128
H = 8
OFF_U = 832
OFF_V = 1856
OFF_GATE = 2880
IN_COLS = 6976
DFF = 5632
EPS = 1e-6
GRAN = 256
ARENA_BYTES = 206 * 1024


class Op:
    __slots__ = ("eng", "fn", "deps", "dma", "waits", "sig", "semval", "sem", "name", "banks", "cost", "odeps")

    def __init__(self, eng, fn, deps, dma, name):
        self.eng, self.fn, self.deps, self.dma, self.name = eng, fn, deps, dma, name
        self.waits = []
        self.sig = False
        self.semval = None
        self.sem = None
        self.banks = ()
        self.cost = 0.5
        self.odeps = set()


class _FakeInst:
    def then_inc(self, *a, **k):
        return self


class _FakeEng:
    def __init__(self):
        self.cost = 0.0
        self.bytes = 0

    @staticmethod
    def _free(ap):
        n = 1
        for d in ap.shape[1:]:
            n *= d
        return n

    def __getattr__(self, name):
        def f(*a, **k):
            if name == "matmul":
                n = self._free(k["rhs"])
                self.cost += max(n, 64) / 2000.0 + 0.01
            elif name == "transpose":
                self.cost += 0.12
            elif name == "dma_start":
                o = k["out"]
                self.bytes += self._free(o) * o.shape[0] * (4 if o.dtype == F32 else 2)
            elif name == "memset":
                self.cost += 0.1
            else:
                o = k.get("out", a[0] if a else None)
                n = self._free(o) if o is not None else 512
                self.cost += 0.2 + n * (0.0065 if name == "reciprocal" else 0.00105)
            return _FakeInst()
        return f


class Sched:
    def __init__(self, nc, n_dma_sems=12):
        self.nc = nc
        self.ops = []
        self.last_w = {}
        self.readers = {}
        self.n_dma_sems = n_dma_sems
        self.dram_keys = []
        self.bank_acc = {}

    @staticmethod
    def _expand(keys):
        out = []
        for k in keys:
            if isinstance(k, tuple) and len(k) == 4 and k[0] == "R":
                for g in range(k[2] // GRAN, (k[3] + GRAN - 1) // GRAN):
                    out.append((k[1], g))
            else:
                out.append(k)
        return out

    def add(self, eng, fn, reads=(), writes=(), dma=False, name=""):
        i = len(self.ops)
        for k in writes:
            if not (isinstance(k, tuple) and len(k) == 4 and k[0] == "R"):
                self.dram_keys.append(k)
        reads = self._expand(reads)
        writes = self._expand(writes)
        deps = set()
        lw, rd = self.last_w, self.readers
        for k in reads:
            j = lw.get(k)
            if j is not None:
                deps.add(j)
        for k in writes:
            j = lw.get(k)
            if j is not None:
                deps.add(j)
            r = rd.get(k)
            if r:
                deps.update(r)
        banks = set()
        for k in reads + writes:
            if isinstance(k, tuple) and len(k) == 2 and k[0] == "P":
                banks.add(k[1] // 8)
        for b in banks:
            d = self.bank_acc.setdefault(b, {})
            for e2, idx in d.items():
                if e2 != eng:
                    deps.add(idx)
            d[eng] = i
        deps.discard(i)
        for k in reads:
            rd.setdefault(k, []).append(i)
        for k in writes:
            lw[k] = i
            rd[k] = []
        op = Op(eng, fn, deps, dma, name)
        op.banks = tuple(banks)
        self.ops.append(op)
        return i

    def pe(self, fn, reads=(), writes=(), name=""):
        return self.add("pe", fn, reads, writes, name=name)

    def act(self, fn, reads=(), writes=(), name=""):
        return self.add("act", fn, reads, writes, name=name)

    def dve(self, fn, reads=(), writes=(), name=""):
        return self.add("dve", fn, reads, writes, name=name)

    def dma(self, out, in_, reads=(), writes=(), q="sp", name="", **kw):
        return self.add(q, lambda e: e.dma_start(out=out, in_=in_, **kw), reads, writes,
                        dma=True, name=name)

    def reorder(self, window=128):
        ops = self.ops
        n = len(ops)
        for op in ops:
            if op.fn is None:
                op.cost = 0.0
                continue
            fe = _FakeEng()
            op.fn(fe)
            if op.dma:
                op.cost = (1.0 if op.eng == "pool" else 0.06, 2.0 + fe.bytes / 250e3)
            else:
                op.cost = fe.cost * (2.5 if op.eng == "pool" else 1.0)
        lastb = {}
        for i, op in enumerate(ops):
            op.odeps = set()
            for b in op.banks:
                j = lastb.get((op.eng, b))
                if j is not None:
                    op.odeps.add(j)
                lastb[(op.eng, b)] = i
        engs = ("pe", "act", "dve", "pool", "sp")
        queues = {e: [i for i, op in enumerate(ops) if op.eng == e] for e in engs}
        qpos = {e: 0 for e in engs}
        succ = [[] for _ in range(n)]
        npred = [0] * n
        for i, op in enumerate(ops):
            ps = op.deps | op.odeps
            npred[i] = len(ps)
            for j in ps:
                succ[j].append(i)
        ready_t = [0.0] * n
        done = [False] * n
        free_t = {e: 0.0 for e in engs}
        last_i = n - 1
        order = []
        remaining = n
        while remaining:
            best = None
            for e in engs:
                q = queues[e]
                p = qpos[e]
                while p < len(q) and done[q[p]]:
                    p += 1
                qpos[e] = p
                cnt = 0
                k = p
                while k < len(q) and cnt < window:
                    i = q[k]
                    k += 1
                    if done[i]:
                        continue
                    cnt += 1
                    if npred[i] or (i == last_i and remaining > 1):
                        continue
                    st = max(free_t[e], ready_t[i])
                    if best is None or st < best[0] - 1e-9 or (abs(st - best[0]) <= 1e-9 and i < best[1]):
                        best = (st, i, e)
                    if ready_t[i] <= free_t[e]:
                        break
            assert best is not None, "scheduler deadlock"
            st, i, e = best
            op = ops[i]
            if op.dma:
                free_t[e] = st + op.cost[0]
                fin = st + op.cost[0] + op.cost[1]
            else:
                free_t[e] = st + op.cost
                fin = free_t[e] + 0.15
            done[i] = True
            remaining -= 1
            order.append(i)
            for j in succ[i]:
                npred[j] -= 1
                if ready_t[j] < fin:
                    ready_t[j] = fin
        newidx = {old: new for new, old in enumerate(order)}
        new_ops = []
        for old in order:
            op = ops[old]
            op.deps = set(newidx[j] for j in op.deps)
            new_ops.append(op)
        self.ops = new_ops
        self.est_time = max(free_t.values())

    def finalize(self, final_reads=(), reorder=True):
        self.add("sp", None, reads=final_reads, name="final")
        if reorder:
            self.reorder()
        ops = self.ops
        engs = ("pe", "act", "dve", "pool", "sp")
        cur = {e: {} for e in engs}
        dma_known = {e: set() for e in engs}
        clock = [None] * len(ops)
        dma_rr = {e: 0 for e in engs}
        dma_last = {}
        dma_uses = {}
        for i, op in enumerate(ops):
            E = op.eng
            c = cur[E]
            deps = set(op.deps)
            if op.dma:
                slot = dma_rr[E] % self.n_dma_sems
                dma_rr[E] += 1
                prev = dma_last.get((E, slot))
                if prev is not None:
                    deps.add(prev)
                dma_last[(E, slot)] = i
                dma_uses[(E, slot)] = dma_uses.get((E, slot), 0) + 1
                op.sem = (E, slot)
                op.semval = 16 * dma_uses[(E, slot)]
            waits = []
            for j in sorted(deps):
                oj = ops[j]
                if oj.dma:
                    if j in dma_known[E]:
                        continue
                    waits.append(j)
                    dma_known[E].add(j)
                else:
                    if c.get(oj.eng, -1) >= j:
                        continue
                    if oj.eng == E and E == "pe":
                        continue
                    waits.append(j)
                    oj.sig = True
            for j in waits:
                for k, v in clock[j].items():
                    if c.get(k, -1) < v:
                        c[k] = v
            op.waits = waits
            ck = dict(c)
            if not op.dma:
                ck[E] = i
                c[E] = max(c.get(E, -1), -1)
            clock[i] = ck
        cnt = {e: 0 for e in engs}
        for op in ops:
            if op.sig and not op.dma:
                cnt[op.eng] += 1
                op.semval = cnt[op.eng]
        self.sig_counts = cnt
        return self

    def emit(self):
        nc = self.nc
        ops = self.ops
        engs = ("pe", "act", "dve", "pool", "sp")
        sems = {}
        for e in ("pe", "act", "dve", "pool"):
            sems[e] = nc.alloc_semaphore(name=f"sig_{e}")
        used = set(op.sem for op in ops if op.dma)
        for key in sorted(used):
            sems[key] = nc.alloc_semaphore(name=f"dma_{key[0]}_{key[1]}")
        per_eng = {e: [op for op in ops if op.eng == e] for e in engs}

        def run(e_name):
            def body(eng):
                for op in per_eng[e_name]:
                    for j in op.waits:
                        oj = ops[j]
                        eng.wait_ge(sems[oj.sem] if oj.dma else sems[oj.eng], oj.semval)
                    if op.fn is None:
                        continue
                    inst = op.fn(eng)
                    if op.dma:
                        inst.then_inc(sems[op.sem], 16)
                    elif op.sig:
                        inst.then_inc(sems[op.eng], 1)
            return body

        with nc.Block() as block:
            block.sync(run("sp"))
            block.tensor(run("pe"))
            block.scalar(run("act"))
            block.vector(run("dve"))
            block.gpsimd(run("pool"))


class Buf:
    def __init__(self, space, base, off, shape, dt):
        self.space, self.off, self.shape, self.dt = space, off, list(shape), dt
        esz = 4 if dt == F32 else 2
        n = 1
        for s in shape[1:]:
            n *= s
        self.nbytes = n * esz
        pad = (self.nbytes + 3) // 4
        v = base[:, off // 4: off // 4 + pad]
        if dt != F32:
            v = v.bitcast(dt)
        v = v[:, 0:n]
        if len(shape) == 3:
            v = v.rearrange("p (a b) -> p a b", a=shape[1])
        elif len(shape) == 4:
            v = v.rearrange("p (a b c) -> p a b c", a=shape[1], b=shape[2])
        if shape[0] < 128:
            v = v[0:shape[0]]
        self.ap = v
        self.part = self.nbytes // shape[1]

    def k(self, i=None, j=None):
        if i is None:
            return ("R", self.space, self.off, self.off + self.nbytes)
        if j is None:
            j = i + 1
        return ("R", self.space, self.off + i * self.part, self.off + j * self.part)


class Arena:
    def __init__(self, space, base, limit):
        self.space, self.base, self.limit, self.ptr = space, base, limit, 0

    def alloc(self, shape, dt):
        b = Buf(self.space, self.base, self.ptr, shape, dt)
        self.ptr += (b.nbytes + GRAN - 1) // GRAN * GRAN
        assert self.ptr <= self.limit, ("arena overflow", self.space, self.ptr)
        return b

    def view(self, off, shape, dt):
        return Buf(self.space, self.base, off, shape, dt)


def build(debug=False, limit=9, kq_nb=9, kq_own=True, kq_sec=9):
    nc = bass.Bass("TRN2", target_bir_lowering=False)

    def din(name, shape):
        return nc.dram_tensor(name, list(shape), F32, kind="ExternalInput").ap()

    def dscr(name, shape, dt):
        return nc.dram_tensor(name, list(shape), dt, kind="ExternalOutput" if debug else "Internal").ap()

    xk = din("xk", [SEQ, D])
    ctxb = din("ctxb", [CTX, D])
    cT_d = din("cT", [128, 16, 2])
    bmT_d = din("bmT", [128, 96])
    n1g_d = din("n1g", [128, 16])
    n2g_d = din("n2g", [128, 16])
    gqn_d = din("gqn", [128, 4])
    gkv_d = din("gkv", [128, 2])
    gvec_d = din("gvec", [128, 8])
    gqk_d = din("gqk", [128, 384])
    sgun_d = din("sgun", [128, 16])
    w_mod = din("w_mod", [D, 6 * D])
    w_in = din("w_in", [D, IN_COLS])
    w_kv = din("w_kv", [D, 384])
    w_uq = din("w_uq", [512, 2048])
    w_ukv = din("w_ukv", [256, 2048])
    wsT_d = din("wsT", [128, 8, 128])
    bs_d = din("bsbc", [128, 1024])
    w_bra = din("w_bra", [1024, D])
    w_brs = din("w_brs", [1024, D])
    w_out = din("w_out", [D, D])
    w_f1 = din("w_f1", [D, 2 * DFF])
    w_f2 = din("w_f2", [DFF, D])
    ident_d = din("ident", [128, 128])
    cos_d = din("cosT", [64, SEQ])
    sin_d = din("sinS", [64, SEQ])
    y = nc.dram_tensor("y", [NOWN, D], F32, kind="ExternalOutput").ap()

    hT_d = dscr("hT_s", [16, 128, NOWN], BF16)
    KT_d = dscr("KT_s", [H, 128, NKEY], BF16)
    V_d = dscr("V_s", [NKT, 128, 1024], BF16)
    QTn_d = dscr("QTn_s", [H, 128, NOWN], BF16)
    QTr_d = dscr("QTr_s", [H, 64, NOWN], BF16)
    sgT_d = dscr("sgT_s", [8, 128, NOWN], BF16)
    atT_d = dscr("atT_s", [H, 128, NOWN], BF16)
    mrow_d = dscr("mrow_s", [96, 128], F32)
    rk_dbg = dscr("rk_s", [128, NKT * H], F32) if debug else None

    es = ExitStack()
    with es:
        arena_t = es.enter_context(nc.sbuf_tensor("arena", [128, ARENA_BYTES // 4], F32))
        psum_t = es.enter_context(nc.psum_tensor("psum", [128, 4096], F32))
        S = Sched(nc)
        A = Arena("S", arena_t, ARENA_BYTES)
        PA = Arena("P", psum_t, 16384)
        PB = [PA.view(b * 2048, [128, 512], F32) for b in range(8)]

        identb = A.alloc([128, 128], BF16)
        identf = A.alloc([128, 128], F32)
        onesb = A.alloc([128, 128], BF16)
        onesf = A.alloc([128, 128], F32)
        scT = A.alloc([128, 16, 2], BF16)
        mrow_sb = A.alloc([128, 128], F32)
        modFM = A.alloc([128, 96, 2], F32)
        A1 = A.alloc([128, 16, 2], F32)
        A2 = A.alloc([128, 16], F32)
        bmT = A.alloc([128, 96], F32)
        n1g = A.alloc([128, 16], F32)
        n2g = A.alloc([128, 16], F32)
        gqn = A.alloc([128, 4], F32)
        gkv = A.alloc([128, 2], F32)
        gvec = A.alloc([128, 8], F32)
        sgun = A.alloc([128, 16], F32)
        small = A.alloc([128, 16], F32)
        rk = A.alloc([128, NKT * H], F32)
        krot = A.alloc([64, NKEY], BF16)
        wslot = []
        wctr = [0]

        def alloc_slots(n):
            wslot.clear()
            wslot.extend(A.alloc([128, 4096], BF16) for _ in range(n))

        def next_slot():
            s = wslot[wctr[0] % len(wslot)]
            wctr[0] += 1
            return s

        def wload(src_ap, kchunks, ncols, reads=()):
            s = next_slot()
            assert kchunks * ncols <= 4096
            v = s.ap[:, 0:kchunks * ncols].rearrange("p (k n) -> p k n", k=kchunks)
            S.dma(v, src_ap.rearrange("(k p) n -> p k n", p=128), reads=list(reads), writes=[s.k()], q="pool")
            return s, v

        phase_base = A.ptr

        alloc_slots(6)
        cT = A.alloc([128, 16, 2], F32)
        gqk = A.alloc([128, 384], F32)
        gq2 = A.alloc([128, 384], F32)

        S.dma(identf.ap, ident_d, writes=[identf.k()])
        S.dma(cT.ap, cT_d, writes=[cT.k()])
        S.dma(bmT.ap, bmT_d, writes=[bmT.k()])
        S.dma(n1g.ap, n1g_d, writes=[n1g.k()])
        S.dma(n2g.ap, n2g_d, writes=[n2g.k()])
        S.dma(gqn.ap, gqn_d, writes=[gqn.k()])
        S.dma(gkv.ap, gkv_d, writes=[gkv.k()])
        S.dma(gvec.ap, gvec_d, writes=[gvec.k()])
        S.dma(sgun.ap, sgun_d, writes=[sgun.k()])
        S.dma(gqk.ap, gqk_d, writes=[gqk.k()])
        S.dve(lambda e: e.tensor_copy(out=identb.ap, in_=identf.ap), reads=[identf.k()], writes=[identb.k()])
        S.dve(lambda e: e.memset(onesb.ap, 1.0), writes=[onesb.k()])
        S.dve(lambda e: e.memset(onesf.ap, 1.0), writes=[onesf.k()])
        S.dve(lambda e: e.memset(small.ap[:, 2:3], EPS), writes=[small.k()])
        S.act(lambda e: e.activation(out=scT.ap, in_=cT.ap, func=AF.Silu), reads=[cT.k()], writes=[scT.k()])

        pmod = PA.view(0, [128, 96, 2], F32)
        for pn in range(16):
            s, v = wload(w_mod[:, pn * 256:(pn + 1) * 256], 16, 256)

            def f(e, v=v, pn=pn):
                i = None
                for nt in range(2):
                    col = pn * 2 + nt
                    for k in range(16):
                        i = e.matmul(pmod.ap[:, col, :], lhsT=v[:, k, nt * 128:(nt + 1) * 128], rhs=scT.ap[:, k, :],
                                     start=(k == 0), stop=(k == 15))
                return i
            S.pe(f, reads=[s.k(), scT.k()], writes=[pmod.k()])
        for r in range(2):
            S.dve(lambda e, r=r: e.tensor_tensor(out=modFM.ap[:, 0:32, r], in0=pmod.ap[:, 0:32, r], in1=bmT.ap[:, 0:32], op=ALU.add),
                  reads=[pmod.k(), bmT.k()], writes=[modFM.k(0, 32)])
        for r in range(2):
            S.dve(lambda e, r=r: e.scalar_tensor_tensor(out=A1.ap[:, :, r], in0=modFM.ap[:, 16:32, r], scalar=1.0,
                                                        in1=n1g.ap, op0=ALU.add, op1=ALU.mult),
                  reads=[modFM.k(0, 32), n1g.k()], writes=[A1.k()])
        S.dve(lambda e: e.tensor_tensor(out=small.ap[:, 0:1], in0=gvec.ap[:, 0:1], in1=gvec.ap[:, 1:2], op=ALU.mult),
              reads=[gvec.k()], writes=[small.k()])
        S.dve(lambda e: e.tensor_tensor(out=gq2.ap, in0=gqk.ap, in1=gqk.ap, op=ALU.mult), reads=[gqk.k()], writes=[gq2.k()])
        S.dve(lambda e: e.tensor_reduce(out=small.ap[:, 4:5], in_=gq2.ap[:, 0:192], axis=AX.X, op=ALU.max),
              reads=[gq2.k()], writes=[small.k()])
        S.dve(lambda e: e.tensor_reduce(out=small.ap[:, 5:6], in_=gq2.ap[:, 192:384], axis=AX.X, op=ALU.max),
              reads=[gq2.k()], writes=[small.k()])
        S.dve(lambda e: e.tensor_tensor(out=small.ap[:, 6:7], in0=small.ap[:, 4:5], in1=small.ap[:, 5:6], op=ALU.mult),
              reads=[small.k()], writes=[small.k()])
        S.act(lambda e: e.activation(out=small.ap[:, 7:8], in_=small.ap[:, 6:7], func=AF.Sqrt), reads=[small.k()], writes=[small.k()])
        S.dve(lambda e: e.tensor_scalar(out=small.ap[:, 1:2], in0=small.ap[:, 7:8], scalar1=-(192.0 ** 0.5), scalar2=None,
                                        op0=ALU.mult), reads=[small.k()], writes=[small.k()])
        if limit == 0:
            S.finalize(final_reads=[])
            S.emit()
            return nc
        A.ptr = phase_base
        bsl = [A.alloc([128, 16, 128], BF16) for _ in range(2)]
        alloc_slots(4)
        pmB = PA.view(6 * 2048 + 1792, [128, 2], F32)

        def emit_partB(idx):
            c = 32 + idx
            bs_ = bsl[idx % 2]
            S.dma(bs_.ap, w_mod[:, c * 128:(c + 1) * 128].rearrange("(k p) n -> p k n", p=128), writes=[bs_.k()], q="pool")

            def f(e, bs_=bs_):
                i = None
                for k in range(16):
                    i = e.matmul(pmB.ap, lhsT=bs_.ap[:, k, :], rhs=scT.ap[:, k, :], start=(k == 0), stop=(k == 15))
                return i
            S.pe(f, reads=[bs_.k(), scT.k()], writes=[pmB.k()])
            S.dve(lambda e, c=c: e.tensor_scalar(out=modFM.ap[:, c, :], in0=pmB.ap, scalar1=bmT.ap[:, c:c + 1], scalar2=None, op0=ALU.add),
                  reads=[pmB.k(), bmT.k()], writes=[("modB", c)])
        xt = [A.alloc([128, D], F32) for _ in range(2)]
        xs = A.alloc([128, 4, D], BF16)
        hTb = A.alloc([128, 16, 512], BF16)
        wkv = A.alloc([128, 16, 384], BF16)
        wukv = A.alloc([128, 2, 2048], BF16)
        kvnT = A.alloc([128, 2, 512], BF16)
        abc = A.alloc([128, 512], F32)
        sq = [A.alloc([128, 512], BF16) for _ in range(3)]
        sqq = [A.alloc([128, 512], BF16) for _ in range(4)]
        cs = A.alloc([64, 512], F32)
        sn = A.alloc([64, 512], F32)
        KTs = [A.alloc([128, 512], BF16) for _ in range(2)]
        Vs = [A.alloc([128, 1024], BF16) for _ in range(2)]
        rt = [A.alloc([64, 512], F32) for _ in range(4)]
        qcg = A.alloc([128, 4, 512], BF16)
        epsq = A.alloc([128, 512], F32)
        rqbs = [A.alloc([128, 512], F32) for _ in range(2)]
        uT = A.alloc([128, 8, 512], BF16)
        vg = A.alloc([128, 2, 1024], F32)
        vns = [A.alloc([128, 1024], BF16) for _ in range(2)]
        sgs = [A.alloc([128, 8, 128], BF16) for _ in range(2)]
        sgts = [A.alloc([128, 8, 128], F32) for _ in range(2)]
        QNs = [A.alloc([128, 512], BF16) for _ in range(2)]
        QRs = [A.alloc([64, 512], BF16) for _ in range(2)]
        T2 = A.alloc([128, 8, 128], F32)
        wsTb = A.alloc([128, 8, 128], BF16)
        stat = A.alloc([128, 64], F32)
        kq_end = A.ptr

        S.dma(wkv.ap, w_kv.rearrange("(k p) n -> p k n", p=128), writes=[wkv.k()], q="pool")
        S.dma(wukv.ap, w_ukv.rearrange("(k p) n -> p k n", p=128), writes=[wukv.k()], q="pool")
        S.dma(wsTb.ap, wsT_d, writes=[wsTb.k()], q="pool")
        S.dma(T2.ap.rearrange("p a b -> p (a b)"), bs_d, writes=[T2.k()])
        for hh in range(2):
            S.pe(lambda e, hh=hh: e.matmul(PB[hh].ap, lhsT=onesb.ap, rhs=wsTb.ap[:, 4 * hh:4 * hh + 4, :], start=True, stop=True),
                 reads=[onesb.k(), wsTb.k()], writes=[PB[hh].k()])
        for g in range(8):
            S.dve(lambda e, g=g: e.scalar_tensor_tensor(out=T2.ap[:, g, :], in0=PB[g // 4].ap[:, (g % 4) * 128:(g % 4 + 1) * 128],
                                                        scalar=sgun.ap[:, 8 + g:9 + g], in1=T2.ap[:, g, :],
                                                        op0=ALU.mult, op1=ALU.add),
                  reads=[PB[g // 4].k(), sgun.k(), T2.k(g)], writes=[T2.k(g)])

        rkP = PA.view(7 * 2048, [128, 32], F32)
        ptr_b = PA.view(0, [128, 4, 512], BF16)
        sctr = [0]

        def rstd_from_sum(dst, src, scale, n_read_keys, wkeys):
            S.dve(lambda e, dst=dst, src=src, scale=scale: e.tensor_scalar(out=dst, in0=src, scalar1=scale, scalar2=EPS, op0=ALU.mult, op1=ALU.add),
                  reads=n_read_keys, writes=wkeys)
            S.act(lambda e, dst=dst: e.activation(out=dst, in_=dst, func=AF.Sqrt), reads=wkeys, writes=wkeys)
            S.dve(lambda e, dst=dst: e.reciprocal(out=dst, in_=dst), reads=wkeys, writes=wkeys)

        def norm_block(xs, stat, src_rows, ntile, dst_hT, Acol, Bcol, mod_keys):
            for a in range(ntile):
                xb, xkey = src_rows(a)
                c0 = (sctr[0] % 16) * 2
                st = stat.ap[:, c0:c0 + 2]
                stk = stat.k(c0, c0 + 2)
                sctr[0] += 1
                xsa = xs.ap[:, a, :]
                S.dve(lambda e, st=st: e.memset(st, 0.0), writes=[stk])
                S.act(lambda e, xb=xb, st=st, xsa=xsa: e.activation(out=xsa, in_=xb, func=AF.Square, accum_out=st[:, 0:1]),
                      reads=[xkey, stk], writes=[xs.k(a), stk])
                rstd_from_sum(st[:, 1:2], st[:, 0:1], 1.0 / D, [stk], [stk])
                S.act(lambda e, xb=xb, st=st, xsa=xsa: e.activation(out=xsa, in_=xb, func=AF.Copy, scale=st[:, 1:2]),
                      reads=[xkey, stk], writes=[xs.k(a)])
            n = ntile * 128
            nbs = int(os.environ.get("NB_STAGE", 9))
            for jg in range(4 if nbs >= 1 else 0):
                def tr(e, jg=jg, xs=xs, ntile=ntile):
                    i = None
                    for jj in range(4):
                        for a in range(ntile):
                            i = e.transpose(out=ptr_b.ap[:, jj, a * 128:(a + 1) * 128],
                                            in_=xs.ap[:, a, (jg * 4 + jj) * 128:(jg * 4 + jj + 1) * 128], identity=identb.ap)
                    return i
                S.pe(tr, reads=[xs.k(), identb.k()], writes=[ptr_b.k()])
                for jj in range(4 if nbs >= 2 else 0):
                    j = jg * 4 + jj
                    o_ap = dst_hT.ap[:, j, 0:n]
                    i_ap = ptr_b.ap[:, jj, 0:n]
                    sc_ap, bi_ap = Acol(j), Bcol(j)
                    if True:
                        S.act(lambda e, o_ap=o_ap, i_ap=i_ap, sc_ap=sc_ap, bi_ap=bi_ap: e.activation(
                            out=o_ap, in_=i_ap, func=AF.Identity, scale=sc_ap, bias=bi_ap),
                            reads=[ptr_b.k(jj)] + mod_keys, writes=[dst_hT.k(j)])
                    else:
                        S.dve(lambda e, o_ap=o_ap, i_ap=i_ap, sc_ap=sc_ap, bi_ap=bi_ap: e.tensor_scalar(
                            out=o_ap, in0=i_ap, scalar1=sc_ap, scalar2=bi_ap, op0=ALU.mult, op1=ALU.add),
                            reads=[ptr_b.k(jj)] + mod_keys, writes=[dst_hT.k(j)])

        xctr = [0]
        for tb in range(9):
            if tb >= kq_nb and (tb != 8 or kq_sec < 0):
                continue
            is_ctx = tb == 8
            if tb < 8 and kq_nb == 9:
                for q_ in range(8):
                    emit_partB(tb * 8 + q_)
            ntile = 2 if is_ctx else 4
            ntok = ntile * 128
            r = 1 if is_ctx else 0
            def src_rows(a, tb=tb, is_ctx=is_ctx):
                b = xt[(tb * 4 + a) % 2]
                src = ctxb[a * 128:(a + 1) * 128, :] if is_ctx else xk[tb * 512 + a * 128: tb * 512 + (a + 1) * 128, :]
                S.dma(b.ap, src, writes=[b.k()])
                return b.ap, b.k()

            norm_block(xs, stat, src_rows, ntile, hTb, lambda j, r=r: A1.ap[:, j, r:r + 1], lambda j, r=r: modFM.ap[:, j, r:r + 1], [A1.k(), modFM.k(0, 32)])
            if tb < 4:
                S.dma(hT_d[:, :, tb * 512:(tb + 1) * 512].rearrange("j p t -> p j t"), hTb.ap, reads=[hTb.k()], writes=[("hT", tb)])
            if kq_sec < 1:
                continue
            pk = [PB[2], PB[3], PB[4], PB[5]]
            for m in range(4):
                mc = (m * 128, 128) if m < 2 else (256 + (m - 2) * 64, 64)

                def f(e, m=m, mc=mc, ntok=ntok):
                    i = None
                    for k in range(16):
                        i = e.matmul(pk[m].ap[0:mc[1], 0:ntok], lhsT=wkv.ap[:, k, mc[0]:mc[0] + mc[1]], rhs=hTb.ap[:, k, 0:ntok],
                                     start=(k == 0), stop=(k == 15))
                    return i
                S.pe(f, reads=[wkv.k(), hTb.k()], writes=[pk[m].k()])
            for c in range(2):
                S.act(lambda e, c=c, ntok=ntok: e.activation(out=sq[c].ap[:, 0:ntok], in_=pk[c].ap[:, 0:ntok], func=AF.Square),
                      reads=[pk[c].k()], writes=[sq[c].k()])

            def f(e, ntok=ntok):
                e.matmul(PB[6].ap[:, 0:ntok], lhsT=onesb.ap, rhs=sq[0].ap[:, 0:ntok], start=True, stop=False)
                return e.matmul(PB[6].ap[:, 0:ntok], lhsT=onesb.ap, rhs=sq[1].ap[:, 0:ntok], start=False, stop=True)
            S.pe(f, reads=[onesb.k(), sq[0].k(), sq[1].k()], writes=[PB[6].k()])
            rstd_from_sum(abc.ap[:, 0:ntok], PB[6].ap[:, 0:ntok], 1.0 / 256, [PB[6].k()], [abc.k()])
            for c in range(2):
                S.dve(lambda e, c=c, ntok=ntok: e.scalar_tensor_tensor(out=kvnT.ap[:, c, 0:ntok], in0=pk[c].ap[:, 0:ntok],
                                                                       scalar=gkv.ap[:, c:c + 1], in1=abc.ap[:, 0:ntok],
                                                                       op0=ALU.mult, op1=ALU.mult),
                      reads=[pk[c].k(), gkv.k(), abc.k()], writes=[kvnT.k(c)])
            S.act(lambda e, ntok=ntok: e.activation(out=sq[2].ap[0:64, 0:ntok], in_=pk[2].ap[0:64, 0:ntok], func=AF.Square),
                  reads=[pk[2].k()], writes=[sq[2].k()])
            kcols = slice(tb * 512, tb * 512 + ntok)
            kr_key = krot.k(tb * 512, tb * 512 + ntok)
            if is_ctx:
                S.dve(lambda e, ntok=ntok, kcols=kcols: e.tensor_scalar(out=krot.ap[:, kcols], in0=pk[2].ap[0:64, 0:ntok],
                                                                        scalar1=gvec.ap[0:64, 4:5], scalar2=None, op0=ALU.mult),
                      reads=[pk[2].k(), gvec.k()], writes=[kr_key])
            else:
                S.dma(cs.ap, cos_d[:, tb * 512:(tb + 1) * 512], writes=[cs.k()])
                S.dma(sn.ap, sin_d[:, tb * 512:(tb + 1) * 512], writes=[sn.k()])
                S.dve(lambda e: e.scalar_tensor_tensor(out=rt[0].ap, in0=pk[2].ap[0:64, :], scalar=gvec.ap[0:64, 4:5], in1=cs.ap,
                                                       op0=ALU.mult, op1=ALU.mult),
                      reads=[pk[2].k(), gvec.k(), cs.k()], writes=[rt[0].k()])
                S.dve(lambda e: e.scalar_tensor_tensor(out=rt[1].ap, in0=pk[3].ap[0:64, :], scalar=gvec.ap[0:64, 5:6], in1=sn.ap,
                                                       op0=ALU.mult, op1=ALU.mult),
                      reads=[pk[3].k(), gvec.k(), sn.k()], writes=[rt[1].k()])
                S.dve(lambda e, kcols=kcols: e.tensor_tensor(out=krot.ap[:, kcols], in0=rt[0].ap, in1=rt[1].ap, op=ALU.add),
                      reads=[rt[0].k(), rt[1].k()], writes=[kr_key])
            for h in range(H if kq_sec >= 2 else 0):
                pb = PB[2 + (h % 2)] if False else PB[2 + (h % 4)]

                def f(e, h=h, pb=pb, ntok=ntok):
                    e.matmul(pb.ap[:, 0:ntok], lhsT=wukv.ap[:, 0, h * 128:(h + 1) * 128], rhs=kvnT.ap[:, 0, 0:ntok], start=True, stop=False)
                    return e.matmul(pb.ap[:, 0:ntok], lhsT=wukv.ap[:, 1, h * 128:(h + 1) * 128], rhs=kvnT.ap[:, 1, 0:ntok], start=False, stop=True)
                S.pe(f, reads=[wukv.k(), kvnT.k()], writes=[pb.k()])
                ks = KTs[h % 2]
                S.act(lambda e, pb=pb, ks=ks, ntok=ntok: e.activation(out=ks.ap[:, 0:ntok], in_=pb.ap[:, 0:ntok], func=AF.Copy),
                      reads=[pb.k()], writes=[ks.k()])
                S.dma(KT_d[h, :, tb * 512: tb * 512 + ntok], ks.ap[:, 0:ntok], reads=[ks.k()], writes=[("KT", h, tb)])
                sqb = sq[h % 2]
                S.act(lambda e, pb=pb, sqb=sqb, ntok=ntok: e.activation(out=sqb.ap[:, 0:ntok], in_=pb.ap[:, 0:ntok], func=AF.Square),
                      reads=[pb.k()], writes=[sqb.k()])

                def f(e, h=h, sqb=sqb, tb=tb, ntile=ntile):
                    i = None
                    for a in range(ntile):
                        col = a * H + h
                        e.matmul(rkP.ap[:, col:col + 1], lhsT=sqb.ap[:, a * 128:(a + 1) * 128], rhs=onesb.ap[:, 0:1], start=True, stop=False)
                        i = e.matmul(rkP.ap[:, col:col + 1], lhsT=sq[2].ap[0:64, a * 128:(a + 1) * 128], rhs=onesb.ap[0:64, 0:1],
                                     start=False, stop=True)
                    return i
                S.pe(f, reads=[sqb.k(), sq[2].k(), onesb.k()], writes=[rkP.k()])
            if kq_sec >= 2:
                S.dve(lambda e, tb=tb, ntile=ntile: e.tensor_copy(out=rk.ap[:, tb * 32: tb * 32 + ntile * H], in_=rkP.ap[:, 0:ntile * H]),
                      reads=[rkP.k()], writes=[rk.k(tb * 32, tb * 32 + ntile * H)])
            for a in range(ntile if kq_sec >= 3 else 0):
                kt = tb * 4 + a
                vb = Vs[a % 2]
                for hh in range(2):
                    pb = PB[2 + ((a * 2 + hh) % 4)]

                    def f(e, a=a, hh=hh, pb=pb):
                        e.matmul(pb.ap, lhsT=kvnT.ap[:, 0, a * 128:(a + 1) * 128], rhs=wukv.ap[:, 0, 1024 + hh * 512:1024 + (hh + 1) * 512],
                                 start=True, stop=False)
                        return e.matmul(pb.ap, lhsT=kvnT.ap[:, 1, a * 128:(a + 1) * 128], rhs=wukv.ap[:, 1, 1024 + hh * 512:1024 + (hh + 1) * 512],
                                        start=False, stop=True)
                    S.pe(f, reads=[kvnT.k(), wukv.k()], writes=[pb.k()])
                    if hh == 0:
                        S.dve(lambda e, pb=pb, vb=vb: e.tensor_copy(out=vb.ap[:, 0:512], in_=pb.ap), reads=[pb.k()], writes=[vb.k(0, 512)])
                    else:
                        S.act(lambda e, pb=pb, vb=vb: e.activation(out=vb.ap[:, 512:1024], in_=pb.ap, func=AF.Copy),
                              reads=[pb.k()], writes=[vb.k(512, 1024)])
                S.dma(V_d[kt], vb.ap, reads=[vb.k()], writes=[("V", kt)])

            if tb >= 4 or not kq_own:
                continue
            tsl = slice(tb * 512, (tb + 1) * 512)
            s0, wq0 = wload(w_in[:, 0:256], 16, 256)
            s1, wq1 = wload(w_in[:, 256:512], 16, 256)
            osub = int(os.environ.get("OWN_SUB", 9))
            for c in range(4):
                ws_, wv_ = (s0, wq0) if c < 2 else (s1, wq1)
                pb = PB[2 + (c % 2)]

                def f(e, c=c, wv_=wv_, pb=pb):
                    i = None
                    for k in range(16):
                        i = e.matmul(pb.ap, lhsT=wv_[:, k, (c % 2) * 128:(c % 2 + 1) * 128], rhs=hTb.ap[:, k, :], start=(k == 0), stop=(k == 15))
                    return i
                if osub >= 1:
                    S.pe(f, reads=[ws_.k(), hTb.k()], writes=[pb.k()])
                if osub >= 2:
                    S.dve(lambda e, c=c, pb=pb: e.tensor_scalar(out=qcg.ap[:, c, :], in0=pb.ap, scalar1=gqn.ap[:, c:c + 1], scalar2=None, op0=ALU.mult),
                          reads=[pb.k(), gqn.k()], writes=[qcg.k(c)])
                sqb = sq[c % 2]
                if osub >= 3:
                    S.act(lambda e, pb=pb, sqb=sqb: e.activation(out=sqb.ap, in_=pb.ap, func=AF.Square), reads=[pb.k()], writes=[sqb.k()])
                if osub >= 4:
                    S.pe(lambda e, c=c, sqb=sqb: e.matmul(PB[6].ap, lhsT=onesb.ap, rhs=sqb.ap, start=(c == 0), stop=(c == 3)),
                         reads=[onesb.k(), sqb.k()], writes=[PB[6].k()])
            if osub >= 5:
                S.dve(lambda e: e.tensor_scalar(out=epsq.ap, in0=PB[6].ap, scalar1=EPS / 512.0, scalar2=EPS * EPS, op0=ALU.mult, op1=ALU.add),
                      reads=[PB[6].k()], writes=[epsq.k()])
            own_sec = int(os.environ.get("OWN_SEC", 9))
            if own_sec < 2:
                continue
            sa, wuA = wload(w_uq[:, 0:1024], 4, 1024)
            sb_, wuB = wload(w_uq[:, 1024:2048], 4, 1024)
            S.dma(cs.ap, cos_d[:, tsl], writes=[cs.k()])
            S.dma(sn.ap, sin_d[:, tsl], writes=[sn.k()])
            for h in range(H):
                pn, pr, psw, pss = (PB[2], PB[3], PB[4], PB[5]) if h % 2 == 0 else (PB[0], PB[1], PB[6], PB[7])
                sq0, sq1 = sqq[(h % 2) * 2], sqq[(h % 2) * 2 + 1]
                rqb = rqbs[h % 2]
                rt0, rt1 = rt[(h % 2) * 2], rt[(h % 2) * 2 + 1]

                def f(e, h=h, wuA=wuA, wuB=wuB, pn=pn, pr=pr, psw=psw):
                    i = None
                    for c in range(4):
                        i = e.matmul(pn.ap, lhsT=wuA[:, c, h * 128:(h + 1) * 128], rhs=qcg.ap[:, c, :], start=(c == 0), stop=(c == 3))
                    for c in range(4):
                        i = e.matmul(pr.ap[0:64, :], lhsT=wuB[:, c, h * 64:(h + 1) * 64], rhs=qcg.ap[:, c, :], start=(c == 0), stop=(c == 3))
                    for c in range(4):
                        i = e.matmul(psw.ap[0:64, :], lhsT=wuB[:, c, 512 + h * 64:512 + (h + 1) * 64], rhs=qcg.ap[:, c, :], start=(c == 0), stop=(c == 3))
                    return i
                S.pe(f, reads=[sa.k(), sb_.k(), qcg.k()], writes=[pn.k(), pr.k(), psw.k()])
                S.act(lambda e, pn=pn, sq0=sq0: e.activation(out=sq0.ap, in_=pn.ap, func=AF.Square), reads=[pn.k()], writes=[sq0.k()])
                S.act(lambda e, pr=pr, sq1=sq1: e.activation(out=sq1.ap[0:64, :], in_=pr.ap[0:64, :], func=AF.Square), reads=[pr.k()], writes=[sq1.k()])

                def f(e, pss=pss, sq0=sq0, sq1=sq1):
                    e.matmul(pss.ap, lhsT=onesb.ap, rhs=sq0.ap, start=True, stop=False)
                    return e.matmul(pss.ap, lhsT=onesb.ap[0:64, :], rhs=sq1.ap[0:64, :], start=False, stop=True)
                S.pe(f, reads=[onesb.k(), sq0.k(), sq1.k()], writes=[pss.k()])
                S.dve(lambda e, pss=pss, rqb=rqb: e.scalar_tensor_tensor(out=rqb.ap, in0=pss.ap, scalar=1.0 / 192, in1=epsq.ap, op0=ALU.mult, op1=ALU.add),
                      reads=[pss.k(), epsq.k()], writes=[rqb.k()])
                S.act(lambda e, rqb=rqb: e.activation(out=rqb.ap, in_=rqb.ap, func=AF.Sqrt), reads=[rqb.k()], writes=[rqb.k()])
                S.dve(lambda e, rqb=rqb: e.reciprocal(out=rqb.ap, in_=rqb.ap), reads=[rqb.k()], writes=[rqb.k()])
                qn_, qr_ = QNs[h % 2], QRs[h % 2]
                S.dve(lambda e, qn_=qn_, pn=pn, rqb=rqb: e.scalar_tensor_tensor(out=qn_.ap, in0=pn.ap, scalar=small.ap[:, 0:1], in1=rqb.ap,
                                                                                op0=ALU.mult, op1=ALU.mult),
                      reads=[pn.k(), small.k(), rqb.k()], writes=[qn_.k()])
                S.dve(lambda e, pr=pr, rt0=rt0: e.scalar_tensor_tensor(out=rt0.ap, in0=pr.ap[0:64, :], scalar=gvec.ap[0:64, 2:3], in1=cs.ap,
                                                                       op0=ALU.mult, op1=ALU.mult),
                      reads=[pr.k(), gvec.k(), cs.k()], writes=[rt0.k()])
                S.dve(lambda e, psw=psw, rt1=rt1: e.scalar_tensor_tensor(out=rt1.ap, in0=psw.ap[0:64, :], scalar=gvec.ap[0:64, 3:4], in1=sn.ap,
                                                                         op0=ALU.mult, op1=ALU.mult),
                      reads=[psw.k(), gvec.k(), sn.k()], writes=[rt1.k()])
                S.dve(lambda e, rt0=rt0, rt1=rt1: e.tensor_tensor(out=rt0.ap, in0=rt0.ap, in1=rt1.ap, op=ALU.add),
                      reads=[rt0.k(), rt1.k()], writes=[rt0.k()])
                S.dve(lambda e, qr_=qr_, rt0=rt0, rqb=rqb: e.tensor_tensor(out=qr_.ap, in0=rt0.ap, in1=rqb.ap[0:64, :], op=ALU.mult),
                      reads=[rt0.k(), rqb.k()], writes=[qr_.k()])
                S.dma(QTn_d[h, :, tsl], qn_.ap, reads=[qn_.k()], writes=[("QTn", h, tb)])
                S.dma(QTr_d[h, :, tsl], qr_.ap, reads=[qr_.k()], writes=[("QTr", h, tb)])
            if own_sec < 3:
                continue
            for pnl in range(4):
                s, wu_ = wload(w_in[:, OFF_U + pnl * 256: OFF_U + (pnl + 1) * 256], 16, 256)
                for nt in range(2):
                    g = pnl * 2 + nt
                    pb = PB[2 + (g % 4)]

                    def f(e, wu_=wu_, nt=nt, pb=pb):
                        i = None
                        for k in range(16):
                            i = e.matmul(pb.ap, lhsT=wu_[:, k, nt * 128:(nt + 1) * 128], rhs=hTb.ap[:, k, :], start=(k == 0), stop=(k == 15))
                        return i
                    S.pe(f, reads=[s.k(), hTb.k()], writes=[pb.k()])
                    S.act(lambda e, g=g, pb=pb: e.activation(out=uT.ap[:, g, :], in_=pb.ap, func=AF.Gelu), reads=[pb.k()], writes=[uT.k(g)])
            for half in range(2 if own_sec >= 4 else 0):
                S.dve(lambda e: e.memset(stat.ap[:, 32:48], 0.0), writes=[stat.k(32, 48)])
                for pnl in range(4):
                    s, wv_ = wload(w_in[:, OFF_V + pnl * 256: OFF_V + (pnl + 1) * 256], 16, 256)
                    for t2 in range(2):
                        a = half * 2 + t2
                        pb = PB[2 + ((pnl * 2 + t2) % 4)]

                        def f(e, wv_=wv_, a=a, pb=pb):
                            i = None
                            for k in range(16):
                                i = e.matmul(pb.ap[:, 0:256], lhsT=hTb.ap[:, k, a * 128:(a + 1) * 128], rhs=wv_[:, k, :], start=(k == 0), stop=(k == 15))
                            return i
                        S.pe(f, reads=[s.k(), hTb.k()], writes=[pb.k()])
                        c0 = 32 + t2 * 8 + pnl
                        S.act(lambda e, pb=pb, t2=t2, pnl=pnl, c0=c0: e.activation(out=vg.ap[:, t2, pnl * 256:(pnl + 1) * 256], in_=pb.ap[:, 0:256],
                                                                                   func=AF.Gelu, accum_out=stat.ap[:, c0:c0 + 1]),
                              reads=[pb.k()], writes=[vg.k(t2), stat.k(c0, c0 + 1)])
                        S.act(lambda e, t2=t2, pnl=pnl, c0=c0: e.activation(out=vns[t2].ap[:, pnl * 256:(pnl + 1) * 256], in_=vg.ap[:, t2, pnl * 256:(pnl + 1) * 256],
                                                                            func=AF.Square, accum_out=stat.ap[:, c0 + 4:c0 + 5]),
                              reads=[vg.k(t2)], writes=[vns[t2].k(), stat.k(c0 + 4, c0 + 5)])
                for t2 in range(2):
                    a = half * 2 + t2
                    c0 = 32 + t2 * 8
                    vn, sgt = vns[t2], sgts[t2]
                    spm0, spm1 = (PB[0], PB[1]) if t2 == 0 else (PB[6], PB[7])
                    mu, e2, var = stat.ap[:, 48 + t2 * 4:49 + t2 * 4], stat.ap[:, 49 + t2 * 4:50 + t2 * 4], stat.ap[:, 50 + t2 * 4:51 + t2 * 4]
                    sk = stat.k(48 + t2 * 4, 52 + t2 * 4)
                    sk_in = stat.k(c0, c0 + 8)
                    S.dve(lambda e, c0=c0, mu=mu: e.tensor_reduce(out=mu, in_=stat.ap[:, c0:c0 + 4], axis=AX.X, op=ALU.add), reads=[sk_in], writes=[sk])
                    S.dve(lambda e, c0=c0, e2=e2: e.tensor_reduce(out=e2, in_=stat.ap[:, c0 + 4:c0 + 8], axis=AX.X, op=ALU.add), reads=[sk_in], writes=[sk])
                    S.dve(lambda e, mu=mu: e.tensor_scalar(out=mu, in0=mu, scalar1=1.0 / 1024, scalar2=None, op0=ALU.mult), reads=[sk], writes=[sk])
                    S.dve(lambda e, mu=mu, var=var: e.tensor_tensor(out=var, in0=mu, in1=mu, op=ALU.mult), reads=[sk], writes=[sk])
                    S.dve(lambda e, e2=e2, var=var: e.scalar_tensor_tensor(out=var, in0=e2, scalar=1.0 / 1024, in1=var, op0=ALU.mult, op1=ALU.subtract),
                          reads=[sk], writes=[sk])
                    S.dve(lambda e, var=var: e.tensor_scalar(out=var, in0=var, scalar1=EPS, scalar2=None, op0=ALU.add), reads=[sk], writes=[sk])
                    S.act(lambda e, var=var: e.activation(out=var, in_=var, func=AF.Sqrt), reads=[sk], writes=[sk])
                    S.dve(lambda e, var=var: e.reciprocal(out=var, in_=var), reads=[sk], writes=[sk])
                    S.dve(lambda e, t2=t2, mu=mu, var=var, vn=vn: e.tensor_scalar(out=vn.ap, in0=vg.ap[:, t2, :], scalar1=mu, scalar2=var,
                                                                                  op0=ALU.subtract, op1=ALU.mult),
                          reads=[vg.k(t2), sk], writes=[vn.k()])

                    def f(e, vn=vn, spm0=spm0, spm1=spm1):
                        i = None
                        for g in range(8):
                            pm = spm0 if g < 4 else spm1
                            i = e.matmul(pm.ap[:, (g % 4) * 128:(g % 4 + 1) * 128], lhsT=vn.ap[:, g * 128:(g + 1) * 128], rhs=wsTb.ap[:, g, :],
                                         start=True, stop=True)
                        return i
                    S.pe(f, reads=[vn.k(), wsTb.k()], writes=[spm0.k(), spm1.k()])
                    for g in range(8):
                        pm = spm0 if g < 4 else spm1
                        S.dve(lambda e, g=g, pm=pm, sgt=sgt: e.scalar_tensor_tensor(out=sgt.ap[:, g, :], in0=pm.ap[:, (g % 4) * 128:(g % 4 + 1) * 128],
                                                                                    scalar=sgun.ap[:, g:g + 1], in1=T2.ap[:, g, :], op0=ALU.mult, op1=ALU.add),
                              reads=[pm.k(), sgun.k(), T2.k(g)], writes=[sgt.k(g)])
                    so = sgs[a % 2]
                    S.dve(lambda e, so=so, a=a, sgt=sgt: e.tensor_tensor(out=so.ap, in0=sgt.ap, in1=uT.ap[:, :, a * 128:(a + 1) * 128], op=ALU.mult),
                          reads=[sgt.k(), uT.k()], writes=[so.k()])
                    S.dma(sgT_d[:, :, tb * 512 + a * 128: tb * 512 + (a + 1) * 128].rearrange("g p t -> p g t"), so.ap,
                          reads=[so.k()], writes=[("sgT", tb, a)])

        S.dve(lambda e: e.scalar_tensor_tensor(out=A2.ap, in0=modFM.ap[:, 64:80, 0], scalar=1.0, in1=n2g.ap,
                                               op0=ALU.add, op1=ALU.mult),
              reads=[("modB", c) for c in range(32, 96)] + [n2g.k()], writes=[A2.k()])
        pm1 = PA.view(2048, [128, 128], F32)
        S.pe(lambda e: e.transpose(out=pm1.ap[0:96, :], in_=modFM.ap[:, :, 0], identity=identf.ap),
             reads=[("modB", c) for c in range(32, 96)] + [modFM.k(0, 32), identf.k()], writes=[pm1.k()])
        S.dve(lambda e: e.tensor_copy(out=mrow_sb.ap[0:96, :], in_=pm1.ap[0:96, :]), reads=[pm1.k()], writes=[mrow_sb.k()])
        S.dma(mrow_d, mrow_sb.ap[0:96, :], reads=[mrow_sb.k()], writes=["mrow"])

        S.dve(lambda e: e.tensor_scalar(out=rk.ap, in0=rk.ap, scalar1=1.0, scalar2=192.0 * EPS, op0=ALU.mult, op1=ALU.add),
              reads=[rk.k()], writes=[rk.k()])
        S.act(lambda e: e.activation(out=rk.ap, in_=rk.ap, func=AF.Sqrt), reads=[rk.k()], writes=[rk.k()])
        S.dve(lambda e: e.reciprocal(out=rk.ap, in_=rk.ap), reads=[rk.k()], writes=[rk.k()])
        if debug:
            S.dma(rk_dbg, rk.ap, reads=[rk.k()], writes=["rkdbg"])

        if limit == 1:
            S.finalize(final_reads=list(S.dram_keys))
            S.emit()
            return nc
        A.ptr = phase_base
        KTh = [A.alloc([128, NKEY], BF16) for _ in range(2)]
        Vh = [A.alloc([128, NKT, 128], BF16) for _ in range(2)]
        Qn = [A.alloc([128, NOWN], BF16) for _ in range(2)]
        Qr = [A.alloc([64, NOWN], BF16) for _ in range(2)]
        PT = [A.alloc([128, 1024], BF16) for _ in range(4)]
        rden = A.alloc([128, 1024], F32)
        lnd = A.alloc([128, 1024], F32)
        daccs = [[A.alloc([128, 1024], F32) for _ in range(4)] for _ in range(2)]
        zer = A.alloc([128, 1024], F32)
        posb = [A.alloc([128, 1024], F32) for _ in range(2)]
        ats = [A.alloc([128, NOWN], BF16) for _ in range(2)]
        ST = [PA.view(i * 4096, [128, 1024], F32) for i in range(3)]
        po = PA.view(3 * 4096, [128, 1024], F32)
        S.dve(lambda e: e.memset(zer.ap, 0.0), writes=[zer.k()])
        qctr = 0
        pending = []

        def epilogue(dacc, psb, ab, q0, hh, last_qb, pd):
            d0, d1, d2, d3 = dacc
            S.add("pool", lambda e: e.tensor_tensor(out=d0.ap, in0=d0.ap, in1=d1.ap, op=ALU.add), reads=[d0.k(), d1.k()], writes=[d0.k()])
            S.add("pool", lambda e: e.tensor_tensor(out=d2.ap, in0=d2.ap, in1=d3.ap, op=ALU.add), reads=[d2.k(), d3.k()], writes=[d2.k()])
            S.add("pool", lambda e: e.tensor_tensor(out=d0.ap, in0=d0.ap, in1=d2.ap, op=ALU.add), reads=[d0.k(), d2.k()], writes=[d0.k()])

            def f(e):
                e.matmul(pd.ap[:, 0:512], lhsT=onesf.ap, rhs=d0.ap[:, 0:512], start=True, stop=True)
                return e.matmul(pd.ap[:, 512:1024], lhsT=onesf.ap, rhs=d0.ap[:, 512:1024], start=True, stop=True)
            S.pe(f, reads=[onesf.k(), d0.k()], writes=[pd.k()])
            S.act(lambda e: e.activation(out=lnd.ap, in_=pd.ap, func=AF.Ln), reads=[pd.k()], writes=[lnd.k()])
            S.act(lambda e: e.activation(out=rden.ap, in_=lnd.ap, func=AF.Exp, scale=-1.0), reads=[lnd.k()], writes=[rden.k()])
            S.add("pool", lambda e: e.tensor_tensor(out=ab.ap[:, q0:q0 + 1024], in0=psb.ap, in1=rden.ap, op=ALU.mult),
                  reads=[psb.k(), rden.k()], writes=[ab.k(q0, q0 + 1024)])
            if last_qb:
                S.dma(atT_d[hh], ab.ap, reads=[ab.k()], writes=[("atT", hh)])

        for h in range(H):
            kb, vb, qn_, qr_ = KTh[h % 2], Vh[h % 2], Qn[h % 2], Qr[h % 2]
            S.dma(kb.ap, KT_d[h], reads=[("KT", h, tb) for tb in range(9)], writes=[kb.k()])
            S.dma(vb.ap, V_d[:, :, h * 128:(h + 1) * 128].rearrange("k p d -> p k d"), reads=[("V", kt) for kt in range(NKT)], writes=[vb.k()])
            S.dma(qn_.ap, QTn_d[h], reads=[("QTn", h, tb) for tb in range(4)], writes=[qn_.k()])
            S.dma(qr_.ap, QTr_d[h], reads=[("QTr", h, tb) for tb in range(4)], writes=[qr_.k()])
            ab = ats[h % 2]
            for qb in range(2):
                q0 = qb * 1024
                dacc = daccs[qctr % 2]
                psb = posb[qctr % 2]
                qctr += 1

                def s_op(kt, kb=kb, qn_=qn_, qr_=qr_, q0=q0):
                    pb = ST[kt % 3]

                    def f(e):
                        i = None
                        for hf in range(2):
                            e.matmul(pb.ap[:, hf * 512:(hf + 1) * 512], lhsT=kb.ap[:, kt * 128:(kt + 1) * 128],
                                     rhs=qn_.ap[:, q0 + hf * 512: q0 + (hf + 1) * 512], start=True, stop=False)
                        for hf in range(2):
                            i = e.matmul(pb.ap[:, hf * 512:(hf + 1) * 512], lhsT=krot.ap[:, kt * 128:(kt + 1) * 128],
                                         rhs=qr_.ap[:, q0 + hf * 512: q0 + (hf + 1) * 512], start=False, stop=True)
                        return i
                    S.pe(f, reads=[kb.k(), qn_.k(), qr_.k(), krot.k()], writes=[pb.k()])
                s_op(0)
                s_op(1)
                for kt in range(NKT):
                    if kt == 14 and pending:
                        epilogue(*pending.pop(0), pd=ST[(kt + 2) % 3])
                    if kt + 2 < NKT and not (kt == 14 and False):
                        s_op(kt + 2)
                    pb = ST[kt % 3]
                    pt = PT[kt % 4]
                    S.act(lambda e, pb=pb, pt=pt, kt=kt, h=h: e.activation(out=pt.ap, in_=pb.ap, func=AF.Exp, scale=rk.ap[:, kt * H + h:kt * H + h + 1],
                                                                           bias=small.ap[:, 1:2]),
                          reads=[pb.k(), rk.k(), small.k()], writes=[pt.k()])

                    def f(e, pt=pt, kt=kt, vb=vb):
                        e.matmul(po.ap[:, 0:512], lhsT=vb.ap[:, kt, :], rhs=pt.ap[:, 0:512], start=(kt == 0), stop=(kt == NKT - 1))
                        return e.matmul(po.ap[:, 512:1024], lhsT=vb.ap[:, kt, :], rhs=pt.ap[:, 512:1024], start=(kt == 0), stop=(kt == NKT - 1))
                    S.pe(f, reads=[pt.k(), vb.k()], writes=[po.k()])
                    da = dacc[kt % 4]
                    src0 = zer if kt < 4 else da
                    S.dve(lambda e, pt=pt, da=da, src0=src0: e.tensor_tensor(out=da.ap, in0=src0.ap, in1=pt.ap, op=ALU.add),
                          reads=[pt.k(), src0.k()], writes=[da.k()])
                S.act(lambda e, psb=psb: e.activation(out=psb.ap, in_=po.ap, func=AF.Copy), reads=[po.k()], writes=[psb.k()])
                pending.append((dacc, psb, ab, q0, h, qb == 1))
                if h == H - 1 and qb == 1:
                    while pending:
                        epilogue(*pending.pop(0), pd=ST[0])

        if limit == 2:
            S.finalize(final_reads=["rkdbg"] + [("atT", h) for h in range(H)])
            S.emit()
            return nc
        A.ptr = phase_base
        alloc_slots(7)
        g1bc = A.alloc([128, D], F32)
        g2bc = A.alloc([128, D], F32)
        hTb2s = [A.alloc([128, 16, 512], BF16) for _ in range(1)]
        actT = A.alloc([128, 44, 512], BF16)
        atbs = [A.alloc([128, 8, 512], BF16)]
        sgbs = [A.alloc([128, 8, 512], BF16)]
        mgT = A.view(actT.off + 16384, [128, 16, 512], BF16)
        xnew = A.alloc([128, 4, D], F32)
        xs2 = A.view(actT.off, [128, 4, D], BF16)
        sg = [A.alloc([128, 512], BF16) for _ in range(2)]
        tt = [A.alloc([128, 512], F32) for _ in range(2)]
        stat2 = A.alloc([128, 64], F32)

        S.dma(g1bc.ap, mrow_d[32:48, :].rearrange("a b -> (a b)").partition_broadcast(128), reads=["mrow"], writes=[g1bc.k()])
        S.dma(g2bc.ap, mrow_d[80:96, :].rearrange("a b -> (a b)").partition_broadcast(128), reads=["mrow"], writes=[g2bc.k()])
        out_keys = []
        for tb in range(4):
            tsl = slice(tb * 512, (tb + 1) * 512)
            hTb2, atb, sgb = hTb2s[0], atbs[0], sgbs[0]
            S.dma(hTb2.ap, hT_d[:, :, tsl].rearrange("j p t -> p j t"), reads=[("hT", tb)], writes=[hTb2.k()])
            S.dma(atb.ap, atT_d[:, :, tsl].rearrange("h p t -> p h t"), reads=[("atT", h) for h in range(H)], writes=[atb.k()])
            S.dma(sgb.ap, sgT_d[:, :, tsl].rearrange("g p t -> p g t"), reads=[("sgT", tb, a) for a in range(4)], writes=[sgb.k()])
            for a in range(4):
                S.dma(xnew.ap[:, a, :], xk[tb * 512 + a * 128: tb * 512 + (a + 1) * 128, :], writes=[xnew.k(a)])
            for n2 in range(8):
                c0 = n2 * 256
                s_ga, w_ga = wload(w_in[:, OFF_GATE + c0: OFF_GATE + c0 + 256], 16, 256)
                s_gs, w_gs = wload(w_in[:, OFF_GATE + D + c0: OFF_GATE + D + c0 + 256], 16, 256)
                s_ba = next_slot()
                s_bs = s_ba
                w_ba = s_ba.ap[:, 0:2048].rearrange("p (k n) -> p k n", k=8)
                w_bs = s_ba.ap[:, 2048:4096].rearrange("p (k n) -> p k n", k=8)
                S.dma(w_ba, w_bra[:, c0:c0 + 256].rearrange("(k p) n -> p k n", p=128), writes=[s_ba.k(0, 2048)], q="pool")
                S.dma(w_bs, w_brs[:, c0:c0 + 256].rearrange("(k p) n -> p k n", p=128), writes=[s_ba.k(2048, 4096)], q="pool")
                for nt in range(2):
                    n = n2 * 2 + nt
                    base = 4 * (n % 2)
                    p0, p1, p2, p3 = PB[base], PB[base + 1], PB[base + 2], PB[base + 3]
                    cs_ = slice(nt * 128, (nt + 1) * 128)

                    def f(e, w_ga=w_ga, w_gs=w_gs, w_ba=w_ba, w_bs=w_bs, cs_=cs_, p0=p0, p1=p1, p2=p2, p3=p3, hTb2=hTb2, atb=atb, sgb=sgb):
                        i = None
                        for k in range(16):
                            i = e.matmul(p0.ap, lhsT=w_ga[:, k, cs_], rhs=hTb2.ap[:, k, :], start=(k == 0), stop=(k == 15))
                        for k in range(16):
                            i = e.matmul(p1.ap, lhsT=w_gs[:, k, cs_], rhs=hTb2.ap[:, k, :], start=(k == 0), stop=(k == 15))
                        for k in range(8):
                            i = e.matmul(p2.ap, lhsT=w_ba[:, k, cs_], rhs=atb.ap[:, k, :], start=(k == 0), stop=(k == 7))
                        for k in range(8):
                            i = e.matmul(p3.ap, lhsT=w_bs[:, k, cs_], rhs=sgb.ap[:, k, :], start=(k == 0), stop=(k == 7))
                        return i
                    S.pe(f, reads=[s_ga.k(), s_gs.k(), s_ba.k(), s_bs.k(), hTb2.k(), atb.k(), sgb.k()],
                         writes=[p0.k(), p1.k(), p2.k(), p3.k()])
                    S.act(lambda e, p0=p0: e.activation(out=sg[0].ap, in_=p0.ap, func=AF.Sigmoid), reads=[p0.k()], writes=[sg[0].k()])
                    S.act(lambda e, p1=p1: e.activation(out=sg[1].ap, in_=p1.ap, func=AF.Sigmoid), reads=[p1.k()], writes=[sg[1].k()])
                    S.dve(lambda e, p2=p2: e.tensor_tensor(out=tt[0].ap, in0=p2.ap, in1=sg[0].ap, op=ALU.mult),
                          reads=[p2.k(), sg[0].k()], writes=[tt[0].k()])
                    S.dve(lambda e, p3=p3: e.tensor_tensor(out=tt[1].ap, in0=p3.ap, in1=sg[1].ap, op=ALU.mult),
                          reads=[p3.k(), sg[1].k()], writes=[tt[1].k()])
                    S.dve(lambda e, n=n: e.tensor_tensor(out=mgT.ap[:, n, :], in0=tt[0].ap, in1=tt[1].ap, op=ALU.add),
                          reads=[tt[0].k(), tt[1].k()], writes=[mgT.k(n)])
            for cb in range(4):
                s_a, w_a = wload(w_out[0:1024, cb * 512:(cb + 1) * 512], 8, 512)
                s_b, w_b = wload(w_out[1024:2048, cb * 512:(cb + 1) * 512], 8, 512)
                for a in range(4):
                    pb = PB[(cb * 4 + a) % 8]

                    def f(e, a=a, w_a=w_a, w_b=w_b, pb=pb):
                        i = None
                        for n in range(16):
                            wv_ = w_a if n < 8 else w_b
                            i = e.matmul(pb.ap, lhsT=mgT.ap[:, n, a * 128:(a + 1) * 128], rhs=wv_[:, n % 8, :], start=(n == 0), stop=(n == 15))
                        return i
                    S.pe(f, reads=[s_a.k(), s_b.k(), mgT.k()], writes=[pb.k()])
                    t_ = tt[a % 2]
                    S.dve(lambda e, pb=pb, t_=t_, cb=cb: e.tensor_tensor(out=t_.ap, in0=pb.ap, in1=g1bc.ap[:, cb * 512:(cb + 1) * 512], op=ALU.mult),
                          reads=[pb.k(), g1bc.k()], writes=[t_.k()])
                    xv = xnew.ap[:, a, cb * 512:(cb + 1) * 512]
                    xkk = ("R", "S", xnew.off + a * xnew.part + cb * 2048, xnew.off + a * xnew.part + (cb + 1) * 2048)
                    S.dve(lambda e, xv=xv, t_=t_: e.tensor_tensor(out=xv, in0=xv, in1=t_.ap, op=ALU.add),
                          reads=[t_.k(), xkk], writes=[xkk])
            norm_block(xs2, stat2, lambda a: (xnew.ap[:, a, :], xnew.k(a)), 4, hTb2, lambda j: A2.ap[:, j:j + 1],
                       lambda j: modFM.ap[:, 48 + j, 0:1], [A2.k()] + [("modB", c) for c in range(48, 64)])
            for fp in range(22):
                s_a, w_a = wload(w_f1[:, fp * 256:(fp + 1) * 256], 16, 256)
                s_b, w_b = wload(w_f1[:, DFF + fp * 256: DFF + (fp + 1) * 256], 16, 256)
                for ft in range(2):
                    f_ = fp * 2 + ft
                    base = 2 * (f_ % 4)
                    pa, pb = PB[base], PB[base + 1]
                    cs_ = slice(ft * 128, (ft + 1) * 128)

                    def f(e, w_a=w_a, w_b=w_b, cs_=cs_, pa=pa, pb=pb, hTb2=hTb2):
                        i = None
                        for k in range(16):
                            i = e.matmul(pa.ap, lhsT=w_a[:, k, cs_], rhs=hTb2.ap[:, k, :], start=(k == 0), stop=(k == 15))
                        for k in range(16):
                            i = e.matmul(pb.ap, lhsT=w_b[:, k, cs_], rhs=hTb2.ap[:, k, :], start=(k == 0), stop=(k == 15))
                        return i
                    S.pe(f, reads=[s_a.k(), s_b.k(), hTb2.k()], writes=[pa.k(), pb.k()])
                    sl = sg[f_ % 2]
                    S.act(lambda e, pa=pa, sl=sl: e.activation(out=sl.ap, in_=pa.ap, func=AF.Silu), reads=[pa.k()], writes=[sl.k()])
                    S.dve(lambda e, pb=pb, sl=sl, f_=f_: e.tensor_tensor(out=actT.ap[:, f_, :], in0=pb.ap, in1=sl.ap, op=ALU.mult),
                          reads=[pb.k(), sl.k()], writes=[actT.k(f_)])
            for ch in range(2):
                for fg in range(11):
                    s_w, w_w = wload(w_f2[fg * 512:(fg + 1) * 512, ch * 1024:(ch + 1) * 1024], 4, 1024)

                    def f(e, fg=fg, w_w=w_w):
                        i = None
                        for fi in range(4):
                            f_ = fg * 4 + fi
                            for a in range(4):
                                for c2 in range(2):
                                    i = e.matmul(PB[a * 2 + c2].ap, lhsT=actT.ap[:, f_, a * 128:(a + 1) * 128], rhs=w_w[:, fi, c2 * 512:(c2 + 1) * 512],
                                                 start=(f_ == 0), stop=(f_ == 43))
                        return i
                    S.pe(f, reads=[s_w.k(), actT.k(fg * 4, fg * 4 + 4)], writes=[PB[b_].k() for b_ in range(8)])
                for a in range(4):
                    for c2 in range(2):
                        cb = ch * 2 + c2
                        pb = PB[a * 2 + c2]
                        t_ = tt[(a * 2 + c2) % 2]
                        S.dve(lambda e, pb=pb, t_=t_, cb=cb: e.tensor_tensor(out=t_.ap, in0=pb.ap, in1=g2bc.ap[:, cb * 512:(cb + 1) * 512], op=ALU.mult),
                              reads=[pb.k(), g2bc.k()], writes=[t_.k()])
                        xv = xnew.ap[:, a, cb * 512:(cb + 1) * 512]
                        xkk = ("R", "S", xnew.off + a * xnew.part + cb * 2048, xnew.off + a * xnew.part + (cb + 1) * 2048)
                        S.dve(lambda e, xv=xv, t_=t_: e.tensor_tensor(out=xv, in0=xv, in1=t_.ap, op=ALU.add),
                              reads=[t_.k(), xkk], writes=[xkk])
            for a in range(4):
                S.dma(y[tb * 512 + a * 128: tb * 512 + (a + 1) * 128, :], xnew.ap[:, a, :], reads=[xnew.k(a)], writes=[("y", tb, a)])
                out_keys.append(("y", tb, a))

        fin = list(out_keys)
        if debug:
            fin += ["rkdbg"]
        S.finalize(final_reads=fin)
        S.emit()
    return nc


def _rope_tables():
    nf = 16
    freqs = (np.float32(10000.0) ** (-np.arange(nf, dtype=np.float32) / np.float32(nf))).astype(np.float32)
    pos = np.arange(SEQ)
    row = (pos // 64).astype(np.float32)
    col = (pos % 64).astype(np.float32)
    ang_r = (row[:, None] * freqs[None, :]).astype(np.float32)
    ang_c = (col[:, None] * freqs[None, :]).astype(np.float32)
    cosT = np.zeros((64, SEQ), np.float32)
    sinS = np.zeros((64, SEQ), np.float32)
    for r in range(64):
        ang = ang_r if r < 32 else ang_c
        f = r % 16
        first = (r % 32) < 16
        cosT[r] = np.cos(ang[:, f])
        sinS[r] = (-np.sin(ang[:, f])) if first else np.sin(ang[:, f])
    return cosT, sinS


_SIGMA = np.array([(r + 16) if (r % 32) < 16 else (r - 16) for r in range(64)])


def _fm(v, nchunk):
    return np.ascontiguousarray(np.asarray(v, np.float32).reshape(nchunk, 128).T)


def make_in_maps(inputs):
    f = lambda k: np.asarray(inputs[k], np.float32)
    x, c, ctx, c_ctx = f("x"), f("c"), f("ctx"), f("c_ctx")
    w_in = np.ascontiguousarray(f("w_in")[0])
    w_uq = f("w_uq")[0].reshape(512, H, 192)
    w_ukv = f("w_ukv")[0].reshape(256, H, 256)
    gq, gk = f("qk_norm_q")[0], f("qk_norm_k")[0]
    cosT, sinS = _rope_tables()
    kr = w_in[:, 768:832]
    w_kv = np.ascontiguousarray(np.concatenate([w_in[:, 512:768], kr, kr[:, _SIGMA]], axis=1))
    w_uq_l = np.ascontiguousarray(np.concatenate([
        w_uq[:, :, 0:128].reshape(512, 1024),
        w_uq[:, :, 128:192].reshape(512, 512),
        w_uq[:, :, 128:192][:, :, _SIGMA].reshape(512, 512)], axis=1))
    w_ukv_l = np.ascontiguousarray(np.concatenate([w_ukv[:, :, 0:128].reshape(256, 1024), w_ukv[:, :, 128:256].reshape(256, 1024)], axis=1))
    gvec = np.zeros((128, 8), np.float32)
    gvec[:, 0] = gq[0:128]
    gvec[:, 1] = gk[0:128]
    gvec[:64, 2] = gq[128:192]
    gvec[:64, 3] = gq[128:192][_SIGMA]
    gvec[:64, 4] = gk[128:192]
    gvec[:64, 5] = gk[128:192][_SIGMA]
    gqk = np.ascontiguousarray(np.broadcast_to(np.concatenate([gq, gk])[None, :], (128, 384)))
    sgun = np.ascontiguousarray(np.concatenate([_fm(f("sgu_norm_g")[0], 8), _fm(f("sgu_norm_b")[0], 8)], axis=1))
    wsT = np.ascontiguousarray(f("w_spatial")[0].transpose(2, 0, 1))
    bsbc = np.ascontiguousarray(np.broadcast_to(f("b_spatial")[0].reshape(1, 1024), (128, 1024)))
    shared = {
        "bmT": _fm(f("b_mod")[0], 96), "n1g": _fm(f("norm1_g")[0], 16), "n2g": _fm(f("norm2_g")[0], 16),
        "gqn": _fm(f("q_norm_g")[0], 4), "gkv": _fm(f("kv_norm_g")[0], 2), "gvec": gvec, "gqk": gqk, "sgun": sgun,
        "w_mod": np.ascontiguousarray(f("w_mod")[0]), "w_in": w_in, "w_kv": w_kv, "w_uq": w_uq_l, "w_ukv": w_ukv_l,
        "wsT": wsT, "bsbc": bsbc, "w_bra": np.ascontiguousarray(f("w_br_attn")[0]), "w_brs": np.ascontiguousarray(f("w_br_sgu")[0]),
        "w_out": np.ascontiguousarray(f("w_out")[0]), "w_f1": np.ascontiguousarray(f("w_ffn_in")[0]),
        "w_f2": np.ascontiguousarray(f("w_ffn_out")[0]), "ident": np.eye(128, dtype=np.float32),
    }
    maps = []
    for i in range(8):
        b, hf = i // 2, i % 2
        own = slice(hf * NOWN, (hf + 1) * NOWN)
        oth = slice((1 - hf) * NOWN, (2 - hf) * NOWN)
        m = dict(shared)
        m["xk"] = np.ascontiguousarray(np.concatenate([x[b, own], x[b, oth]], axis=0))
        m["ctxb"] = np.ascontiguousarray(ctx[b])
        cT = np.stack([_fm(c[b], 16), _fm(c_ctx, 16)], axis=-1)
        m["cT"] = np.ascontiguousarray(cT)
        m["cosT"] = np.ascontiguousarray(np.concatenate([cosT[:, own], cosT[:, oth]], axis=1))
        m["sinS"] = np.ascontiguousarray(np.concatenate([sinS[:, own], sinS[:, oth]], axis=1))
        maps.append(m)
    return maps


_NC_CACHE = {}


def kernel(**inputs):
    maps = make_in_maps(inputs)
    if "nc" not in _NC_CACHE:
        _NC_CACHE["nc"] = build(False)
    res = run_bass_kernel_spmd(_NC_CACHE["nc"], maps, core_ids=list(range(8)))
    out = np.empty((4, SEQ, D), np.float32)
    for i in range(8):
        b, hf = i // 2, i % 2
        out[b, hf * NOWN:(hf + 1) * NOWN] = res.results[i]["y"]
    return out
```

```python
import os
import numpy as np
from contextlib import ExitStack
import concourse.bass as bass
import concourse.mybir as mybir
from concourse.bass_utils import run_bass_kernel_spmd

F32 = mybir.dt.float32
BF16 = mybir.dt.bfloat16
AF = mybir.ActivationFunctionType
ALU = mybir.AluOpType
AX = mybir.AxisListType

D = 2048
SEQ = 4096
NOWN = 2048
CTX = 256
NKEY = SEQ + CTX
NKT = NKEY // 128
H = 8
OFF_U = 832
OFF_V = 1856
OFF_GATE = 2880
IN_COLS = 6976
DFF = 5632
EPS = 1e-6
GRAN = 256
ARENA_BYTES = 206 * 1024


class Op:
    __slots__ = ("eng", "fn", "deps", "dma", "waits", "sig", "semval", "sem", "name", "banks", "cost", "odeps")

    def __init__(self, eng, fn, deps, dma, name):
        self.eng, self.fn, self.deps, self.dma, self.name = eng, fn, deps, dma, name
        self.waits = []
        self.sig = False
        self.semval = None
        self.sem = None
        self.banks = ()
        self.cost = 0.5
        self.odeps = set()


class _FakeInst:
    def then_inc(self, *a, **k):
        return self


class _FakeEng:
    def __init__(self):
        self.cost = 0.0
        self.bytes = 0

    @staticmethod
    def _free(ap):
        n = 1
        for d in ap.shape[1:]:
            n *= d
        return n

    def __getattr__(self, name):
        def f(*a, **k):
            if name == "matmul":
                n = self._free(k["rhs"])
                self.cost += max(n, 64) / 2000.0 + 0.01
            elif name == "transpose":
                self.cost += 0.12
            elif name == "dma_start":
                o = k["out"]
                self.bytes += self._free(o) * o.shape[0] * (4 if o.dtype == F32 else 2)
            elif name == "memset":
                self.cost += 0.1
            else:
                o = k.get("out", a[0] if a else None)
                n = self._free(o) if o is not None else 512
                self.cost += 0.2 + n * (0.0065 if name == "reciprocal" else 0.00105)
            return _FakeInst()
        return f


class Sched:
    def __init__(self, nc, n_dma_sems=12):
        self.nc = nc
        self.ops = []
        self.last_w = {}
        self.readers = {}
        self.n_dma_sems = n_dma_sems
        self.dram_keys = []
        self.bank_acc = {}

    @staticmethod
    def _expand(keys):
        out = []
        for k in keys:
            if isinstance(k, tuple) and len(k) == 4 and k[0] == "R":
                for g in range(k[2] // GRAN, (k[3] + GRAN - 1) // GRAN):
                    out.append((k[1], g))
            else:
                out.append(k)
        return out

    def add(self, eng, fn, reads=(), writes=(), dma=False, name=""):
        i = len(self.ops)
        for k in writes:
            if not (isinstance(k, tuple) and len(k) == 4 and k[0] == "R"):
                self.dram_keys.append(k)
        reads = self._expand(reads)
        writes = self._expand(writes)
        deps = set()
        lw, rd = self.last_w, self.readers
        for k in reads:
            j = lw.get(k)
            if j is not None:
                deps.add(j)
        for k in writes:
            j = lw.get(k)
            if j is not None:
                deps.add(j)
            r = rd.get(k)
            if r:
                deps.update(r)
        banks = set()
        for k in reads + writes:
            if isinstance(k, tuple) and len(k) == 2 and k[0] == "P":
                banks.add(k[1] // 8)
        for b in banks:
            d = self.bank_acc.setdefault(b, {})
            for e2, idx in d.items():
                if e2 != eng:
                    deps.add(idx)
            d[eng] = i
        deps.discard(i)
        for k in reads:
            rd.setdefault(k, []).append(i)
        for k in writes:
            lw[k] = i
            rd[k] = []
        op = Op(eng, fn, deps, dma, name)
        op.banks = tuple(banks)
        self.ops.append(op)
        return i

    def pe(self, fn, reads=(), writes=(), name=""):
        return self.add("pe", fn, reads, writes, name=name)

    def act(self, fn, reads=(), writes=(), name=""):
        return self.add("act", fn, reads, writes, name=name)

    def dve(self, fn, reads=(), writes=(), name=""):
        return self.add("dve", fn, reads, writes, name=name)

    def dma(self, out, in_, reads=(), writes=(), q="sp", name="", **kw):
        return self.add(q, lambda e: e.dma_start(out=out, in_=in_, **kw), reads, writes,
                        dma=True, name=name)

    def reorder(self, window=400):
        ops = self.ops
        n = len(ops)
        for op in ops:
            if op.fn is None:
                op.cost = 0.0
                continue
            fe = _FakeEng()
            op.fn(fe)
            if op.dma:
                op.cost = (1.0 if op.eng == "pool" else 0.06, 2.0 + fe.bytes / 250e3)
            else:
                op.cost = fe.cost * (2.5 if op.eng == "pool" else 1.0)
        lastb = {}
        for i, op in enumerate(ops):
            op.odeps = set()
            for b in op.banks:
                j = lastb.get((op.eng, b))
                if j is not None:
                    op.odeps.add(j)
                lastb[(op.eng, b)] = i
        engs = ("pe", "act", "dve", "pool", "sp")
        queues = {e: [i for i, op in enumerate(ops) if op.eng == e] for e in engs}
        qpos = {e: 0 for e in engs}
        succ = [[] for _ in range(n)]
        npred = [0] * n
        for i, op in enumerate(ops):
            ps = op.deps | op.odeps
            npred[i] = len(ps)
            for j in ps:
                succ[j].append(i)
        ready_t = [0.0] * n
        done = [False] * n
        free_t = {e: 0.0 for e in engs}
        last_i = n - 1
        order = []
        remaining = n
        while remaining:
            best = None
            for e in engs:
                q = queues[e]
                p = qpos[e]
                while p < len(q) and done[q[p]]:
                    p += 1
                qpos[e] = p
                cnt = 0
                k = p
                while k < len(q) and cnt < window:
                    i = q[k]
                    k += 1
                    if done[i]:
                        continue
                    cnt += 1
                    if npred[i] or (i == last_i and remaining > 1):
                        continue
                    st = max(free_t[e], ready_t[i])
                    if best is None or st < best[0] - 1e-9 or (abs(st - best[0]) <= 1e-9 and i < best[1]):
                        best = (st, i, e)
                    if ready_t[i] <= free_t[e]:
                        break
            assert best is not None, "scheduler deadlock"
            st, i, e = best
            op = ops[i]
            if op.dma:
                free_t[e] = st + op.cost[0]
                fin = st + op.cost[0] + op.cost[1]
            else:
                free_t[e] = st + op.cost
                fin = free_t[e] + 0.15
            done[i] = True
            remaining -= 1
            order.append(i)
            for j in succ[i]:
                npred[j] -= 1
                if ready_t[j] < fin:
                    ready_t[j] = fin
        newidx = {old: new for new, old in enumerate(order)}
        new_ops = []
        for old in order:
            op = ops[old]
            op.deps = set(newidx[j] for j in op.deps)
            new_ops.append(op)
        self.ops = new_ops
        self.est_time = max(free_t.values())

    def finalize(self, final_reads=(), reorder=True):
        self.add("sp", None, reads=final_reads, name="final")
        if reorder:
            self.reorder()
        ops = self.ops
        engs = ("pe", "act", "dve", "pool", "sp")
        cur = {e: {} for e in engs}
        dma_known = {e: set() for e in engs}
        clock = [None] * len(ops)
        dma_rr = {e: 0 for e in engs}
        dma_last = {}
        dma_uses = {}
        for i, op in enumerate(ops):
            E = op.eng
            c = cur[E]
            deps = set(op.deps)
            if op.dma:
                slot = dma_rr[E] % self.n_dma_sems
                dma_rr[E] += 1
                prev = dma_last.get((E, slot))
                if prev is not None:
                    deps.add(prev)
                dma_last[(E, slot)] = i
                dma_uses[(E, slot)] = dma_uses.get((E, slot), 0) + 1
                op.sem = (E, slot)
                op.semval = 16 * dma_uses[(E, slot)]
            waits = []
            for j in sorted(deps):
                oj = ops[j]
                if oj.dma:
                    if j in dma_known[E]:
                        continue
                    waits.append(j)
                    dma_known[E].add(j)
                else:
                    if c.get(oj.eng, -1) >= j:
                        continue
                    if oj.eng == E and E == "pe":
                        continue
                    waits.append(j)
                    oj.sig = True
            for j in waits:
                for k, v in clock[j].items():
                    if c.get(k, -1) < v:
                        c[k] = v
            op.waits = waits
            ck = dict(c)
            if not op.dma:
                ck[E] = i
                c[E] = max(c.get(E, -1), -1)
            clock[i] = ck
        cnt = {e: 0 for e in engs}
        for op in ops:
            if op.sig and not op.dma:
                cnt[op.eng] += 1
                op.semval = cnt[op.eng]
        self.sig_counts = cnt
        return self

    def emit(self):
        nc = self.nc
        ops = self.ops
        engs = ("pe", "act", "dve", "pool", "sp")
        sems = {}
        for e in ("pe", "act", "dve", "pool"):
            sems[e] = nc.alloc_semaphore(name=f"sig_{e}")
        used = set(op.sem for op in ops if op.dma)
        for key in sorted(used):
            sems[key] = nc.alloc_semaphore(name=f"dma_{key[0]}_{key[1]}")
        per_eng = {e: [op for op in ops if op.eng == e] for e in engs}

        def run(e_name):
            def body(eng):
                for op in per_eng[e_name]:
                    for j in op.waits:
                        oj = ops[j]
                        eng.wait_ge(sems[oj.sem] if oj.dma else sems[oj.eng], oj.semval)
                    if op.fn is None:
                        continue
                    inst = op.fn(eng)
                    if op.dma:
                        inst.then_inc(sems[op.sem], 16)
                    elif op.sig:
                        inst.then_inc(sems[op.eng], 1)
            return body

        with nc.Block() as block:
            block.sync(run("sp"))
            block.tensor(run("pe"))
            block.scalar(run("act"))
            block.vector(run("dve"))
            block.gpsimd(run("pool"))


class Buf:
    def __init__(self, space, base, off, shape, dt):
        self.space, self.off, self.shape, self.dt = space, off, list(shape), dt
        esz = 4 if dt == F32 else 2
        n = 1
        for s in shape[1:]:
            n *= s
        self.nbytes = n * esz
        pad = (self.nbytes + 3) // 4
        v = base[:, off // 4: off // 4 + pad]
        if dt != F32:
            v = v.bitcast(dt)
        v = v[:, 0:n]
        if len(shape) == 3:
            v = v.rearrange("p (a b) -> p a b", a=shape[1])
        elif len(shape) == 4:
            v = v.rearrange("p (a b c) -> p a b c", a=shape[1], b=shape[2])
        if shape[0] < 128:
            v = v[0:shape[0]]
        self.ap = v
        self.part = self.nbytes // shape[1]

    def k(self, i=None, j=None):
        if i is None:
            return ("R", self.space, self.off, self.off + self.nbytes)
        if j is None:
            j = i + 1
        return ("R", self.space, self.off + i * self.part, self.off + j * self.part)


class Arena:
    def __init__(self, space, base, limit):
        self.space, self.base, self.limit, self.ptr = space, base, limit, 0

    def alloc(self, shape, dt):
        b = Buf(self.space, self.base, self.ptr, shape, dt)
        self.ptr += (b.nbytes + GRAN - 1) // GRAN * GRAN
        assert self.ptr <= self.limit, ("arena overflow", self.space, self.ptr)
        return b

    def view(self, off, shape, dt):
        return Buf(self.space, self.base, off, shape, dt)


def build(debug=False, limit=9, kq_nb=9, kq_own=True, kq_sec=9):
    nc = bass.Bass("TRN2", target_bir_lowering=False)

    def din(name, shape):
        return nc.dram_tensor(name, list(shape), F32, kind="ExternalInput").ap()

    def dscr(name, shape, dt):
        return nc.dram_tensor(name, list(shape), dt, kind="ExternalOutput" if debug else "Internal").ap()

    xk = din("xk", [SEQ, D])
    ctxb = din("ctxb", [CTX, D])
    cT_d = din("cT", [128, 16, 2])
    bmT_d = din("bmT", [128, 96])
    n1g_d = din("n1g", [128, 16])
    n2g_d = din("n2g", [128, 16])
    gqn_d = din("gqn", [128, 4])
    gkv_d = din("gkv", [128, 2])
    gvec_d = din("gvec", [128, 8])
    gqk_d = din("gqk", [128, 384])
    sgun_d = din("sgun", [128, 16])
    w_mod = din("w_mod", [D, 6 * D])
    w_in = din("w_in", [D, IN_COLS])
    w_kv = din("w_kv", [D, 384])
    w_uq = din("w_uq", [512, 2048])
    w_ukv = din("w_ukv", [256, 2048])
    wsT_d = din("wsT", [128, 8, 128])
    bs_d = din("bsbc", [128, 1024])
    w_bra = din("w_bra", [1024, D])
    w_brs = din("w_brs", [1024, D])
    w_out = din("w_out", [D, D])
    w_f1 = din("w_f1", [D, 2 * DFF])
    w_f2 = din("w_f2", [DFF, D])
    ident_d = din("ident", [128, 128])
    cos_d = din("cosT", [64, SEQ])
    sin_d = din("sinS", [64, SEQ])
    y = nc.dram_tensor("y", [NOWN, D], F32, kind="ExternalOutput").ap()

    hT_d = dscr("hT_s", [16, 128, NOWN], BF16)
    KT_d = dscr("KT_s", [H, 128, NKEY], BF16)
    V_d = dscr("V_s", [NKT, 128, 1024], BF16)
    QTn_d = dscr("QTn_s", [H, 128, NOWN], BF16)
    QTr_d = dscr("QTr_s", [H, 64, NOWN], BF16)
    sgT_d = dscr("sgT_s", [8, 128, NOWN], BF16)
    atT_d = dscr("atT_s", [H, 128, NOWN], BF16)
    mrow_d = dscr("mrow_s", [96, 128], F32)
    rk_dbg = dscr("rk_s", [128, NKT * H], F32) if debug else None

    es = ExitStack()
    with es:
        arena_t = es.enter_context(nc.sbuf_tensor("arena", [128, ARENA_BYTES // 4], F32))
        psum_t = es.enter_context(nc.psum_tensor("psum", [128, 4096], F32))
        S = Sched(nc)
        A = Arena("S", arena_t, ARENA_BYTES)
        PA = Arena("P", psum_t, 16384)
        PB = [PA.view(b * 2048, [128, 512], F32) for b in range(8)]

        identb = A.alloc([128, 128], BF16)
        identf = A.alloc([128, 128], F32)
        onesb = A.alloc([128, 128], BF16)
        onesf = A.alloc([128, 128], F32)
        scT = A.alloc([128, 16, 2], BF16)
        mrow_sb = A.alloc([128, 128], F32)
        modFM = A.alloc([128, 96, 2], F32)
        A1 = A.alloc([128, 16, 2], F32)
        A2 = A.alloc([128, 16], F32)
        bmT = A.alloc([128, 96], F32)
        n1g = A.alloc([128, 16], F32)
        n2g = A.alloc([128, 16], F32)
        gqn = A.alloc([128, 4], F32)
        gkv = A.alloc([128, 2], F32)
        gvec = A.alloc([128, 8], F32)
        sgun = A.alloc([128, 16], F32)
        small = A.alloc([128, 16], F32)
        rk = A.alloc([128, NKT * H], F32)
        krot = A.alloc([64, NKEY], BF16)
        wslot = []
        wctr = [0]

        def alloc_slots(n):
            wslot.clear()
            wslot.extend(A.alloc([128, 4096], BF16) for _ in range(n))

        def next_slot():
            s = wslot[wctr[0] % len(wslot)]
            wctr[0] += 1
            return s

        def wload(src_ap, kchunks, ncols, reads=()):
            s = next_slot()
            assert kchunks * ncols <= 4096
            v = s.ap[:, 0:kchunks * ncols].rearrange("p (k n) -> p k n", k=kchunks)
            S.dma(v, src_ap.rearrange("(k p) n -> p k n", p=128), reads=list(reads), writes=[s.k()], q="pool")
            return s, v

        phase_base = A.ptr

        alloc_slots(6)
        cT = A.alloc([128, 16, 2], F32)
        gqk = A.alloc([128, 384], F32)
        gq2 = A.alloc([128, 384], F32)

        S.dma(identf.ap, ident_d, writes=[identf.k()])
        S.dma(cT.ap, cT_d, writes=[cT.k()])
        S.dma(bmT.ap, bmT_d, writes=[bmT.k()])
        S.dma(n1g.ap, n1g_d, writes=[n1g.k()])
        S.dma(n2g.ap, n2g_d, writes=[n2g.k()])
        S.dma(gqn.ap, gqn_d, writes=[gqn.k()])
        S.dma(gkv.ap, gkv_d, writes=[gkv.k()])
        S.dma(gvec.ap, gvec_d, writes=[gvec.k()])
        S.dma(sgun.ap, sgun_d, writes=[sgun.k()])
        S.dma(gqk.ap, gqk_d, writes=[gqk.k()])
        S.dve(lambda e: e.tensor_copy(out=identb.ap, in_=identf.ap), reads=[identf.k()], writes=[identb.k()])
        S.dve(lambda e: e.memset(onesb.ap, 1.0), writes=[onesb.k()])
        S.dve(lambda e: e.memset(onesf.ap, 1.0), writes=[onesf.k()])
        S.dve(lambda e: e.memset(small.ap[:, 2:3], EPS), writes=[small.k()])
        S.act(lambda e: e.activation(out=scT.ap, in_=cT.ap, func=AF.Silu), reads=[cT.k()], writes=[scT.k()])

        pmod = PA.view(0, [128, 96, 2], F32)
        for pn in range(16):
            s, v = wload(w_mod[:, pn * 256:(pn + 1) * 256], 16, 256)

            def f(e, v=v, pn=pn):
                i = None
                for nt in range(2):
                    col = pn * 2 + nt
                    for k in range(16):
                        i = e.matmul(pmod.ap[:, col, :], lhsT=v[:, k, nt * 128:(nt + 1) * 128], rhs=scT.ap[:, k, :],
                                     start=(k == 0), stop=(k == 15))
                return i
            S.pe(f, reads=[s.k(), scT.k()], writes=[pmod.k()])
        for r in range(2):
            S.dve(lambda e, r=r: e.tensor_tensor(out=modFM.ap[:, 0:32, r], in0=pmod.ap[:, 0:32, r], in1=bmT.ap[:, 0:32], op=ALU.add),
                  reads=[pmod.k(), bmT.k()], writes=[modFM.k(0, 32)])
        for r in range(2):
            S.dve(lambda e, r=r: e.scalar_tensor_tensor(out=A1.ap[:, :, r], in0=modFM.ap[:, 16:32, r], scalar=1.0,
                                                        in1=n1g.ap, op0=ALU.add, op1=ALU.mult),
                  reads=[modFM.k(0, 32), n1g.k()], writes=[A1.k()])
        S.dve(lambda e: e.tensor_tensor(out=small.ap[:, 0:1], in0=gvec.ap[:, 0:1], in1=gvec.ap[:, 1:2], op=ALU.mult),
              reads=[gvec.k()], writes=[small.k()])
        S.dve(lambda e: e.tensor_tensor(out=gq2.ap, in0=gqk.ap, in1=gqk.ap, op=ALU.mult), reads=[gqk.k()], writes=[gq2.k()])
        S.dve(lambda e: e.tensor_reduce(out=small.ap[:, 4:5], in_=gq2.ap[:, 0:192], axis=AX.X, op=ALU.max),
              reads=[gq2.k()], writes=[small.k()])
        S.dve(lambda e: e.tensor_reduce(out=small.ap[:, 5:6], in_=gq2.ap[:, 192:384], axis=AX.X, op=ALU.max),
              reads=[gq2.k()], writes=[small.k()])
        S.dve(lambda e: e.tensor_tensor(out=small.ap[:, 6:7], in0=small.ap[:, 4:5], in1=small.ap[:, 5:6], op=ALU.mult),
              reads=[small.k()], writes=[small.k()])
        S.act(lambda e: e.activation(out=small.ap[:, 7:8], in_=small.ap[:, 6:7], func=AF.Sqrt), reads=[small.k()], writes=[small.k()])
        S.dve(lambda e: e.tensor_scalar(out=small.ap[:, 1:2], in0=small.ap[:, 7:8], scalar1=-(192.0 ** 0.5), scalar2=None,
                                        op0=ALU.mult), reads=[small.k()], writes=[small.k()])
        if limit == 0:
            S.finalize(final_reads=[])
            S.emit()
            return nc
        A.ptr = phase_base
        bsl = [A.alloc([128, 16, 128], BF16) for _ in range(2)]
        alloc_slots(4)
        pmB = PA.view(6 * 2048 + 1792, [128, 2], F32)

        def emit_partB(idx):
            c = 32 + idx
            bs_ = bsl[idx % 2]
            S.dma(bs_.ap, w_mod[:, c * 128:(c + 1) * 128].rearrange("(k p) n -> p k n", p=128), writes=[bs_.k()], q="pool")

            def f(e, bs_=bs_):
                i = None
                for k in range(16):
                    i = e.matmul(pmB.ap, lhsT=bs_.ap[:, k, :], rhs=scT.ap[:, k, :], start=(k == 0), stop=(k == 15))
                return i
            S.pe(f, reads=[bs_.k(), scT.k()], writes=[pmB.k()])
            S.dve(lambda e, c=c: e.tensor_scalar(out=modFM.ap[:, c, :], in0=pmB.ap, scalar1=bmT.ap[:, c:c + 1], scalar2=None, op0=ALU.add),
                  reads=[pmB.k(), bmT.k()], writes=[("modB", c)])
        xt = [A.alloc([128, D], F32) for _ in range(2)]
        xs = A.alloc([128, 4, D], BF16)
        hTb = A.alloc([128, 16, 512], BF16)
        wkv = A.alloc([128, 16, 384], BF16)
        wukv = A.alloc([128, 2, 2048], BF16)
        kvnT = A.alloc([128, 2, 512], BF16)
        abc = A.alloc([128, 512], F32)
        sq = [A.alloc([128, 512], BF16) for _ in range(3)]
        sqq = [A.alloc([128, 512], BF16) for _ in range(4)]
        cs = A.alloc([64, 512], F32)
        sn = A.alloc([64, 512], F32)
        KTs = [A.alloc([128, 512], BF16) for _ in range(2)]
        Vs = [A.alloc([128, 1024], BF16) for _ in range(2)]
        rt = [A.alloc([64, 512], F32) for _ in range(4)]
        qcg = A.alloc([128, 4, 512], BF16)
        epsq = A.alloc([128, 512], F32)
        rqbs = [A.alloc([128, 512], F32) for _ in range(2)]
        uT = A.alloc([128, 8, 512], BF16)
        vg = A.alloc([128, 2, 1024], F32)
        vns = [A.alloc([128, 1024], BF16) for _ in range(2)]
        sgs = [A.alloc([128, 8, 128], BF16) for _ in range(2)]
        sgts = [A.alloc([128, 8, 128], F32) for _ in range(2)]
        QNs = [A.alloc([128, 512], BF16) for _ in range(2)]
        QRs = [A.alloc([64, 512], BF16) for _ in range(2)]
        T2 = A.alloc([128, 8, 128], F32)
        wsTb = A.alloc([128, 8, 128], BF16)
        stat = A.alloc([128, 64], F32)
        kq_end = A.ptr

        S.dma(wkv.ap, w_kv.rearrange("(k p) n -> p k n", p=128), writes=[wkv.k()], q="pool")
        S.dma(wukv.ap, w_ukv.rearrange("(k p) n -> p k n", p=128), writes=[wukv.k()], q="pool")
        S.dma(wsTb.ap, wsT_d, writes=[wsTb.k()], q="pool")
        S.dma(T2.ap.rearrange("p a b -> p (a b)"), bs_d, writes=[T2.k()])
        for hh in range(2):
            S.pe(lambda e, hh=hh: e.matmul(PB[hh].ap, lhsT=onesb.ap, rhs=wsTb.ap[:, 4 * hh:4 * hh + 4, :], start=True, stop=True),
                 reads=[onesb.k(), wsTb.k()], writes=[PB[hh].k()])
        for g in range(8):
            S.dve(lambda e, g=g: e.scalar_tensor_tensor(out=T2.ap[:, g, :], in0=PB[g // 4].ap[:, (g % 4) * 128:(g % 4 + 1) * 128],
                                                        scalar=sgun.ap[:, 8 + g:9 + g], in1=T2.ap[:, g, :],
                                                        op0=ALU.mult, op1=ALU.add),
                  reads=[PB[g // 4].k(), sgun.k(), T2.k(g)], writes=[T2.k(g)])

        rkP = PA.view(7 * 2048, [128, 32], F32)
        ptr_b = PA.view(0, [128, 4, 512], BF16)
        sctr = [0]

        def rstd_from_sum(dst, src, scale, n_read_keys, wkeys):
            S.dve(lambda e, dst=dst, src=src, scale=scale: e.tensor_scalar(out=dst, in0=src, scalar1=scale, scalar2=EPS, op0=ALU.mult, op1=ALU.add),
                  reads=n_read_keys, writes=wkeys)
            S.act(lambda e, dst=dst: e.activation(out=dst, in_=dst, func=AF.Sqrt), reads=wkeys, writes=wkeys)
            S.dve(lambda e, dst=dst: e.reciprocal(out=dst, in_=dst), reads=wkeys, writes=wkeys)

        def norm_block(xs, stat, src_rows, ntile, dst_hT, Acol, Bcol, mod_keys):
            for a in range(ntile):
                xb, xkey = src_rows(a)
                c0 = (sctr[0] % 16) * 2
                st = stat.ap[:, c0:c0 + 2]
                stk = stat.k(c0, c0 + 2)
                sctr[0] += 1
                xsa = xs.ap[:, a, :]
                S.dve(lambda e, st=st: e.memset(st, 0.0), writes=[stk])
                S.act(lambda e, xb=xb, st=st, xsa=xsa: e.activation(out=xsa, in_=xb, func=AF.Square, accum_out=st[:, 0:1]),
                      reads=[xkey, stk], writes=[xs.k(a), stk])
                rstd_from_sum(st[:, 1:2], st[:, 0:1], 1.0 / D, [stk], [stk])
                S.act(lambda e, xb=xb, st=st, xsa=xsa: e.activation(out=xsa, in_=xb, func=AF.Copy, scale=st[:, 1:2]),
                      reads=[xkey, stk], writes=[xs.k(a)])
            n = ntile * 128
            nbs = int(os.environ.get("NB_STAGE", 9))
            for jg in range(4 if nbs >= 1 else 0):
                def tr(e, jg=jg, xs=xs, ntile=ntile):
                    i = None
                    for jj in range(4):
                        for a in range(ntile):
                            i = e.transpose(out=ptr_b.ap[:, jj, a * 128:(a + 1) * 128],
                                            in_=xs.ap[:, a, (jg * 4 + jj) * 128:(jg * 4 + jj + 1) * 128], identity=identb.ap)
                    return i
                S.pe(tr, reads=[xs.k(), identb.k()], writes=[ptr_b.k()])
                for jj in range(4 if nbs >= 2 else 0):
                    j = jg * 4 + jj
                    o_ap = dst_hT.ap[:, j, 0:n]
                    i_ap = ptr_b.ap[:, jj, 0:n]
                    sc_ap, bi_ap = Acol(j), Bcol(j)
                    if True:
                        S.act(lambda e, o_ap=o_ap, i_ap=i_ap, sc_ap=sc_ap, bi_ap=bi_ap: e.activation(
                            out=o_ap, in_=i_ap, func=AF.Identity, scale=sc_ap, bias=bi_ap),
                            reads=[ptr_b.k(jj)] + mod_keys, writes=[dst_hT.k(j)])
                    else:
                        S.dve(lambda e, o_ap=o_ap, i_ap=i_ap, sc_ap=sc_ap, bi_ap=bi_ap: e.tensor_scalar(
                            out=o_ap, in0=i_ap, scalar1=sc_ap, scalar2=bi_ap, op0=ALU.mult, op1=ALU.add),
                            reads=[ptr_b.k(jj)] + mod_keys, writes=[dst_hT.k(j)])

        xctr = [0]
        for tb in range(9):
            if tb >= kq_nb and (tb != 8 or kq_sec < 0):
                continue
            is_ctx = tb == 8
            if tb < 8 and kq_nb == 9:
                for q_ in range(8):
                    emit_partB(tb * 8 + q_)
            ntile = 2 if is_ctx else 4
            ntok = ntile * 128
            r = 1 if is_ctx else 0
            def src_rows(a, tb=tb, is_ctx=is_ctx):
                b = xt[(tb * 4 + a) % 2]
                src = ctxb[a * 128:(a + 1) * 128, :] if is_ctx else xk[tb * 512 + a * 128: tb * 512 + (a + 1) * 128, :]
                S.dma(b.ap, src, writes=[b.k()])
                return b.ap, b.k()

            norm_block(xs, stat, src_rows, ntile, hTb, lambda j, r=r: A1.ap[:, j, r:r + 1], lambda j, r=r: modFM.ap[:, j, r:r + 1], [A1.k(), modFM.k(0, 32)])
            if tb < 4:
                S.dma(hT_d[:, :, tb * 512:(tb + 1) * 512].rearrange("j p t -> p j t"), hTb.ap, reads=[hTb.k()], writes=[("hT", tb)])
            if kq_sec < 1:
                continue
            pk = [PB[2], PB[3], PB[4], PB[5]]
            for m in range(4):
                mc = (m * 128, 128) if m < 2 else (256 + (m - 2) * 64, 64)

                def f(e, m=m, mc=mc, ntok=ntok):
                    i = None
                    for k in range(16):
                        i = e.matmul(pk[m].ap[0:mc[1], 0:ntok], lhsT=wkv.ap[:, k, mc[0]:mc[0] + mc[1]], rhs=hTb.ap[:, k, 0:ntok],
                                     start=(k == 0), stop=(k == 15))
                    return i
                S.pe(f, reads=[wkv.k(), hTb.k()], writes=[pk[m].k()])
            for c in range(2):
                S.act(lambda e, c=c, ntok=ntok: e.activation(out=sq[c].ap[:, 0:ntok], in_=pk[c].ap[:, 0:ntok], func=AF.Square),
                      reads=[pk[c].k()], writes=[sq[c].k()])

            def f(e, ntok=ntok):
                e.matmul(PB[6].ap[:, 0:ntok], lhsT=onesb.ap, rhs=sq[0].ap[:, 0:ntok], start=True, stop=False)
                return e.matmul(PB[6].ap[:, 0:ntok], lhsT=onesb.ap, rhs=sq[1].ap[:, 0:ntok], start=False, stop=True)
            S.pe(f, reads=[onesb.k(), sq[0].k(), sq[1].k()], writes=[PB[6].k()])
            rstd_from_sum(abc.ap[:, 0:ntok], PB[6].ap[:, 0:ntok], 1.0 / 256, [PB[6].k()], [abc.k()])
            for c in range(2):
                S.dve(lambda e, c=c, ntok=ntok: e.scalar_tensor_tensor(out=kvnT.ap[:, c, 0:ntok], in0=pk[c].ap[:, 0:ntok],
                                                                       scalar=gkv.ap[:, c:c + 1], in1=abc.ap[:, 0:ntok],
                                                                       op0=ALU.mult, op1=ALU.mult),
                      reads=[pk[c].k(), gkv.k(), abc.k()], writes=[kvnT.k(c)])
            S.act(lambda e, ntok=ntok: e.activation(out=sq[2].ap[0:64, 0:ntok], in_=pk[2].ap[0:64, 0:ntok], func=AF.Square),
                  reads=[pk[2].k()], writes=[sq[2].k()])
            kcols = slice(tb * 512, tb * 512 + ntok)
            kr_key = krot.k(tb * 512, tb * 512 + ntok)
            if is_ctx:
                S.dve(lambda e, ntok=ntok, kcols=kcols: e.tensor_scalar(out=krot.ap[:, kcols], in0=pk[2].ap[0:64, 0:ntok],
                                                                        scalar1=gvec.ap[0:64, 4:5], scalar2=None, op0=ALU.mult),
                      reads=[pk[2].k(), gvec.k()], writes=[kr_key])
            else:
                S.dma(cs.ap, cos_d[:, tb * 512:(tb + 1) * 512], writes=[cs.k()])
                S.dma(sn.ap, sin_d[:, tb * 512:(tb + 1) * 512], writes=[sn.k()])
                S.dve(lambda e: e.scalar_tensor_tensor(out=rt[0].ap, in0=pk[2].ap[0:64, :], scalar=gvec.ap[0:64, 4:5], in1=cs.ap,
                                                       op0=ALU.mult, op1=ALU.mult),
                      reads=[pk[2].k(), gvec.k(), cs.k()], writes=[rt[0].k()])
                S.dve(lambda e: e.scalar_tensor_tensor(out=rt[1].ap, in0=pk[3].ap[0:64, :], scalar=gvec.ap[0:64, 5:6], in1=sn.ap,
                                                       op0=ALU.mult, op1=ALU.mult),
                      reads=[pk[3].k(), gvec.k(), sn.k()], writes=[rt[1].k()])
                S.dve(lambda e, kcols=kcols: e.tensor_tensor(out=krot.ap[:, kcols], in0=rt[0].ap, in1=rt[1].ap, op=ALU.add),
                      reads=[rt[0].k(), rt[1].k()], writes=[kr_key])
            for h in range(H if kq_sec >= 2 else 0):
                pb = PB[2 + (h % 2)] if False else PB[2 + (h % 4)]

                def f(e, h=h, pb=pb, ntok=ntok):
                    e.matmul(pb.ap[:, 0:ntok], lhsT=wukv.ap[:, 0, h * 128:(h + 1) * 128], rhs=kvnT.ap[:, 0, 0:ntok], start=True, stop=False)
                    return e.matmul(pb.ap[:, 0:ntok], lhsT=wukv.ap[:, 1, h * 128:(h + 1) * 128], rhs=kvnT.ap[:, 1, 0:ntok], start=False, stop=True)
                S.pe(f, reads=[wukv.k(), kvnT.k()], writes=[pb.k()])
                ks = KTs[h % 2]
                S.act(lambda e, pb=pb, ks=ks, ntok=ntok: e.activation(out=ks.ap[:, 0:ntok], in_=pb.ap[:, 0:ntok], func=AF.Copy),
                      reads=[pb.k()], writes=[ks.k()])
                S.dma(KT_d[h, :, tb * 512: tb * 512 + ntok], ks.ap[:, 0:ntok], reads=[ks.k()], writes=[("KT", h, tb)])
                sqb = sq[h % 2]
                S.act(lambda e, pb=pb, sqb=sqb, ntok=ntok: e.activation(out=sqb.ap[:, 0:ntok], in_=pb.ap[:, 0:ntok], func=AF.Square),
                      reads=[pb.k()], writes=[sqb.k()])

                def f(e, h=h, sqb=sqb, tb=tb, ntile=ntile):
                    i = None
                    for a in range(ntile):
                        col = a * H + h
                        e.matmul(rkP.ap[:, col:col + 1], lhsT=sqb.ap[:, a * 128:(a + 1) * 128], rhs=onesb.ap[:, 0:1], start=True, stop=False)
                        i = e.matmul(rkP.ap[:, col:col + 1], lhsT=sq[2].ap[0:64, a * 128:(a + 1) * 128], rhs=onesb.ap[0:64, 0:1],
                                     start=False, stop=True)
                    return i
                S.pe(f, reads=[sqb.k(), sq[2].k(), onesb.k()], writes=[rkP.k()])
            if kq_sec >= 2:
                S.dve(lambda e, tb=tb, ntile=ntile: e.tensor_copy(out=rk.ap[:, tb * 32: tb * 32 + ntile * H], in_=rkP.ap[:, 0:ntile * H]),
                      reads=[rkP.k()], writes=[rk.k(tb * 32, tb * 32 + ntile * H)])
            for a in range(ntile if kq_sec >= 3 else 0):
                kt = tb * 4 + a
                vb = Vs[a % 2]
                for hh in range(2):
                    pb = PB[2 + ((a * 2 + hh) % 4)]

                    def f(e, a=a, hh=hh, pb=pb):
                        e.matmul(pb.ap, lhsT=kvnT.ap[:, 0, a * 128:(a + 1) * 128], rhs=wukv.ap[:, 0, 1024 + hh * 512:1024 + (hh + 1) * 512],
                                 start=True, stop=False)
                        return e.matmul(pb.ap, lhsT=kvnT.ap[:, 1, a * 128:(a + 1) * 128], rhs=wukv.ap[:, 1, 1024 + hh * 512:1024 + (hh + 1) * 512],
                                        start=False, stop=True)
                    S.pe(f, reads=[kvnT.k(), wukv.k()], writes=[pb.k()])
                    if hh == 0:
                        S.dve(lambda e, pb=pb, vb=vb: e.tensor_copy(out=vb.ap[:, 0:512], in_=pb.ap), reads=[pb.k()], writes=[vb.k(0, 512)])
                    else:
                        S.act(lambda e, pb=pb, vb=vb: e.activation(out=vb.ap[:, 512:1024], in_=pb.ap, func=AF.Copy),
                              reads=[pb.k()], writes=[vb.k(512, 1024)])
                S.dma(V_d[kt], vb.ap, reads=[vb.k()], writes=[("V", kt)])

            if tb >= 4 or not kq_own:
                continue
            tsl = slice(tb * 512, (tb + 1) * 512)
            s0, wq0 = wload(w_in[:, 0:256], 16, 256)
            s1, wq1 = wload(w_in[:, 256:512], 16, 256)
            osub = int(os.environ.get("OWN_SUB", 9))
            for c in range(4):
                ws_, wv_ = (s0, wq0) if c < 2 else (s1, wq1)
                pb = PB[2 + (c % 2)]

                def f(e, c=c, wv_=wv_, pb=pb):
                    i = None
                    for k in range(16):
                        i = e.matmul(pb.ap, lhsT=wv_[:, k, (c % 2) * 128:(c % 2 + 1) * 128], rhs=hTb.ap[:, k, :], start=(k == 0), stop=(k == 15))
                    return i
                if osub >= 1:
                    S.pe(f, reads=[ws_.k(), hTb.k()], writes=[pb.k()])
                if osub >= 2:
                    S.dve(lambda e, c=c, pb=pb: e.tensor_scalar(out=qcg.ap[:, c, :], in0=pb.ap, scalar1=gqn.ap[:, c:c + 1], scalar2=None, op0=ALU.mult),
                          reads=[pb.k(), gqn.k()], writes=[qcg.k(c)])
                sqb = sq[c % 2]
                if osub >= 3:
                    S.act(lambda e, pb=pb, sqb=sqb: e.activation(out=sqb.ap, in_=pb.ap, func=AF.Square), reads=[pb.k()], writes=[sqb.k()])
                if osub >= 4:
                    S.pe(lambda e, c=c, sqb=sqb: e.matmul(PB[6].ap, lhsT=onesb.ap, rhs=sqb.ap, start=(c == 0), stop=(c == 3)),
                         reads=[onesb.k(), sqb.k()], writes=[PB[6].k()])
            if osub >= 5:
                S.dve(lambda e: e.tensor_scalar(out=epsq.ap, in0=PB[6].ap, scalar1=EPS / 512.0, scalar2=EPS * EPS, op0=ALU.mult, op1=ALU.add),
                      reads=[PB[6].k()], writes=[epsq.k()])
            own_sec = int(os.environ.get("OWN_SEC", 9))
            if own_sec < 2:
                continue
            sa, wuA = wload(w_uq[:, 0:1024], 4, 1024)
            sb_, wuB = wload(w_uq[:, 1024:2048], 4, 1024)
            S.dma(cs.ap, cos_d[:, tsl], writes=[cs.k()])
            S.dma(sn.ap, sin_d[:, tsl], writes=[sn.k()])
            for h in range(H):
                pn, pr, psw, pss = (PB[2], PB[3], PB[4], PB[5]) if h % 2 == 0 else (PB[0], PB[1], PB[6], PB[7])
                sq0, sq1 = sqq[(h % 2) * 2], sqq[(h % 2) * 2 + 1]
                rqb = rqbs[h % 2]
                rt0, rt1 = rt[(h % 2) * 2], rt[(h % 2) * 2 + 1]

                def f(e, h=h, wuA=wuA, wuB=wuB, pn=pn, pr=pr, psw=psw):
                    i = None
                    for c in range(4):
                        i = e.matmul(pn.ap, lhsT=wuA[:, c, h * 128:(h + 1) * 128], rhs=qcg.ap[:, c, :], start=(c == 0), stop=(c == 3))
                    for c in range(4):
                        i = e.matmul(pr.ap[0:64, :], lhsT=wuB[:, c, h * 64:(h + 1) * 64], rhs=qcg.ap[:, c, :], start=(c == 0), stop=(c == 3))
                    for c in range(4):
                        i = e.matmul(psw.ap[0:64, :], lhsT=wuB[:, c, 512 + h * 64:512 + (h + 1) * 64], rhs=qcg.ap[:, c, :], start=(c == 0), stop=(c == 3))
                    return i
                S.pe(f, reads=[sa.k(), sb_.k(), qcg.k()], writes=[pn.k(), pr.k(), psw.k()])
                S.act(lambda e, pn=pn, sq0=sq0: e.activation(out=sq0.ap, in_=pn.ap, func=AF.Square), reads=[pn.k()], writes=[sq0.k()])
                S.act(lambda e, pr=pr, sq1=sq1: e.activation(out=sq1.ap[0:64, :], in_=pr.ap[0:64, :], func=AF.Square), reads=[pr.k()], writes=[sq1.k()])

                def f(e, pss=pss, sq0=sq0, sq1=sq1):
                    e.matmul(pss.ap, lhsT=onesb.ap, rhs=sq0.ap, start=True, stop=False)
                    return e.matmul(pss.ap, lhsT=onesb.ap[0:64, :], rhs=sq1.ap[0:64, :], start=False, stop=True)
                S.pe(f, reads=[onesb.k(), sq0.k(), sq1.k()], writes=[pss.k()])
                S.dve(lambda e, pss=pss, rqb=rqb: e.scalar_tensor_tensor(out=rqb.ap, in0=pss.ap, scalar=1.0 / 192, in1=epsq.ap, op0=ALU.mult, op1=ALU.add),
                      reads=[pss.k(), epsq.k()], writes=[rqb.k()])
                S.act(lambda e, rqb=rqb: e.activation(out=rqb.ap, in_=rqb.ap, func=AF.Sqrt), reads=[rqb.k()], writes=[rqb.k()])
                S.dve(lambda e, rqb=rqb: e.reciprocal(out=rqb.ap, in_=rqb.ap), reads=[rqb.k()], writes=[rqb.k()])
                qn_, qr_ = QNs[h % 2], QRs[h % 2]
                S.dve(lambda e, qn_=qn_, pn=pn, rqb=rqb: e.scalar_tensor_tensor(out=qn_.ap, in0=pn.ap, scalar=small.ap[:, 0:1], in1=rqb.ap,
                                                                                op0=ALU.mult, op1=ALU.mult),
                      reads=[pn.k(), small.k(), rqb.k()], writes=[qn_.k()])
                S.dve(lambda e, pr=pr, rt0=rt0: e.scalar_tensor_tensor(out=rt0.ap, in0=pr.ap[0:64, :], scalar=gvec.ap[0:64, 2:3], in1=cs.ap,
                                                                       op0=ALU.mult, op1=ALU.mult),
                      reads=[pr.k(), gvec.k(), cs.k()], writes=[rt0.k()])
                S.dve(lambda e, psw=psw, rt1=rt1: e.scalar_tensor_tensor(out=rt1.ap, in0=psw.ap[0:64, :], scalar=gvec.ap[0:64, 3:4], in1=sn.ap,
                                                                         op0=ALU.mult, op1=ALU.mult),
                      reads=[psw.k(), gvec.k(), sn.k()], writes=[rt1.k()])
                S.dve(lambda e, rt0=rt0, rt1=rt1: e.tensor_tensor(out=rt0.ap, in0=rt0.ap, in1=rt1.ap, op=ALU.add),
                      reads=[rt0.k(), rt1.k()], writes=[rt0.k()])
                S.dve(lambda e, qr_=qr_, rt0=rt0, rqb=rqb: e.tensor_tensor(out=qr_.ap, in0=rt0.ap, in1=rqb.ap[0:64, :], op=ALU.mult),
                      reads=[rt0.k(), rqb.k()], writes=[qr_.k()])
                S.dma(QTn_d[h, :, tsl], qn_.ap, reads=[qn_.k()], writes=[("QTn", h, tb)])
                S.dma(QTr_d[h, :, tsl], qr_.ap, reads=[qr_.k()], writes=[("QTr", h, tb)])
            if own_sec < 3:
                continue
            for pnl in range(4):
                s, wu_ = wload(w_in[:, OFF_U + pnl * 256: OFF_U + (pnl + 1) * 256], 16, 256)
                for nt in range(2):
                    g = pnl * 2 + nt
                    pb = PB[2 + (g % 4)]

                    def f(e, wu_=wu_, nt=nt, pb=pb):
                        i = None
                        for k in range(16):
                            i = e.matmul(pb.ap, lhsT=wu_[:, k, nt * 128:(nt + 1) * 128], rhs=hTb.ap[:, k, :], start=(k == 0), stop=(k == 15))
                        return i
                    S.pe(f, reads=[s.k(), hTb.k()], writes=[pb.k()])
                    S.act(lambda e, g=g, pb=pb: e.activation(out=uT.ap[:, g, :], in_=pb.ap, func=AF.Gelu), reads=[pb.k()], writes=[uT.k(g)])
            for half in range(2 if own_sec >= 4 else 0):
                S.dve(lambda e: e.memset(stat.ap[:, 32:48], 0.0), writes=[stat.k(32, 48)])
                for pnl in range(4):
                    s, wv_ = wload(w_in[:, OFF_V + pnl * 256: OFF_V + (pnl + 1) * 256], 16, 256)
                    for t2 in range(2):
                        a = half * 2 + t2
                        pb = PB[2 + ((pnl * 2 + t2) % 4)]

                        def f(e, wv_=wv_, a=a, pb=pb):
                            i = None
                            for k in range(16):
                                i = e.matmul(pb.ap[:, 0:256], lhsT=hTb.ap[:, k, a * 128:(a + 1) * 128], rhs=wv_[:, k, :], start=(k == 0), stop=(k == 15))
                            return i
                        S.pe(f, reads=[s.k(), hTb.k()], writes=[pb.k()])
                        c0 = 32 + t2 * 8 + pnl
                        S.act(lambda e, pb=pb, t2=t2, pnl=pnl, c0=c0: e.activation(out=vg.ap[:, t2, pnl * 256:(pnl + 1) * 256], in_=pb.ap[:, 0:256],
                                                                                   func=AF.Gelu, accum_out=stat.ap[:, c0:c0 + 1]),
                              reads=[pb.k()], writes=[vg.k(t2), stat.k(c0, c0 + 1)])
                        S.act(lambda e, t2=t2, pnl=pnl, c0=c0: e.activation(out=vns[t2].ap[:, pnl * 256:(pnl + 1) * 256], in_=vg.ap[:, t2, pnl * 256:(pnl + 1) * 256],
                                                                            func=AF.Square, accum_out=stat.ap[:, c0 + 4:c0 + 5]),
                              reads=[vg.k(t2)], writes=[vns[t2].k(), stat.k(c0 + 4, c0 + 5)])
                for t2 in range(2):
                    a = half * 2 + t2
                    c0 = 32 + t2 * 8
                    vn, sgt = vns[t2], sgts[t2]
                    spm0, spm1 = (PB[0], PB[1]) if t2 == 0 else (PB[6], PB[7])
                    mu, e2, var = stat.ap[:, 48 + t2 * 4:49 + t2 * 4], stat.ap[:, 49 + t2 * 4:50 + t2 * 4], stat.ap[:, 50 + t2 * 4:51 + t2 * 4]
                    sk = stat.k(48 + t2 * 4, 52 + t2 * 4)
                    sk_in = stat.k(c0, c0 + 8)
                    S.dve(lambda e, c0=c0, mu=mu: e.tensor_reduce(out=mu, in_=stat.ap[:, c0:c0 + 4], axis=AX.X, op=ALU.add), reads=[sk_in], writes=[sk])
                    S.dve(lambda e, c0=c0, e2=e2: e.tensor_reduce(out=e2, in_=stat.ap[:, c0 + 4:c0 + 8], axis=AX.X, op=ALU.add), reads=[sk_in], writes=[sk])
                    S.dve(lambda e, mu=mu: e.tensor_scalar(out=mu, in0=mu, scalar1=1.0 / 1024, scalar2=None, op0=ALU.mult), reads=[sk], writes=[sk])
                    S.dve(lambda e, mu=mu, var=var: e.tensor_tensor(out=var, in0=mu, in1=mu, op=ALU.mult), reads=[sk], writes=[sk])
                    S.dve(lambda e, e2=e2, var=var: e.scalar_tensor_tensor(out=var, in0=e2, scalar=1.0 / 1024, in1=var, op0=ALU.mult, op1=ALU.subtract),
                          reads=[sk], writes=[sk])
                    S.dve(lambda e, var=var: e.tensor_scalar(out=var, in0=var, scalar1=EPS, scalar2=None, op0=ALU.add), reads=[sk], writes=[sk])
                    S.act(lambda e, var=var: e.activation(out=var, in_=var, func=AF.Sqrt), reads=[sk], writes=[sk])
                    S.dve(lambda e, var=var: e.reciprocal(out=var, in_=var), reads=[sk], writes=[sk])
                    S.dve(lambda e, t2=t2, mu=mu, var=var, vn=vn: e.tensor_scalar(out=vn.ap, in0=vg.ap[:, t2, :], scalar1=mu, scalar2=var,
                                                                                  op0=ALU.subtract, op1=ALU.mult),
                          reads=[vg.k(t2), sk], writes=[vn.k()])

                    def f(e, vn=vn, spm0=spm0, spm1=spm1):
                        i = None
                        for g in range(8):
                            pm = spm0 if g < 4 else spm1
                            i = e.matmul(pm.ap[:, (g % 4) * 128:(g % 4 + 1) * 128], lhsT=vn.ap[:, g * 128:(g + 1) * 128], rhs=wsTb.ap[:, g, :],
                                         start=True, stop=True)
                        return i
                    S.pe(f, reads=[vn.k(), wsTb.k()], writes=[spm0.k(), spm1.k()])
                    for g in range(8):
                        pm = spm0 if g < 4 else spm1
                        S.dve(lambda e, g=g, pm=pm, sgt=sgt: e.scalar_tensor_tensor(out=sgt.ap[:, g, :], in0=pm.ap[:, (g % 4) * 128:(g % 4 + 1) * 128],
                                                                                    scalar=sgun.ap[:, g:g + 1], in1=T2.ap[:, g, :], op0=ALU.mult, op1=ALU.add),
                              reads=[pm.k(), sgun.k(), T2.k(g)], writes=[sgt.k(g)])
                    so = sgs[a % 2]
                    S.dve(lambda e, so=so, a=a, sgt=sgt: e.tensor_tensor(out=so.ap, in0=sgt.ap, in1=uT.ap[:, :, a * 128:(a + 1) * 128], op=ALU.mult),
                          reads=[sgt.k(), uT.k()], writes=[so.k()])
                    S.dma(sgT_d[:, :, tb * 512 + a * 128: tb * 512 + (a + 1) * 128].rearrange("g p t -> p g t"), so.ap,
                          reads=[so.k()], writes=[("sgT", tb, a)])

        S.dve(lambda e: e.scalar_tensor_tensor(out=A2.ap, in0=modFM.ap[:, 64:80, 0], scalar=1.0, in1=n2g.ap,
                                               op0=ALU.add, op1=ALU.mult),
              reads=[("modB", c) for c in range(32, 96)] + [n2g.k()], writes=[A2.k()])
        pm1 = PA.view(2048, [128, 128], F32)
        S.pe(lambda e: e.transpose(out=pm1.ap[0:96, :], in_=modFM.ap[:, :, 0], identity=identf.ap),
             reads=[("modB", c) for c in range(32, 96)] + [modFM.k(0, 32), identf.k()], writes=[pm1.k()])
        S.dve(lambda e: e.tensor_copy(out=mrow_sb.ap[0:96, :], in_=pm1.ap[0:96, :]), reads=[pm1.k()], writes=[mrow_sb.k()])
        S.dma(mrow_d, mrow_sb.ap[0:96, :], reads=[mrow_sb.k()], writes=["mrow"])

        S.dve(lambda e: e.tensor_scalar(out=rk.ap, in0=rk.ap, scalar1=1.0, scalar2=192.0 * EPS, op0=ALU.mult, op1=ALU.add),
              reads=[rk.k()], writes=[rk.k()])
        S.act(lambda e: e.activation(out=rk.ap, in_=rk.ap, func=AF.Sqrt), reads=[rk.k()], writes=[rk.k()])
        S.dve(lambda e: e.reciprocal(out=rk.ap, in_=rk.ap), reads=[rk.k()], writes=[rk.k()])
        if debug:
            S.dma(rk_dbg, rk.ap, reads=[rk.k()], writes=["rkdbg"])

        if limit == 1:
            S.finalize(final_reads=list(S.dram_keys))
            S.emit()
            return nc
        A.ptr = phase_base
        KTh = [A.alloc([128, NKEY], BF16) for _ in range(2)]
        Vh = [A.alloc([128, NKT, 128], BF16) for _ in range(2)]
        Qn = [A.alloc([128, NOWN], BF16) for _ in range(2)]
        Qr = [A.alloc([64, NOWN], BF16) for _ in range(2)]
        PT = [A.alloc([128, 1024], BF16) for _ in range(4)]
        rden = A.alloc([128, 1024], F32)
        lnd = A.alloc([128, 1024], F32)
        daccs = [[A.alloc([128, 1024], F32) for _ in range(4)] for _ in range(2)]
        zer = A.alloc([128, 1024], F32)
        posb = [A.alloc([128, 1024], F32) for _ in range(2)]
        ats = [A.alloc([128, NOWN], BF16) for _ in range(2)]
        ST = [PA.view(i * 4096, [128, 1024], F32) for i in range(3)]
        po = PA.view(3 * 4096, [128, 1024], F32)
        S.dve(lambda e: e.memset(zer.ap, 0.0), writes=[zer.k()])
        qctr = 0
        pending = []

        def epilogue(dacc, psb, ab, q0, hh, last_qb, pd):
            d0, d1, d2, d3 = dacc
            S.add("pool", lambda e: e.tensor_tensor(out=d0.ap, in0=d0.ap, in1=d1.ap, op=ALU.add), reads=[d0.k(), d1.k()], writes=[d0.k()])
            S.add("pool", lambda e: e.tensor_tensor(out=d2.ap, in0=d2.ap, in1=d3.ap, op=ALU.add), reads=[d2.k(), d3.k()], writes=[d2.k()])
            S.add("pool", lambda e: e.tensor_tensor(out=d0.ap, in0=d0.ap, in1=d2.ap, op=ALU.add), reads=[d0.k(), d2.k()], writes=[d0.k()])

            def f(e):
                e.matmul(pd.ap[:, 0:512], lhsT=onesf.ap, rhs=d0.ap[:, 0:512], start=True, stop=True)
                return e.matmul(pd.ap[:, 512:1024], lhsT=onesf.ap, rhs=d0.ap[:, 512:1024], start=True, stop=True)
            S.pe(f, reads=[onesf.k(), d0.k()], writes=[pd.k()])
            S.act(lambda e: e.activation(out=lnd.ap, in_=pd.ap, func=AF.Ln), reads=[pd.k()], writes=[lnd.k()])
            S.act(lambda e: e.activation(out=rden.ap, in_=lnd.ap, func=AF.Exp, scale=-1.0), reads=[lnd.k()], writes=[rden.k()])
            S.add("pool", lambda e: e.tensor_tensor(out=ab.ap[:, q0:q0 + 1024], in0=psb.ap, in1=rden.ap, op=ALU.mult),
                  reads=[psb.k(), rden.k()], writes=[ab.k(q0, q0 + 1024)])
            if last_qb:
                S.dma(atT_d[hh], ab.ap, reads=[ab.k()], writes=[("atT", hh)])

        for h in range(H):
            kb, vb, qn_, qr_ = KTh[h % 2], Vh[h % 2], Qn[h % 2], Qr[h % 2]
            S.dma(kb.ap, KT_d[h], reads=[("KT", h, tb) for tb in range(9)], writes=[kb.k()])
            S.dma(vb.ap, V_d[:, :, h * 128:(h + 1) * 128].rearrange("k p d -> p k d"), reads=[("V", kt) for kt in range(NKT)], writes=[vb.k()])
            S.dma(qn_.ap, QTn_d[h], reads=[("QTn", h, tb) for tb in range(4)], writes=[qn_.k()])
            S.dma(qr_.ap, QTr_d[h], reads=[("QTr", h, tb) for tb in range(4)], writes=[qr_.k()])
            ab = ats[h % 2]
            for qb in range(2):
                q0 = qb * 1024
                dacc = daccs[qctr % 2]
                psb = posb[qctr % 2]
                qctr += 1

                def s_op(kt, kb=kb, qn_=qn_, qr_=qr_, q0=q0):
                    pb = ST[kt % 3]

                    def f(e):
                        i = None
                        for hf in range(2):
                            e.matmul(pb.ap[:, hf * 512:(hf + 1) * 512], lhsT=kb.ap[:, kt * 128:(kt + 1) * 128],
                                     rhs=qn_.ap[:, q0 + hf * 512: q0 + (hf + 1) * 512], start=True, stop=False)
                        for hf in range(2):
                            i = e.matmul(pb.ap[:, hf * 512:(hf + 1) * 512], lhsT=krot.ap[:, kt * 128:(kt + 1) * 128],
                                         rhs=qr_.ap[:, q0 + hf * 512: q0 + (hf + 1) * 512], start=False, stop=True)
                        return i
                    S.pe(f, reads=[kb.k(), qn_.k(), qr_.k(), krot.k()], writes=[pb.k()])
                s_op(0)
                s_op(1)
                for kt in range(NKT):
                    if kt == 14 and pending:
                        epilogue(*pending.pop(0), pd=ST[(kt + 2) % 3])
                    if kt + 2 < NKT and not (kt == 14 and False):
                        s_op(kt + 2)
                    pb = ST[kt % 3]
                    pt = PT[kt % 4]
                    S.act(lambda e, pb=pb, pt=pt, kt=kt, h=h: e.activation(out=pt.ap, in_=pb.ap, func=AF.Exp, scale=rk.ap[:, kt * H + h:kt * H + h + 1],
                                                                           bias=small.ap[:, 1:2]),
                          reads=[pb.k(), rk.k(), small.k()], writes=[pt.k()])

                    def f(e, pt=pt, kt=kt, vb=vb):
                        e.matmul(po.ap[:, 0:512], lhsT=vb.ap[:, kt, :], rhs=pt.ap[:, 0:512], start=(kt == 0), stop=(kt == NKT - 1))
                        return e.matmul(po.ap[:, 512:1024], lhsT=vb.ap[:, kt, :], rhs=pt.ap[:, 512:1024], start=(kt == 0), stop=(kt == NKT - 1))
                    S.pe(f, reads=[pt.k(), vb.k()], writes=[po.k()])
                    da = dacc[kt % 4]
                    src0 = zer if kt < 4 else da
                    S.dve(lambda e, pt=pt, da=da, src0=src0: e.tensor_tensor(out=da.ap, in0=src0.ap, in1=pt.ap, op=ALU.add),
                          reads=[pt.k(), src0.k()], writes=[da.k()])
                S.act(lambda e, psb=psb: e.activation(out=psb.ap, in_=po.ap, func=AF.Copy), reads=[po.k()], writes=[psb.k()])
                pending.append((dacc, psb, ab, q0, h, qb == 1))
                if h == H - 1 and qb == 1:
                    while pending:
                        epilogue(*pending.pop(0), pd=ST[0])

        if limit == 2:
            S.finalize(final_reads=["rkdbg"] + [("atT", h) for h in range(H)])
            S.emit()
            return nc
        A.ptr = phase_base
        alloc_slots(7)
        g1bc = A.alloc([128, D], F32)
        g2bc = A.alloc([128, D], F32)
        hTb2s = [A.alloc([128, 16, 512], BF16) for _ in range(1)]
        actT = A.alloc([128, 44, 512], BF16)
        atbs = [A.alloc([128, 8, 512], BF16)]
        sgbs = [A.alloc([128, 8, 512], BF16)]
        mgT = A.view(actT.off + 16384, [128, 16, 512], BF16)
        xnew = A.alloc([128, 4, D], F32)
        xs2 = A.view(actT.off, [128, 4, D], BF16)
        sg = [A.alloc([128, 512], BF16) for _ in range(2)]
        tt = [A.alloc([128, 512], F32) for _ in range(2)]
        stat2 = A.alloc([128, 64], F32)

        S.dma(g1bc.ap, mrow_d[32:48, :].rearrange("a b -> (a b)").partition_broadcast(128), reads=["mrow"], writes=[g1bc.k()])
        S.dma(g2bc.ap, mrow_d[80:96, :].rearrange("a b -> (a b)").partition_broadcast(128), reads=["mrow"], writes=[g2bc.k()])
        out_keys = []
        for tb in range(4):
            tsl = slice(tb * 512, (tb + 1) * 512)
            hTb2, atb, sgb = hTb2s[0], atbs[0], sgbs[0]
            S.dma(hTb2.ap, hT_d[:, :, tsl].rearrange("j p t -> p j t"), reads=[("hT", tb)], writes=[hTb2.k()])
            S.dma(atb.ap, atT_d[:, :, tsl].rearrange("h p t -> p h t"), reads=[("atT", h) for h in range(H)], writes=[atb.k()])
            S.dma(sgb.ap, sgT_d[:, :, tsl].rearrange("g p t -> p g t"), reads=[("sgT", tb, a) for a in range(4)], writes=[sgb.k()])
            for a in range(4):
                S.dma(xnew.ap[:, a, :], xk[tb * 512 + a * 128: tb * 512 + (a + 1) * 128, :], writes=[xnew.k(a)])
            for n2 in range(8):
                c0 = n2 * 256
                s_ga, w_ga = wload(w_in[:, OFF_GATE + c0: OFF_GATE + c0 + 256], 16, 256)
                s_gs, w_gs = wload(w_in[:, OFF_GATE + D + c0: OFF_GATE + D + c0 + 256], 16, 256)
                s_ba = next_slot()
                s_bs = s_ba
                w_ba = s_ba.ap[:, 0:2048].rearrange("p (k n) -> p k n", k=8)
                w_bs = s_ba.ap[:, 2048:4096].rearrange("p (k n) -> p k n", k=8)
                S.dma(w_ba, w_bra[:, c0:c0 + 256].rearrange("(k p) n -> p k n", p=128), writes=[s_ba.k(0, 2048)], q="pool")
                S.dma(w_bs, w_brs[:, c0:c0 + 256].rearrange("(k p) n -> p k n", p=128), writes=[s_ba.k(2048, 4096)], q="pool")
                for nt in range(2):
                    n = n2 * 2 + nt
                    base = 4 * (n % 2)
                    p0, p1, p2, p3 = PB[base], PB[base + 1], PB[base + 2], PB[base + 3]
                    cs_ = slice(nt * 128, (nt + 1) * 128)

                    def f(e, w_ga=w_ga, w_gs=w_gs, w_ba=w_ba, w_bs=w_bs, cs_=cs_, p0=p0, p1=p1, p2=p2, p3=p3, hTb2=hTb2, atb=atb, sgb=sgb):
                        i = None
                        for k in range(16):
                            i = e.matmul(p0.ap, lhsT=w_ga[:, k, cs_], rhs=hTb2.ap[:, k, :], start=(k == 0), stop=(k == 15))
                        for k in range(16):
                            i = e.matmul(p1.ap, lhsT=w_gs[:, k, cs_], rhs=hTb2.ap[:, k, :], start=(k == 0), stop=(k == 15))
                        for k in range(8):
                            i = e.matmul(p2.ap, lhsT=w_ba[:, k, cs_], rhs=atb.ap[:, k, :], start=(k == 0), stop=(k == 7))
                        for k in range(8):
                            i = e.matmul(p3.ap, lhsT=w_bs[:, k, cs_], rhs=sgb.ap[:, k, :], start=(k == 0), stop=(k == 7))
                        return i
                    S.pe(f, reads=[s_ga.k(), s_gs.k(), s_ba.k(), s_bs.k(), hTb2.k(), atb.k(), sgb.k()],
                         writes=[p0.k(), p1.k(), p2.k(), p3.k()])
                    S.act(lambda e, p0=p0: e.activation(out=sg[0].ap, in_=p0.ap, func=AF.Sigmoid), reads=[p0.k()], writes=[sg[0].k()])
                    S.act(lambda e, p1=p1: e.activation(out=sg[1].ap, in_=p1.ap, func=AF.Sigmoid), reads=[p1.k()], writes=[sg[1].k()])
                    S.dve(lambda e, p2=p2: e.tensor_tensor(out=tt[0].ap, in0=p2.ap, in1=sg[0].ap, op=ALU.mult),
                          reads=[p2.k(), sg[0].k()], writes=[tt[0].k()])
                    S.dve(lambda e, p3=p3: e.tensor_tensor(out=tt[1].ap, in0=p3.ap, in1=sg[1].ap, op=ALU.mult),
                          reads=[p3.k(), sg[1].k()], writes=[tt[1].k()])
                    S.dve(lambda e, n=n: e.tensor_tensor(out=mgT.ap[:, n, :], in0=tt[0].ap, in1=tt[1].ap, op=ALU.add),
                          reads=[tt[0].k(), tt[1].k()], writes=[mgT.k(n)])
            for cb in range(4):
                s_a, w_a = wload(w_out[0:1024, cb * 512:(cb + 1) * 512], 8, 512)
                s_b, w_b = wload(w_out[1024:2048, cb * 512:(cb + 1) * 512], 8, 512)
                for a in range(4):
                    pb = PB[(cb * 4 + a) % 8]

                    def f(e, a=a, w_a=w_a, w_b=w_b, pb=pb):
                        i = None
                        for n in range(16):
                            wv_ = w_a if n < 8 else w_b
                            i = e.matmul(pb.ap, lhsT=mgT.ap[:, n, a * 128:(a + 1) * 128], rhs=wv_[:, n % 8, :], start=(n == 0), stop=(n == 15))
                        return i
                    S.pe(f, reads=[s_a.k(), s_b.k(), mgT.k()], writes=[pb.k()])
                    t_ = tt[a % 2]
                    S.dve(lambda e, pb=pb, t_=t_, cb=cb: e.tensor_tensor(out=t_.ap, in0=pb.ap, in1=g1bc.ap[:, cb * 512:(cb + 1) * 512], op=ALU.mult),
                          reads=[pb.k(), g1bc.k()], writes=[t_.k()])
                    xv = xnew.ap[:, a, cb * 512:(cb + 1) * 512]
                    xkk = ("R", "S", xnew.off + a * xnew.part + cb * 2048, xnew.off + a * xnew.part + (cb + 1) * 2048)
                    S.dve(lambda e, xv=xv, t_=t_: e.tensor_tensor(out=xv, in0=xv, in1=t_.ap, op=ALU.add),
                          reads=[t_.k(), xkk], writes=[xkk])
            norm_block(xs2, stat2, lambda a: (xnew.ap[:, a, :], xnew.k(a)), 4, hTb2, lambda j: A2.ap[:, j:j + 1],
                       lambda j: modFM.ap[:, 48 + j, 0:1], [A2.k()] + [("modB", c) for c in range(48, 64)])
            for fp in range(22):
                s_a, w_a = wload(w_f1[:, fp * 256:(fp + 1) * 256], 16, 256)
                s_b, w_b = wload(w_f1[:, DFF + fp * 256: DFF + (fp + 1) * 256], 16, 256)
                for ft in range(2):
                    f_ = fp * 2 + ft
                    base = 2 * (f_ % 4)
                    pa, pb = PB[base], PB[base + 1]
                    cs_ = slice(ft * 128, (ft + 1) * 128)

                    def f(e, w_a=w_a, w_b=w_b, cs_=cs_, pa=pa, pb=pb, hTb2=hTb2):
                        i = None
                        for k in range(16):
                            i = e.matmul(pa.ap, lhsT=w_a[:, k, cs_], rhs=hTb2.ap[:, k, :], start=(k == 0), stop=(k == 15))
                        for k in range(16):
                            i = e.matmul(pb.ap, lhsT=w_b[:, k, cs_], rhs=hTb2.ap[:, k, :], start=(k == 0), stop=(k == 15))
                        return i
                    S.pe(f, reads=[s_a.k(), s_b.k(), hTb2.k()], writes=[pa.k(), pb.k()])
                    sl = sg[f_ % 2]
                    S.act(lambda e, pa=pa, sl=sl: e.activation(out=sl.ap, in_=pa.ap, func=AF.Silu), reads=[pa.k()], writes=[sl.k()])
                    S.dve(lambda e, pb=pb, sl=sl, f_=f_: e.tensor_tensor(out=actT.ap[:, f_, :], in0=pb.ap, in1=sl.ap, op=ALU.mult),
                          reads=[pb.k(), sl.k()], writes=[actT.k(f_)])
            for ch in range(2):
                for fg in range(11):
                    s_w, w_w = wload(w_f2[fg * 512:(fg + 1) * 512, ch * 1024:(ch + 1) * 1024], 4, 1024)

                    def f(e, fg=fg, w_w=w_w):
                        i = None
                        for fi in range(4):
                            f_ = fg * 4 + fi
                            for a in range(4):
                                for c2 in range(2):
                                    i = e.matmul(PB[a * 2 + c2].ap, lhsT=actT.ap[:, f_, a * 128:(a + 1) * 128], rhs=w_w[:, fi, c2 * 512:(c2 + 1) * 512],
                                                 start=(f_ == 0), stop=(f_ == 43))
                        return i
                    S.pe(f, reads=[s_w.k(), actT.k(fg * 4, fg * 4 + 4)], writes=[PB[b_].k() for b_ in range(8)])
                for a in range(4):
                    for c2 in range(2):
                        cb = ch * 2 + c2
                        pb = PB[a * 2 + c2]
                        t_ = tt[(a * 2 + c2) % 2]
                        S.dve(lambda e, pb=pb, t_=t_, cb=cb: e.tensor_tensor(out=t_.ap, in0=pb.ap, in1=g2bc.ap[:, cb * 512:(cb + 1) * 512], op=ALU.mult),
                              reads=[pb.k(), g2bc.k()], writes=[t_.k()])
                        xv = xnew.ap[:, a, cb * 512:(cb + 1) * 512]
                        xkk = ("R", "S", xnew.off + a * xnew.part + cb * 2048, xnew.off + a * xnew.part + (cb + 1) * 2048)
                        S.dve(lambda e, xv=xv, t_=t_: e.tensor_tensor(out=xv, in0=xv, in1=t_.ap, op=ALU.add),
                              reads=[t_.k(), xkk], writes=[xkk])
            for a in range(4):
                S.dma(y[tb * 512 + a * 128: tb * 512 + (a + 1) * 128, :], xnew.ap[:, a, :], reads=[xnew.k(a)], writes=[("y", tb, a)])
                out_keys.append(("y", tb, a))

        fin = list(out_keys)
        if debug:
            fin += ["rkdbg"]
        S.finalize(final_reads=fin)
        S.emit()
    return nc


def _rope_tables():
    nf = 16
    freqs = (np.float32(10000.0) ** (-np.arange(nf, dtype=np.float32) / np.float32(nf))).astype(np.float32)
    pos = np.arange(SEQ)
    row = (pos // 64).astype(np.float32)
    col = (pos % 64).astype(np.float32)
    ang_r = (row[:, None] * freqs[None, :]).astype(np.float32)
    ang_c = (col[:, None] * freqs[None, :]).astype(np.float32)
    cosT = np.zeros((64, SEQ), np.float32)
    sinS = np.zeros((64, SEQ), np.float32)
    for r in range(64):
        ang = ang_r if r < 32 else ang_c
        f = r % 16
        first = (r % 32) < 16
        cosT[r] = np.cos(ang[:, f])
        sinS[r] = (-np.sin(ang[:, f])) if first else np.sin(ang[:, f])
    return cosT, sinS


_SIGMA = np.array([(r + 16) if (r % 32) < 16 else (r - 16) for r in range(64)])


def _fm(v, nchunk):
    return np.ascontiguousarray(np.asarray(v, np.float32).reshape(nchunk, 128).T)


def make_in_maps(inputs):
    f = lambda k: np.asarray(inputs[k], np.float32)
    x, c, ctx, c_ctx = f("x"), f("c"), f("ctx"), f("c_ctx")
    w_in = np.ascontiguousarray(f("w_in")[0])
    w_uq = f("w_uq")[0].reshape(512, H, 192)
    w_ukv = f("w_ukv")[0].reshape(256, H, 256)
    gq, gk = f("qk_norm_q")[0], f("qk_norm_k")[0]
    cosT, sinS = _rope_tables()
    kr = w_in[:, 768:832]
    w_kv = np.ascontiguousarray(np.concatenate([w_in[:, 512:768], kr, kr[:, _SIGMA]], axis=1))
    w_uq_l = np.ascontiguousarray(np.concatenate([
        w_uq[:, :, 0:128].reshape(512, 1024),
        w_uq[:, :, 128:192].reshape(512, 512),
        w_uq[:, :, 128:192][:, :, _SIGMA].reshape(512, 512)], axis=1))
    w_ukv_l = np.ascontiguousarray(np.concatenate([w_ukv[:, :, 0:128].reshape(256, 1024), w_ukv[:, :, 128:256].reshape(256, 1024)], axis=1))
    gvec = np.zeros((128, 8), np.float32)
    gvec[:, 0] = gq[0:128]
    gvec[:, 1] = gk[0:128]
    gvec[:64, 2] = gq[128:192]
    gvec[:64, 3] = gq[128:192][_SIGMA]
    gvec[:64, 4] = gk[128:192]
    gvec[:64, 5] = gk[128:192][_SIGMA]
    gqk = np.ascontiguousarray(np.broadcast_to(np.concatenate([gq, gk])[None, :], (128, 384)))
    sgun = np.ascontiguousarray(np.concatenate([_fm(f("sgu_norm_g")[0], 8), _fm(f("sgu_norm_b")[0], 8)], axis=1))
    wsT = np.ascontiguousarray(f("w_spatial")[0].transpose(2, 0, 1))
    bsbc = np.ascontiguousarray(np.broadcast_to(f("b_spatial")[0].reshape(1, 1024), (128, 1024)))
    shared = {
        "bmT": _fm(f("b_mod")[0], 96), "n1g": _fm(f("norm1_g")[0], 16), "n2g": _fm(f("norm2_g")[0], 16),
        "gqn": _fm(f("q_norm_g")[0], 4), "gkv": _fm(f("kv_norm_g")[0], 2), "gvec": gvec, "gqk": gqk, "sgun": sgun,
        "w_mod": np.ascontiguousarray(f("w_mod")[0]), "w_in": w_in, "w_kv": w_kv, "w_uq": w_uq_l, "w_ukv": w_ukv_l,
        "wsT": wsT, "bsbc": bsbc, "w_bra": np.ascontiguousarray(f("w_br_attn")[0]), "w_brs": np.ascontiguousarray(f("w_br_sgu")[0]),
        "w_out": np.ascontiguousarray(f("w_out")[0]), "w_f1": np.ascontiguousarray(f("w_ffn_in")[0]),
        "w_f2": np.ascontiguousarray(f("w_ffn_out")[0]), "ident": np.eye(128, dtype=np.float32),
    }
    maps = []
    for i in range(8):
        b, hf = i // 2, i % 2
        own = slice(hf * NOWN, (hf + 1) * NOWN)
        oth = slice((1 - hf) * NOWN, (2 - hf) * NOWN)
        m = dict(shared)
        m["xk"] = np.ascontiguousarray(np.concatenate([x[b, own], x[b, oth]], axis=0))
        m["ctxb"] = np.ascontiguousarray(ctx[b])
        cT = np.stack([_fm(c[b], 16), _fm(c_ctx, 16)], axis=-1)
        m["cT"] = np.ascontiguousarray(cT)
        m["cosT"] = np.ascontiguousarray(np.concatenate([cosT[:, own], cosT[:, oth]], axis=1))
        m["sinS"] = np.ascontiguousarray(np.concatenate([sinS[:, own], sinS[:, oth]], axis=1))
        maps.append(m)
    return maps


_NC_CACHE = {}


def kernel(**inputs):
    maps = make_in_maps(inputs)
    if "nc" not in _NC_CACHE:
        _NC_CACHE["nc"] = build(False)
    res = run_bass_kernel_spmd(_NC_CACHE["nc"], maps, core_ids=list(range(8)))
    out = np.empty((4, SEQ, D), np.float32)
    for i in range(8):
        b, hf = i // 2, i % 2
        out[b, hf * NOWN:(hf + 1) * NOWN] = res.results[i]["y"]
    return out
```
